# Optimizing a Trainium2 kernel written in Bass

```python
import jax, jax.numpy as jnp
from jax import lax
import numpy as np

D_MODEL = 2048
BATCH = 4
SEQ = 4096
DEPTH = 4

CHUNK = 64
N_META = 16
D_MIX = D_MODEL
CONV_W = D_MIX // 2
CONV_K = 3
RWKV_W = D_MIX - CONV_W
HEAD_DIM = 64
RWKV_HEADS = RWKV_W // HEAD_DIM
R_DECAY = 64
R_ICLR = 64
R_VRES = 32
R_GATE = 160
D_FF = 5632
NORM_EPS = 1e-6
LNX_EPS = 64e-5
P_CONV = 3 * CONV_W
P_RWKV = 3 * RWKV_W + R_DECAY + R_ICLR + R_GATE
P_IN = P_CONV + P_RWKV

kernel_name = 'hymba_style_conv_rwkv7_macaron_trunk'


def rms_norm(x, g):
    xf = x.astype(jnp.float32)
    y = xf * lax.rsqrt(jnp.mean(xf * xf, axis=-1, keepdims=True) + NORM_EPS)
    return (y * g.astype(jnp.float32)).astype(x.dtype)


def swiglu(h, w_gu, w_down):
    gate, up = jnp.split(h @ w_gu, 2, axis=-1)
    return (jax.nn.silu(gate) * up) @ w_down


def token_shift(p, mu):
    p_prev = jnp.pad(p, ((0, 0), (1, 0), (0, 0)))[:, :-1]
    return p + mu * (p_prev - p)


def short_conv_group(p, conv_w, conv_g):
    gate_b, gate_c, u = jnp.split(p, 3, axis=-1)
    u = gate_c * u
    L = u.shape[1]
    u_pad = jnp.pad(u, ((0, 0), (CONV_K - 1, 0), (0, 0)))
    conv = conv_w[0] * u_pad[:, 0:L]
    for j in range(1, CONV_K):
        conv = conv + conv_w[j] * u_pad[:, j:j + L]
    return rms_norm(gate_b * conv, conv_g)


def wkv7_scan(r, decay, k, v, a_vec, b_vec):
    B, L, H, N = r.shape

    def step(S, inp):
        r_t, w_t, k_t, v_t, a_t, b_t = inp
        sa = jnp.einsum('bhvk,bhk->bhv', S, a_t)
        S = S * w_t[:, :, None, :] + sa[..., None] * b_t[:, :, None, :] + v_t[..., None] * k_t[:, :, None, :]
        return S, jnp.einsum('bhvk,bhk->bhv', S, r_t)

    def run_chunk(S, seq):
        return lax.scan(step, S, seq)

    seq = tuple(jnp.moveaxis(t, 1, 0) for t in (r, decay, k, v, a_vec, b_vec))
    meta = tuple(t[:N_META] for t in seq)
    frames = tuple(t[N_META:].reshape(-1, CHUNK, B, H, N) for t in seq)
    S0 = jnp.zeros((B, H, N, N), jnp.float32)
    S, y_meta = run_chunk(S0, meta)
    S, y_frames = lax.scan(run_chunk, S, frames)
    y = jnp.concatenate([y_meta, y_frames.reshape(-1, B, H, N)], axis=0)
    return jnp.moveaxis(y, 0, 1)


def rwkv7_group(p, v_first, v_mix, w0, w2, a0, a2, g2, k_k, k_a, r_k, lnx_g, lnx_b):
    B, L, _ = p.shape
    f32 = jnp.float32
    split_at = [RWKV_W, 2 * RWKV_W, 3 * RWKV_W, 3 * RWKV_W + R_DECAY,
                3 * RWKV_W + R_DECAY + R_ICLR, P_RWKV]
    r, k, v, wd, ad, gd, vd = jnp.split(p, split_at, axis=-1)
    if v_mix is None:
        v_first = v
    else:
        v0, v2 = v_mix
        v = v + (v_first - v) * jax.nn.sigmoid(v0 + vd @ v2)
    wlog = -jax.nn.softplus(-(w0 + jnp.tanh(wd) @ w2).astype(f32)) - 0.5
    decay = jnp.exp(-jnp.exp(wlog))
    a = jax.nn.sigmoid((a0 + ad @ a2).astype(f32))
    g = jax.nn.sigmoid(gd) @ g2

    def heads(t):
        return t.astype(f32).reshape(B, L, RWKV_HEADS, HEAD_DIM)

    r_h, k_h, v_h, a_h, w_h = heads(r), heads(k), heads(v), heads(a), heads(decay)
    kk = heads(k * k_k)
    kk = kk / jnp.maximum(jnp.linalg.norm(kk, axis=-1, keepdims=True), 1e-12)
    k_h = k_h * (1.0 + (a_h - 1.0) * k_a.astype(f32).reshape(RWKV_HEADS, HEAD_DIM))
    y = wkv7_scan(r_h, w_h, k_h, v_h, -kk, kk * a_h)
    mean = jnp.mean(y, axis=-1, keepdims=True)
    var = jnp.mean(jnp.square(y - mean), axis=-1, keepdims=True)
    y = ((y - mean) * lax.rsqrt(var + LNX_EPS)).reshape(B, L, RWKV_W) * lnx_g + lnx_b
    bonus = jnp.sum(r_h * k_h * r_k.astype(f32), axis=-1, keepdims=True) * v_h
    y = (y + bonus.reshape(B, L, RWKV_W)) * g.astype(f32)
    return y.astype(p.dtype), v_first


def setup_inputs(seed: int = 0) -> dict:
    key = jax.random.key(seed)
    ks = iter(jax.random.split(key, 40))

    def nrm(shape, scale):
        return scale * jax.random.normal(next(ks), shape, jnp.float32)

    def gain(shape):
        return 1.0 + nrm(shape, 0.02)

    def unif(shape, lo, hi):
        return jax.random.uniform(next(ks), shape, jnp.float32, lo, hi)

    DM1 = DEPTH - 1
    return {
        'x': nrm((BATCH, SEQ, D_MODEL), 1.0),
        'meta_tokens': nrm((N_META, D_MODEL), 1.0),
        'ffn1_pre_g': gain((DEPTH, D_MODEL)),
        'ffn1_w_gu': nrm((DEPTH, D_MODEL, 2 * D_FF), D_MODEL ** -0.5),
        'ffn1_w_down': nrm((DEPTH, D_FF, D_MODEL), D_FF ** -0.5),
        'ffn1_post_g': gain((DEPTH, D_MODEL)),
        'mix_pre_g': gain((DEPTH, D_MODEL)),
        'w_in': nrm((DEPTH, D_MODEL, P_IN), D_MODEL ** -0.5),
        'w_in_vres': nrm((DM1, D_MODEL, R_VRES), D_MODEL ** -0.5),
        'mu_rwkv': unif((DEPTH, P_RWKV), 0.0, 1.0),
        'mu_vres': unif((DM1, R_VRES), 0.0, 1.0),
        'conv_w': nrm((DEPTH, CONV_K, CONV_W), CONV_K ** -0.5),
        'conv_norm_g': gain((DEPTH, CONV_W)),
        'decay_w0': unif((DEPTH, RWKV_W), -6.0, -1.0),
        'decay_w2': nrm((DEPTH, R_DECAY, RWKV_W), 0.1),
        'iclr_a0': nrm((DEPTH, RWKV_W), 0.1),
        'iclr_a2': nrm((DEPTH, R_ICLR, RWKV_W), 0.1),
        'vres_v0': nrm((DM1, RWKV_W), 0.1),
        'vres_v2': nrm((DM1, R_VRES, RWKV_W), 0.1),
        'gate_g2': nrm((DEPTH, R_GATE, RWKV_W), R_GATE ** -0.5),
        'k_k': 0.85 + nrm((DEPTH, RWKV_W), 0.05),
        'k_a': 1.0 + nrm((DEPTH, RWKV_W), 0.05),
        'r_k': nrm((DEPTH, RWKV_HEADS, HEAD_DIM), 0.1),
        'lnx_g': gain((DEPTH, RWKV_W)),
        'lnx_b': nrm((DEPTH, RWKV_W), 0.02),
        'w_out': nrm((DEPTH, D_MIX, D_MODEL), D_MIX ** -0.5),
        'mix_post_g': gain((DEPTH, D_MODEL)),
        'ffn2_pre_g': gain((DEPTH, D_MODEL)),
        'ffn2_w_gu': nrm((DEPTH, D_MODEL, 2 * D_FF), D_MODEL ** -0.5),
        'ffn2_w_down': nrm((DEPTH, D_FF, D_MODEL), D_FF ** -0.5),
        'ffn2_post_g': gain((DEPTH, D_MODEL)),
    }


def reference(x, meta_tokens, ffn1_pre_g, ffn1_w_gu, ffn1_w_down, ffn1_post_g,
              mix_pre_g, w_in, w_in_vres, mu_rwkv, mu_vres, conv_w, conv_norm_g,
              decay_w0, decay_w2, iclr_a0, iclr_a2, vres_v0, vres_v2, gate_g2,
              k_k, k_a, r_k, lnx_g, lnx_b, w_out, mix_post_g,
              ffn2_pre_g, ffn2_w_gu, ffn2_w_down, ffn2_post_g):
    B = x.shape[0]
    meta = jnp.broadcast_to(meta_tokens.astype(x.dtype)[None], (B, N_META, D_MODEL))
    h = jnp.concatenate([meta, x], axis=1)
    v_first = None
    for i in range(DEPTH):
        h = h + 0.5 * rms_norm(swiglu(rms_norm(h, ffn1_pre_g[i]), ffn1_w_gu[i], ffn1_w_down[i]), ffn1_post_g[i])
        u = rms_norm(h, mix_pre_g[i])
        if i == 0:
            w_cat, mu_cat, v_mix = w_in[i], mu_rwkv[i], None
        else:
            w_cat = jnp.concatenate([w_in[i], w_in_vres[i - 1]], axis=1)
            mu_cat = jnp.concatenate([mu_rwkv[i], mu_vres[i - 1]], axis=0)
            v_mix = (vres_v0[i - 1], vres_v2[i - 1])
        proj = u @ w_cat
        y_conv = short_conv_group(proj[..., :P_CONV], conv_w[i], conv_norm_g[i])
        p_rwkv = token_shift(proj[..., P_CONV:], mu_cat)
        y_rwkv, v_first = rwkv7_group(p_rwkv, v_first, v_mix, decay_w0[i], decay_w2[i],
                                      iclr_a0[i], iclr_a2[i], gate_g2[i], k_k[i], k_a[i],
                                      r_k[i], lnx_g[i], lnx_b[i])
        y = jnp.concatenate([y_conv, y_rwkv], axis=-1) @ w_out[i]
        h = h + rms_norm(y, mix_post_g[i])
        h = h + 0.5 * rms_norm(swiglu(rms_norm(h, ffn2_pre_g[i]), ffn2_w_gu[i], ffn2_w_down[i]), ffn2_post_g[i])
    return h[:, N_META:]
```

```python
import numpy as np
from contextlib import ExitStack
import concourse.bass as bass
import concourse.mybir as mybir
from concourse.bass_utils import run_bass_kernel_spmd

F32 = mybir.dt.float32
BF16 = mybir.dt.bfloat16
ALU = mybir.AluOpType
AF = mybir.ActivationFunctionType
AX = mybir.AxisListType

D = 2048
DFF = 5632
NMETA = 16
SEQ = 4096
BATCH = 4
DEPTH = 4
RW = 1024
NJ_IN = 51
TT = 512
TM = 256
NC_F = D // 128
NJ_FF = DFF // 128
NORM_EPS = 1e-6
LNX_EPS = 64e-5
DEC_C = 0.6065306597126334

PP_G1, PP_GM, PP_G2 = 0, 16, 32
PP_P1, PP_PM, PP_P2 = 48, 64, 80
PP_MU = 96
PP_CW = 123
PP_CG = 147
PP_W0, PP_A0, PP_V0, PP_KK, PP_KA, PP_RK, PP_LG, PP_LB = 155, 163, 171, 179, 187, 195, 203, 211
NPP = 219


class _Stop(Exception):
    pass


class Buf:
    __slots__ = ("name", "w", "r", "pend")

    def __init__(self, name):
        self.name = name
        self.w = None
        self.r = {}
        self.pend = None


class Sem:
    __slots__ = ("h", "val")

    def __init__(self, h):
        self.h = h
        self.val = 0


class Eng:
    def __init__(self, raw, sem, name):
        self.raw = raw
        self.sem = sem
        self.name = name
        self.seen = {}
        self.pr = []
        self.pw = []
        self.dsems = []
        self.di = 0


class MK:
    def __init__(self, nc, es):
        self.nc = nc
        self.es = es
        self.nsem = 0
        self.pe = Eng(nc.tensor, self.newsem("pe"), "pe")
        self.act = Eng(nc.scalar, self.newsem("act"), "act")
        self.dve = Eng(nc.vector, self.newsem("dve"), "dve")
        self.pool = Eng(nc.gpsimd, self.newsem("pool"), "pool")
        self.sp = Eng(nc.sync, self.newsem("sp"), "sp")
        self.engs = [self.pe, self.act, self.dve, self.pool, self.sp]
        for q in (self.sp, self.pool):
            q.dsems = [self.newsem("d%s%d" % (q.name, i)) for i in range(12)]
        self.ninst = 0

    def newsem(self, name):
        self.nsem += 1
        return Sem(self.es.enter_context(self.nc.semaphore(name)))

    def _waits(self, eng, need):
        for sem, v in need.items():
            if eng.seen.get(sem, 0) < v:
                eng.raw.wait_ge(sem.h, v)
                eng.seen[sem] = v
                self.ninst += 1

    def _need(self, eng, reads, writes, need):
        for b in reads:
            assert b.pend is None or b.pend is eng, (b.name, eng.name)
            if b.w is not None and need.get(b.w[0], 0) < b.w[1]:
                need[b.w[0]] = b.w[1]
        for b in writes:
            assert b.pend is None or b.pend is eng, (b.name, eng.name)
            if b.w is not None and need.get(b.w[0], 0) < b.w[1]:
                need[b.w[0]] = b.w[1]
            for sem, v in b.r.items():
                if need.get(sem, 0) < v:
                    need[sem] = v

    def op(self, eng, fn, reads=(), writes=(), signal=True):
        need = {}
        self._need(eng, reads, writes, need)
        self._waits(eng, need)
        ins = fn()
        self.ninst += 1
        eng.pr.extend(reads)
        eng.pw.extend(writes)
        for b in reads:
            b.pend = eng
        for b in writes:
            b.pend = eng
        if signal:
            eng.sem.val += 1
            ins.then_inc(eng.sem.h, 1)
            v = eng.sem.val
            for b in eng.pw:
                b.w = (eng.sem, v)
                b.r = {}
                b.pend = None
            for b in eng.pr:
                if not (b.w is not None and b.w[0] is eng.sem and b.w[1] == v):
                    b.r[eng.sem] = v
                b.pend = None
            eng.pr = []
            eng.pw = []
            eng.seen[eng.sem] = max(eng.seen.get(eng.sem, 0), 0)
        return ins

    def dma(self, q, out_ap, in_ap, reads=(), writes=()):
        sem = q.dsems[q.di]
        q.di = (q.di + 1) % len(q.dsems)
        need = {sem: sem.val} if sem.val > 0 else {}
        self._need(q, reads, writes, need)
        self._waits(q, need)
        ins = q.raw.dma_start(out=out_ap, in_=in_ap)
        self.ninst += 1
        sem.val += 16
        ins.then_inc(sem.h, 16)
        for b in writes:
            b.w = (sem, sem.val)
            b.r = {}
        for b in reads:
            b.r[sem] = sem.val

    def barrier(self):
        need = {}
        for e in self.engs:
            assert not e.pr and not e.pw, e.name
            if e.sem.val:
                need[e.sem] = e.sem.val
            for s in e.dsems:
                if s.val:
                    need[s] = s.val
        for e in self.engs:
            self._waits(e, dict(need))

    def finish(self):
        need = {}
        for e in self.engs:
            if e.sem.val:
                need[e.sem] = e.sem.val
            for s in e.dsems:
                if s.val:
                    need[s] = s.val
        self._waits(self.sp, need)


def build_program(T_PAD, depth):
    assert T_PAD % TT == 0
    nc = bass.Bass("TRN2", target_bir_lowering=False)
    dr = {}

    def dram(name, shape, dt, kind):
        t = nc.dram_tensor(name, list(shape), dt, kind=kind).ap()
        dr[name] = t
        return t

    h0 = dram("h0", [D, T_PAD], F32, "ExternalInput")
    hout = dram("hout", [D, T_PAD], F32, "ExternalOutput")
    hres = dram("hres", [D, T_PAD], F32, "Internal")
    vfd = dram("vfd", [RW, T_PAD], F32, "Internal")
    import os as _os
    phases = _os.environ.get("MK_PHASES", "f1,mix,f2").split(",")
    wgu1 = wdn1 = wgu2 = wdn2 = None
    if "f1" in phases:
        wgu1 = dram("wgu1", [depth, NJ_FF, 128, NC_F * 256], F32, "ExternalInput")
        wdn1 = dram("wdn1", [depth, NC_F, 128, NJ_FF * 128], F32, "ExternalInput")
    if "f2" in phases:
        wgu2 = dram("wgu2", [depth, NJ_FF, 128, NC_F * 256], F32, "ExternalInput")
        wdn2 = dram("wdn2", [depth, NC_F, 128, NJ_FF * 128], F32, "ExternalInput")
    win = dram("win", [depth, NJ_IN, 128, NC_F * 128], F32, "ExternalInput")
    wout = dram("wout", [depth, NC_F, 128, NC_F * 128], F32, "ExternalInput")
    lwd = dram("lw", [depth, 3, 128, RW], F32, "ExternalInput")
    ppd = dram("pp", [depth, 128, NPP], F32, "ExternalInput")
    cst = dram("cst", [128, 128 + 128 + 256 + 64 + 64], F32, "ExternalInput")
    cst2 = dram("cst2", [64, 2048], F32, "ExternalInput")

    es = ExitStack()
    mk = MK(nc, es)
    pe, act, dve, pool, sp = mk.pe, mk.act, mk.dve, mk.pool, mk.sp

    uid = [0]

    def sbt(stack, name, shape, dt):
        uid[0] += 1
        t = stack.enter_context(nc.sbuf_tensor("s%d_%s" % (uid[0], name), list(shape), dt))
        return t

    psum = es.enter_context(nc.psum_tensor("psum", [128, 8 * 512], F32))
    pbank = [Buf("bank%d" % i) for i in range(8)]
    bank_rr = [0]

    def bank(n=1):
        b = bank_rr[0]
        if n == 2 and b % 2 == 1:
            b = (b + 1) % 7
        if b + n > 7:
            b = 0
        bank_rr[0] = (b + n) % 7
        return b, pbank[b:b + n]

    def pap(b, n=1):
        return psum[:, b * 512:(b + n) * 512]

    cst_f = sbt(es, "cst_f", [128, 640], F32)
    idb128 = sbt(es, "idb128", [128, 128], BF16)
    cst_fb = Buf("cst_f")
    ones_bf = sbt(es, "ones_bf", [128, 128], BF16)
    blk_bf = sbt(es, "blk_bf", [128, 128], BF16)
    idb = sbt(es, "idb", [64, 64], BF16)
    cbf_b = Buf("cbf")
    pp = sbt(es, "pp", [128, NPP], F32)
    pp_b = Buf("pp")
    lwt = sbt(es, "lwt", [128, 3, RW], BF16)
    lw_b = Buf("lw")
    hT = sbt(es, "hT", [128, NC_F, TT], F32)
    hT_b = [Buf("hT%d" % c) for c in range(NC_F)]
    yT = sbt(es, "yT", [128, NC_F, TT], F32)
    yT_b = [Buf("yT%d" % c) for c in range(NC_F)]
    xT = sbt(es, "xT", [128, NC_F, TT], BF16)
    xT_b = [Buf("xT%d" % c) for c in range(NC_F)]
    sqt = [sbt(es, "sq%d" % i, [128, TT], BF16) for i in range(2)]
    sq_b = [Buf("sq%d" % i) for i in range(2)]
    rbc = sbt(es, "rbc", [128, TT], F32)
    rbc_b = Buf("rbc")
    rbt = sbt(es, "rbt", [128, TT], F32)
    rbt_b = Buf("rbt")
    Hf = sbt(es, "Hf", [128, 8, 64], F32)
    Hf_b = [Buf("Hf%d" % i) for i in range(8)]
    Hb = sbt(es, "Hb", [128, 8, 64], BF16)
    Hb_b = [Buf("Hb%d" % i) for i in range(8)]
    carry = sbt(es, "carry", [128, 27], F32)
    carry_b = [Buf("carry%d" % i) for i in range(27)]
    ccar = sbt(es, "ccar", [128, 8, 2], F32)
    ccar_b = [Buf("ccar%d" % i) for i in range(8)]

    mk.dma(sp, cst_f[:], cst[:, :], writes=[cst_fb])
    cst2_f = sbt(es, "cst2_f", [64, 2048], F32)
    mk.dma(sp, cst2_f[:], cst2[:, :], writes=[cst_fb])
    mk.op(dve, lambda: nc.vector.tensor_copy(out=ones_bf[:], in_=cst_f[:, 0:128]), [cst_fb], [cbf_b])
    mk.op(dve, lambda: nc.vector.tensor_copy(out=blk_bf[:], in_=cst_f[:, 128:256]), [cst_fb], [cbf_b])
    mk.op(dve, lambda: nc.vector.tensor_copy(out=idb[:], in_=cst_f[0:64, 576:640]), [cst_fb], [cbf_b])
    mk.op(dve, lambda: nc.vector.tensor_copy(out=idb128[:], in_=cst_f[:, 384:512]), [cst_fb], [cbf_b])
    maskNA = cst_f[0:64, 256:384]
    maskNT = cst_f[0:64, 512:576]
    identf = cst_f[0:64, 576:640]

    def rms_stats(src_tile, src_bufs, nt, eps_scale=1.0 / D):
        b, bb = bank()
        for c in range(NC_F):
            s = c % 2
            mk.op(act, lambda c=c, s=s: nc.scalar.activation(out=sqt[s][:, :nt], in_=src_tile[:, c, :nt], func=AF.Square),
                  [src_bufs[c]], [sq_b[s]])
            mk.op(pe, lambda c=c, s=s: nc.tensor.matmul(pap(b)[:, :nt], ones_bf[:], sqt[s][:, :nt], start=(c == 0), stop=(c == NC_F - 1)),
                  [cbf_b, sq_b[s]], bb, signal=True)
        mk.op(act, lambda: nc.scalar.activation(out=rbt[:, :nt], in_=pap(b)[:, :nt], func=AF.Sqrt, bias=NORM_EPS, scale=eps_scale),
              bb, [rbt_b])
        mk.op(dve, lambda: nc.vector.reciprocal(out=rbc[:, :nt], in_=rbt[:, :nt]), [rbt_b], [rbc_b])

    def load_h(src, t0, nt):
        v = src.rearrange("(c p) t -> p c t", p=128)
        for half in range(2):
            cs = slice(half * 8, half * 8 + 8)
            mk.dma(sp, hT[:, cs, :nt], v[:, cs, t0:t0 + nt], writes=hT_b[half * 8:half * 8 + 8])

    def store_h(dst, t0, nt):
        v = dst.rearrange("(c p) t -> p c t", p=128)
        for half in range(2):
            cs = slice(half * 8, half * 8 + 8)
            mk.dma(sp, v[:, cs, t0:t0 + nt], hT[:, cs, :nt], reads=hT_b[half * 8:half * 8 + 8])

    def prenorm(gcol, nt):
        rms_stats(hT, hT_b, nt)
        for c in range(NC_F):
            mk.op(dve, lambda c=c: nc.vector.scalar_tensor_tensor(out=xT[:, c, :nt], in0=hT[:, c, :nt], scalar=pp[:, gcol + c:gcol + c + 1],
                                                                 in1=rbc[:, :nt], op0=ALU.mult, op1=ALU.mult),
                  [hT_b[c], pp_b, rbc_b], [xT_b[c]])

    def postnorm_residual(pcol, nt, half):
        rms_stats(yT, yT_b, nt)
        for c in range(NC_F):
            mk.op(dve, lambda c=c: nc.vector.scalar_tensor_tensor(out=yT[:, c, :nt], in0=yT[:, c, :nt], scalar=pp[:, pcol + c:pcol + c + 1],
                                                                 in1=rbc[:, :nt], op0=ALU.mult, op1=ALU.mult),
                  [yT_b[c], pp_b, rbc_b], [yT_b[c]])
            mk.op(dve, lambda c=c: nc.vector.scalar_tensor_tensor(out=hT[:, c, :nt], in0=yT[:, c, :nt], scalar=float(half),
                                                                 in1=hT[:, c, :nt], op0=ALU.mult, op1=ALU.add),
                  [yT_b[c], hT_b[c]], [hT_b[c]])

    def load_params(l):
        mk.dma(sp, pp[:], ppd[l], writes=[pp_b])
        mk.dma(pool, lwt[:], lwd[l].rearrange("k p n -> p k n"), writes=[lw_b])

    def ffn_phase(l, wgu, wdn, gcol, pcol, src, dst):
        with ExitStack() as st:
            hid = sbt(st, "hid", [128, NJ_FF, TT], BF16)
            hid_b = [Buf("hid%d" % j) for j in range(NJ_FF)]
            wg = [sbt(st, "wg%d" % i, [128, NC_F, 256], BF16) for i in range(2)]
            wg_b = [Buf("wg%d" % i) for i in range(2)]
            wd = [sbt(st, "wd%d" % i, [128, NJ_FF, 128], BF16) for i in range(2)]
            wd_b = [Buf("wd%d" % i) for i in range(2)]
            sg = [sbt(st, "sg%d" % i, [128, TT], F32) for i in range(2)]
            sg_b = [Buf("sg%d" % i) for i in range(2)]
            nt = TT
            for ti in range(T_PAD // TT):
                t0 = ti * TT
                load_h(src, t0, nt)
                prenorm(gcol, nt)
                for j in range(NJ_FF):
                    s = j % 2
                    mk.dma(pool, wg[s][:], wgu[l, j].rearrange("p (c n) -> p c n", n=256), writes=[wg_b[s]])
                    ba, bba = bank()
                    bu, bbu = bank()
                    for c in range(NC_F):
                        mk.op(pe, lambda c=c, s=s: nc.tensor.matmul(pap(ba)[:, :nt], wg[s][:, c, 0:128], xT[:, c, :nt], start=(c == 0), stop=(c == NC_F - 1)),
                              [wg_b[s], xT_b[c]], bba, signal=(c == NC_F - 1))
                    for c in range(NC_F):
                        mk.op(pe, lambda c=c, s=s: nc.tensor.matmul(pap(bu)[:, :nt], wg[s][:, c, 128:256], xT[:, c, :nt], start=(c == 0), stop=(c == NC_F - 1)),
                              [wg_b[s], xT_b[c]], bbu, signal=(c == NC_F - 1))
                    mk.op(act, lambda s=s: nc.scalar.activation(out=sg[s][:, :nt], in_=pap(ba)[:, :nt], func=AF.Silu), bba, [sg_b[s]])
                    mk.op(dve, lambda s=s, j=j: nc.vector.tensor_tensor(out=hid[:, j, :nt], in0=sg[s][:, :nt], in1=pap(bu)[:, :nt], op=ALU.mult),
                          [sg_b[s]] + bbu, [hid_b[j]])
                for fc in range(NC_F):
                    s = fc % 2
                    mk.dma(pool, wd[s][:], wdn[l, fc].rearrange("p (k n) -> p k n", n=128), writes=[wd_b[s]])
                    b, bb = bank()
                    for kc in range(NJ_FF):
                        mk.op(pe, lambda kc=kc, s=s: nc.tensor.matmul(pap(b)[:, :nt], wd[s][:, kc, :], hid[:, kc, :nt], start=(kc == 0), stop=(kc == NJ_FF - 1)),
                              [wd_b[s], hid_b[kc]], bb, signal=(kc == NJ_FF - 1))
                    mk.op(act, lambda fc=fc: nc.scalar.copy(out=yT[:, fc, :nt], in_=pap(b)[:, :nt]), bb, [yT_b[fc]])
                postnorm_residual(pcol, nt, 0.5)
                store_h(dst, t0, nt)
        mk.barrier()

    import os as _os2
    MIXSTOP = int(_os2.environ.get("MK_MIXSTOP", "0"))

    DBG = int(_os2.environ.get("MK_DBG", "0"))

    def mm64(pt, col, L, a, hh, c, rhs, stop=True):
        if hh == 0:
            return nc.tensor.matmul(pt[0:64, col:col + 64], L[0:64, a, c * 64:c * 64 + 64], rhs, start=True, stop=stop)
        return nc.tensor.matmul(pt[:, col:col + 64], L[64:128, a, c * 64:c * 64 + 128], rhs, start=True, stop=stop)

    def chk(n):
        if n == MIXSTOP:
            raise _Stop()

    def mixer_phase(l, src, dst):
        with ExitStack() as st:
            nt = TM
            NCH = TM // 64
            wi = [sbt(st, "wi%d" % i, [128, NC_F, 128], BF16) for i in range(3)]
            wi_b = [Buf("wi%d" % i) for i in range(3)]
            wrr = [0]

            def wslot():
                s = wrr[0]
                wrr[0] = (s + 1) % 3
                return s

            def f32t(name, w=TM):
                return sbt(st, name, [128, w], F32), Buf(name)

            def bf16t(name, w=TM):
                return sbt(st, name, [128, w], BF16), Buf(name)

            praw, praw_b = f32t("praw", TM + 1)
            dtmp, dtmp_b = f32t("dtmp")
            L48, L48_b = f32t("L48")
            L49, L49_b = f32t("L49")
            L50, L50_b = f32t("L50")
            LB48, LB48_b = bf16t("LB48")
            LB49, LB49_b = bf16t("LB49")
            LB50, LB50_b = bf16t("LB50")
            gcs, gcs_b = f32t("gcs")
            uc, uc_b = f32t("uc", TM + 2)
            cv, cv_b = f32t("cv")
            zg = sbt(st, "zg", [128, 8, TM], F32)
            zg_b = [Buf("zg%d" % q) for q in range(8)]
            zsq, zsq_b = bf16t("zsq")
            opT = sbt(st, "opT", [128, NC_F, TM], BF16)
            opT_b = [Buf("opT%d" % q) for q in range(NC_F)]
            rT, rT_b = f32t("rT")
            kT, kT_b = f32t("kT")
            vT, vT_b = f32t("vT")
            names = ["sw", "aa", "gg", "sv", "vf", "kk", "rt", "rn", "kkn", "t1", "kmod", "bvec", "csa", "csb",
                     "wincl", "exc", "wexc", "winv", "c1", "c2", "bs"]
            F = {}
            FB = {}
            for n_ in names:
                F[n_], FB[n_] = f32t(n_)
            kksq, kksq_b = bf16t("kksq")
            rkb, rkb_b = bf16t("rkb")
            vbf, vbf_b = bf16t("vbf")
            AR = sbt(st, "AR", [128, 2, TM + 64], BF16)
            AR_b = Buf("AR")
            BK = sbt(st, "BK", [128, 2, TM + 64], BF16)
            BK_b = Buf("BK")
            Vtok = sbt(st, "Vtok", [64, NCH, 128], BF16)
            Btok = sbt(st, "Btok", [64, NCH, 128], BF16)
            Ktok = sbt(st, "Ktok", [64, NCH, 128], BF16)
            Vtok_b, Btok_b, Ktok_b = Buf("Vtok"), Buf("Btok"), Buf("Ktok")
            NG = NCH * 2
            NA = sbt(st, "NA", [64, NG, 128], BF16)
            KA = sbt(st, "KA", [64, NG, 128], BF16)
            NA_b, KA_b = Buf("NA"), Buf("KA")
            Qs = [sbt(st, "Q%d" % i, [64, NG, 64], BF16) for i in range(2)]
            Qts = [sbt(st, "Qt%d" % i, [64, NG, 64], BF16) for i in range(2)]
            Ps = [sbt(st, "P%d" % i, [64, NG, 64], BF16) for i in range(2)]
            Q_b = [Buf("Q%d" % i) for i in range(2)]
            Qt_b = [Buf("Qt%d" % i) for i in range(2)]
            P_b = [Buf("P%d" % i) for i in range(2)]
            ru = sbt(st, "ru", [64, 2, 64], BF16)
            ru_b = Buf("ru")
            Us = sbt(st, "Us", [64, 2, 64], BF16)
            Us_b = Buf("Us")
            ysb = sbt(st, "ysb", [64, NCH, 128], F32)
            ysb_b = Buf("ysb")
            ysq = sbt(st, "ysq", [64, NCH, 128], F32)
            ysq_b = Buf("ysq")
            yn = sbt(st, "yn", [64, NCH, 128], F32)
            yn_b = Buf("yn")
            st1 = sbt(st, "st1", [64, 6, NG], F32)
            st_b = Buf("st1")

            if l == 0:
                pass
            mk.op(dve, lambda: nc.vector.memset(AR[:], 0.0), [], [AR_b])
            mk.op(dve, lambda: nc.vector.memset(BK[:], 0.0), [], [BK_b])
            mk.op(dve, lambda: nc.vector.memset(carry[:], 0.0), [], carry_b)
            mk.op(dve, lambda: nc.vector.memset(ccar[:], 0.0), [], ccar_b)
            mk.op(dve, lambda: nc.vector.memset(Hf[:], 0.0), [], Hf_b)
            mk.op(dve, lambda: nc.vector.memset(Hb[:], 0.0), [], Hb_b)

            def proj(j):
                s = wslot()
                mk.dma(pool, wi[s][:], win[l, j].rearrange("p (c n) -> p c n", n=128), writes=[wi_b[s]])
                b, bb = bank()
                for c in range(NC_F):
                    mk.op(pe, lambda c=c, s=s: nc.tensor.matmul(pap(b)[:, :nt], wi[s][:, c, :], xT[:, c, :nt], start=(c == 0), stop=(c == NC_F - 1)),
                          [wi_b[s], xT_b[c]], bb, signal=(c == NC_F - 1))
                return b, bb

            def shifted(j, dst_t, dst_b):
                m = j - 24
                b, bb = proj(j)
                mk.op(act, lambda: nc.scalar.copy(out=praw[:, 1:nt + 1], in_=pap(b)[:, :nt]), bb, [praw_b])
                mk.op(act, lambda: nc.scalar.copy(out=praw[:, 0:1], in_=carry[:, m:m + 1]), [carry_b[m]], [praw_b])
                mk.op(act, lambda: nc.scalar.copy(out=carry[:, m:m + 1], in_=praw[:, nt:nt + 1]), [praw_b], [carry_b[m]])
                mk.op(dve, lambda: nc.vector.tensor_tensor(out=dtmp[:, :nt], in0=praw[:, 0:nt], in1=praw[:, 1:nt + 1], op=ALU.subtract),
                      [praw_b], [dtmp_b])
                mk.op(dve, lambda: nc.vector.scalar_tensor_tensor(out=dst_t[:, :nt], in0=dtmp[:, :nt], scalar=pp[:, PP_MU + m:PP_MU + m + 1],
                                                                 in1=praw[:, 1:nt + 1], op0=ALU.mult, op1=ALU.add),
                      [dtmp_b, praw_b, pp_b], [dst_b])

            for ti in range(T_PAD // TM):
                t0 = ti * TM
                load_h(src, t0, nt)
                prenorm(PP_GM, nt)
                try:
                    shifted(48, L48, L48_b)
                    shifted(49, L49, L49_b)
                    shifted(50, L50, L50_b)
                    mk.op(act, lambda: nc.scalar.activation(out=LB48[0:64, :nt], in_=L48[0:64, :nt], func=AF.Tanh), [L48_b], [LB48_b])
                    mk.op(act, lambda: nc.scalar.copy(out=LB48[64:128, :nt], in_=L48[64:128, :nt]), [L48_b], [LB48_b])
                    mk.op(act, lambda: nc.scalar.activation(out=LB49[:, :nt], in_=L49[:, :nt], func=AF.Sigmoid), [L49_b], [LB49_b])
                    mk.op(act, lambda: nc.scalar.activation(out=LB50[0:32, :nt], in_=L50[0:32, :nt], func=AF.Sigmoid), [L50_b], [LB50_b])
                    mk.op(act, lambda: nc.scalar.copy(out=LB50[32:64, :nt], in_=L50[32:64, :nt]), [L50_b], [LB50_b])
                    chk(1)
                    bz, bbz = 7, pbank[7:8]
                    for q in range(8):
                        bgb, bbgb = proj(q)
                        bgc, bbgc = proj(8 + q)
                        bu_, bbu_ = proj(16 + q)
                        mk.op(act, lambda: nc.scalar.copy(out=gcs[:, :nt], in_=pap(bgc)[:, :nt]), bbgc, [gcs_b])
                        mk.op(act, lambda q=q: nc.scalar.copy(out=uc[:, 0:2], in_=ccar[:, q, :]), [ccar_b[q]], [uc_b])
                        mk.op(dve, lambda: nc.vector.tensor_tensor(out=uc[:, 2:nt + 2], in0=gcs[:, :nt], in1=pap(bu_)[:, :nt], op=ALU.mult),
                              [gcs_b] + bbu_, [uc_b])
                        mk.op(act, lambda q=q: nc.scalar.copy(out=ccar[:, q, :], in_=uc[:, nt:nt + 2]), [uc_b], [ccar_b[q]])
                        cw = PP_CW + q
                        mk.op(dve, lambda cw=cw: nc.vector.tensor_scalar(out=cv[:, :nt], in0=uc[:, 2:nt + 2], scalar1=pp[:, cw + 16:cw + 17], scalar2=None, op0=ALU.mult),
                              [uc_b, pp_b], [cv_b])
                        mk.op(dve, lambda cw=cw: nc.vector.scalar_tensor_tensor(out=cv[:, :nt], in0=uc[:, 1:nt + 1], scalar=pp[:, cw + 8:cw + 9], in1=cv[:, :nt],
                                                                               op0=ALU.mult, op1=ALU.add), [uc_b, pp_b, cv_b], [cv_b])
                        mk.op(dve, lambda cw=cw: nc.vector.scalar_tensor_tensor(out=cv[:, :nt], in0=uc[:, 0:nt], scalar=pp[:, cw:cw + 1], in1=cv[:, :nt],
                                                                               op0=ALU.mult, op1=ALU.add), [uc_b, pp_b, cv_b], [cv_b])
                        mk.op(dve, lambda: nc.vector.tensor_tensor(out=cv[:, :nt], in0=cv[:, :nt], in1=pap(bgb)[:, :nt], op=ALU.mult),
                              [cv_b] + bbgb, [cv_b])
                        mk.op(act, lambda: nc.scalar.activation(out=zsq[:, :nt], in_=cv[:, :nt], func=AF.Square), [cv_b], [zsq_b])
                        mk.op(pe, lambda q=q: nc.tensor.matmul(pap(bz)[:, :nt], ones_bf[:], zsq[:, :nt], start=(q == 0), stop=(q == 7)),
                              [cbf_b, zsq_b], bbz)
                        mk.op(dve, lambda q=q: nc.vector.tensor_scalar(out=zg[:, q, :nt], in0=cv[:, :nt], scalar1=pp[:, PP_CG + q:PP_CG + q + 1], scalar2=None, op0=ALU.mult),
                              [cv_b, pp_b], [zg_b[q]])
                    mk.op(act, lambda: nc.scalar.activation(out=rbt[:, :nt], in_=pap(bz)[:, :nt], func=AF.Sqrt, bias=NORM_EPS, scale=1.0 / 1024), bbz, [rbt_b])
                    mk.op(dve, lambda: nc.vector.reciprocal(out=rbc[:, :nt], in_=rbt[:, :nt]), [rbt_b], [rbc_b])
                    for q in range(8):
                        mk.op(dve, lambda q=q: nc.vector.tensor_tensor(out=opT[:, q, :nt], in0=zg[:, q, :nt], in1=rbc[:, :nt], op=ALU.mult),
                              [zg_b[q], rbc_b], [opT_b[q]])
                    chk(2)
                    for fc in range(8):
                        shifted(24 + fc, rT, rT_b)
                        shifted(32 + fc, kT, kT_b)
                        shifted(40 + fc, vT, vT_b)
                        fcs = slice(fc * 128, fc * 128 + 128)
                        bd, bbd = bank()
                        mk.op(pe, lambda: nc.tensor.matmul(pap(bd)[:, :nt], lwt[0:64, 0, fcs], LB48[0:64, :nt], start=True, stop=True), [lw_b, LB48_b], bbd)
                        mk.op(act, lambda: nc.scalar.activation(out=F["sw"][:, :nt], in_=pap(bd)[:, :nt], func=AF.Sigmoid, bias=pp[:, PP_W0 + fc:PP_W0 + fc + 1]),
                              bbd + [pp_b], [FB["sw"]])
                        ba_, bba_ = bank()
                        mk.op(pe, lambda: nc.tensor.matmul(pap(ba_)[:, :nt], lwt[64:128, 0, fcs], LB48[64:128, :nt], start=True, stop=True), [lw_b, LB48_b], bba_)
                        mk.op(act, lambda: nc.scalar.activation(out=F["aa"][:, :nt], in_=pap(ba_)[:, :nt], func=AF.Sigmoid, bias=pp[:, PP_A0 + fc:PP_A0 + fc + 1]),
                              bba_ + [pp_b], [FB["aa"]])
                        bg_, bbg_ = bank()
                        mk.op(pe, lambda: nc.tensor.matmul(pap(bg_)[:, :nt], lwt[:, 1, fcs], LB49[:, :nt], start=True, stop=False), [lw_b, LB49_b], bbg_, signal=False)
                        mk.op(pe, lambda: nc.tensor.matmul(pap(bg_)[:, :nt], lwt[0:32, 2, fcs], LB50[0:32, :nt], start=False, stop=True), [lw_b, LB50_b], bbg_)
                        mk.op(act, lambda: nc.scalar.copy(out=F["gg"][:, :nt], in_=pap(bg_)[:, :nt]), bbg_, [FB["gg"]])
                        vfv = vfd[fc * 128:(fc + 1) * 128, t0:t0 + nt]
                        if l == 0:
                            mk.dma(sp, vfv, vT[:, :nt], reads=[vT_b])
                        else:
                            bv_, bbv_ = bank()
                            mk.op(pe, lambda: nc.tensor.matmul(pap(bv_)[:, :nt], lwt[32:64, 2, fcs], LB50[32:64, :nt], start=True, stop=True), [lw_b, LB50_b], bbv_)
                            mk.op(act, lambda: nc.scalar.activation(out=F["sv"][:, :nt], in_=pap(bv_)[:, :nt], func=AF.Sigmoid, bias=pp[:, PP_V0 + fc:PP_V0 + fc + 1]),
                                  bbv_ + [pp_b], [FB["sv"]])
                            mk.dma(sp, F["vf"][:, :nt], vfv, writes=[FB["vf"]])
                            mk.op(dve, lambda: nc.vector.tensor_tensor(out=F["vf"][:, :nt], in0=F["vf"][:, :nt], in1=vT[:, :nt], op=ALU.subtract), [FB["vf"], vT_b], [FB["vf"]])
                            mk.op(dve, lambda: nc.vector.tensor_tensor(out=F["vf"][:, :nt], in0=F["vf"][:, :nt], in1=F["sv"][:, :nt], op=ALU.mult), [FB["vf"], FB["sv"]], [FB["vf"]])
                            mk.op(dve, lambda: nc.vector.tensor_tensor(out=vT[:, :nt], in0=vT[:, :nt], in1=F["vf"][:, :nt], op=ALU.add), [FB["vf"], vT_b], [vT_b])
                        mk.op(dve, lambda: nc.vector.tensor_scalar(out=F["kk"][:, :nt], in0=kT[:, :nt], scalar1=pp[:, PP_KK + fc:PP_KK + fc + 1], scalar2=None, op0=ALU.mult),
                              [kT_b, pp_b], [FB["kk"]])
                        mk.op(act, lambda: nc.scalar.activation(out=kksq[:, :nt], in_=F["kk"][:, :nt], func=AF.Square), [FB["kk"]], [kksq_b])
                        bn_, bbn_ = bank()
                        mk.op(pe, lambda: nc.tensor.matmul(pap(bn_)[:, :nt], blk_bf[:], kksq[:, :nt], start=True, stop=True), [cbf_b, kksq_b], bbn_)
                        mk.op(act, lambda: nc.scalar.activation(out=F["rt"][:, :nt], in_=pap(bn_)[:, :nt], func=AF.Sqrt, bias=1e-24, scale=1.0), bbn_, [FB["rt"]])
                        mk.op(dve, lambda: nc.vector.reciprocal(out=F["rn"][:, :nt], in_=F["rt"][:, :nt]), [FB["rt"]], [FB["rn"]])
                        mk.op(dve, lambda: nc.vector.tensor_tensor(out=F["kkn"][:, :nt], in0=F["kk"][:, :nt], in1=F["rn"][:, :nt], op=ALU.mult), [FB["kk"], FB["rn"]], [FB["kkn"]])
                        mk.op(dve, lambda: nc.vector.tensor_scalar(out=F["t1"][:, :nt], in0=F["aa"][:, :nt], scalar1=-1.0, scalar2=pp[:, PP_KA + fc:PP_KA + fc + 1],
                                                                   op0=ALU.add, op1=ALU.mult), [FB["aa"], pp_b], [FB["t1"]])
                        mk.op(dve, lambda: nc.vector.scalar_tensor_tensor(out=F["kmod"][:, :nt], in0=F["t1"][:, :nt], scalar=1.0, in1=kT[:, :nt], op0=ALU.add, op1=ALU.mult),
                              [FB["t1"], kT_b], [FB["kmod"]])
                        mk.op(dve, lambda: nc.vector.tensor_tensor(out=F["bvec"][:, :nt], in0=F["kkn"][:, :nt], in1=F["aa"][:, :nt], op=ALU.mult), [FB["kkn"], FB["aa"]], [FB["bvec"]])
                        mk.op(dve, lambda: nc.vector.scalar_tensor_tensor(out=rkb[:, :nt], in0=rT[:, :nt], scalar=pp[:, PP_RK + fc:PP_RK + fc + 1], in1=F["kmod"][:, :nt],
                                                                         op0=ALU.mult, op1=ALU.mult), [rT_b, pp_b, FB["kmod"]], [rkb_b])
                        bb_, bbb_ = bank()
                        mk.op(pe, lambda: nc.tensor.matmul(pap(bb_)[:, :nt], blk_bf[:], rkb[:, :nt], start=True, stop=True), [cbf_b, rkb_b], bbb_)
                        mk.op(dve, lambda: nc.vector.tensor_tensor(out=F["bs"][:, :nt], in0=vT[:, :nt], in1=pap(bb_)[:, :nt], op=ALU.mult), [vT_b] + bbb_, [FB["bs"]])
                        mk.op(dve, lambda: nc.vector.scalar_tensor_tensor(out=F["c1"][:, :nt], in0=F["bs"][:, :nt], scalar=pp[:, PP_LB + fc:PP_LB + fc + 1], in1=F["gg"][:, :nt],
                                                                         op0=ALU.add, op1=ALU.mult), [FB["bs"], pp_b, FB["gg"]], [FB["c1"]])
                        mk.op(dve, lambda: nc.vector.tensor_scalar(out=F["c2"][:, :nt], in0=F["gg"][:, :nt], scalar1=pp[:, PP_LG + fc:PP_LG + fc + 1], scalar2=None, op0=ALU.mult),
                              [FB["gg"], pp_b], [FB["c2"]])
                        cur, curb = F["sw"], FB["sw"]
                        pingpong = [(F["csa"], FB["csa"]), (F["csb"], FB["csb"])]
                        for si, sh in enumerate([1, 2, 4, 8, 16, 32]):
                            nxt, nxtb = pingpong[si % 2]
                            cv3 = cur[:, :nt].rearrange("p (c t) -> p c t", t=64)
                            nv3 = nxt[:, :nt].rearrange("p (c t) -> p c t", t=64)
                            mk.op(act, lambda cv3=cv3, nv3=nv3, sh=sh: nc.scalar.copy(out=nv3[:, :, 0:sh], in_=cv3[:, :, 0:sh]), [curb], [nxtb])
                            mk.op(dve, lambda cv3=cv3, nv3=nv3, sh=sh: nc.vector.tensor_tensor(out=nv3[:, :, sh:64], in0=cv3[:, :, sh:64], in1=cv3[:, :, 0:64 - sh], op=ALU.add),
                                  [curb], [nxtb])
                            cur, curb = nxt, nxtb
                        cum, cum_b = cur, curb
                        mk.op(act, lambda: nc.scalar.activation(out=F["wincl"][:, :nt], in_=cum[:, :nt], func=AF.Exp, scale=-DEC_C), [cum_b], [FB["wincl"]])
                        mk.op(dve, lambda: nc.vector.tensor_tensor(out=F["exc"][:, :nt], in0=cum[:, :nt], in1=F["sw"][:, :nt], op=ALU.subtract), [cum_b, FB["sw"]], [FB["exc"]])
                        mk.op(act, lambda: nc.scalar.activation(out=F["wexc"][:, :nt], in_=F["exc"][:, :nt], func=AF.Exp, scale=-DEC_C), [FB["exc"]], [FB["wexc"]])
                        mk.op(act, lambda: nc.scalar.activation(out=F["winv"][:, :nt], in_=cum[:, :nt], func=AF.Exp, scale=DEC_C), [cum_b], [FB["winv"]])
                        mk.op(dve, lambda: nc.vector.scalar_tensor_tensor(out=AR[:, 0, :nt], in0=F["kkn"][:, :nt], scalar=-1.0, in1=F["wexc"][:, :nt], op0=ALU.mult, op1=ALU.mult),
                              [FB["kkn"], FB["wexc"]], [AR_b])
                        mk.op(dve, lambda: nc.vector.tensor_tensor(out=AR[:, 1, :nt], in0=rT[:, :nt], in1=F["wincl"][:, :nt], op=ALU.mult), [rT_b, FB["wincl"]], [AR_b])
                        mk.op(dve, lambda: nc.vector.tensor_tensor(out=BK[:, 0, :nt], in0=F["bvec"][:, :nt], in1=F["winv"][:, :nt], op=ALU.mult), [FB["bvec"], FB["winv"]], [BK_b])
                        mk.op(dve, lambda: nc.vector.tensor_tensor(out=BK[:, 1, :nt], in0=F["kmod"][:, :nt], in1=F["winv"][:, :nt], op=ALU.mult), [FB["kmod"], FB["winv"]], [BK_b])
                        mk.op(act, lambda: nc.scalar.copy(out=vbf[:, :nt], in_=vT[:, :nt]), [vT_b], [vbf_b])
                        chk(3)
                        for (srcap, srcb, dstt, dstb) in ((vbf, vbf_b, Vtok, Vtok_b), (BK[:, 0, :], BK_b, Btok, Btok_b), (BK[:, 1, :], BK_b, Ktok, Ktok_b)):
                            bt, bbt = bank()
                            pv = pap(bt).bitcast(BF16)
                            for c in range(NCH):
                                mk.op(pe, lambda c=c, srcap=srcap, pv=pv: nc.tensor.transpose(pv[0:64, c * 128:(c + 1) * 128], srcap[:, c * 64:(c + 1) * 64], idb128[:]),
                                      [srcb, cbf_b], bbt, signal=(c == NCH - 1))
                            mk.op(act, lambda dstt=dstt, pv=pv: nc.scalar.copy(out=dstt[:].rearrange("p c f -> p (c f)"), in_=pv[0:64, 0:NCH * 128]), bbt, [dstb])
                        chk(4)
                        b1, bb1 = bank(2)
                        b2, bb2 = bank(2)
                        b3, bb3 = bank()
                        p1 = pap(b1, 2)
                        p2 = pap(b2, 2)
                        p1v = p1[0:64, 0:NG * 128].rearrange("p (g a n) -> p g a n", a=2, n=64)
                        p2v = p2[0:64, 0:NG * 128].rearrange("p (g a n) -> p g a n", a=2, n=64)
                        p3 = pap(b3)
                        for c in range(NCH):
                            for hh in range(1 if DBG == 1 else 2):
                                g = c * 2 + hh
                                hp = slice(hh * 64, hh * 64 + 64)
                                cc = slice(c * 64, c * 64 + 64)
                                last = (g == (NG - 2 if DBG == 1 else NG - 1))
                                mk.op(pe, lambda g=g, hp=hp, cc=cc, hh=hh, c=c: (mm64(p1, g * 128, BK, 0, hh, c, AR[hp, 0, cc]), mm64(p1, g * 128 + 64, BK, 0, hh, c, AR[hp, 1, cc]))[1],
                                      [BK_b, AR_b], bb1, signal=last)
                        for c in range(NCH):
                            for hh in range(1 if DBG == 1 else 2):
                                g = c * 2 + hh
                                hp = slice(hh * 64, hh * 64 + 64)
                                cc = slice(c * 64, c * 64 + 64)
                                last = (g == (NG - 2 if DBG == 1 else NG - 1))
                                mk.op(pe, lambda g=g, hp=hp, cc=cc, hh=hh, c=c: (mm64(p2, g * 128, BK, 1, hh, c, AR[hp, 0, cc]), mm64(p2, g * 128 + 64, BK, 1, hh, c, AR[hp, 1, cc]))[1],
                                      [BK_b, AR_b], bb2, signal=last)
                        for c in range(NCH):
                            for hh in range(1 if DBG == 1 else 2):
                                g = c * 2 + hh
                                hp = slice(hh * 64, hh * 64 + 64)
                                cc = slice(c * 64, c * 64 + 64)
                                last = (g == (NG - 2 if DBG == 1 else NG - 1))
                                mk.op(pe, lambda g=g, hp=hp, cc=cc, hh=hh, c=c: mm64(p3, g * 64, AR, 0, hh, c, BK[hp, 0, cc]),
                                      [BK_b, AR_b], bb3, signal=last)
                        for hb_ in range(2):
                            gs = slice(hb_ * 4, hb_ * 4 + 4)
                            mk.op(dve, lambda hb_=hb_, gs=gs: nc.vector.tensor_tensor(out=NA[:, gs, :], in0=p1[0:64, hb_ * 512:(hb_ + 1) * 512].rearrange("p (g n) -> p g n", n=128),
                                                                                     in1=mask3(None, 0)[:, 0:4, :], op=ALU.mult), bb1 + [cst_fb], [NA_b])
                            mk.op(dve, lambda hb_=hb_, gs=gs: nc.vector.tensor_tensor(out=KA[:, gs, :], in0=p2[0:64, hb_ * 512:(hb_ + 1) * 512].rearrange("p (g n) -> p g n", n=128),
                                                                                     in1=mask3(None, 0)[:, 0:4, :], op=ALU.mult), bb2 + [cst_fb], [KA_b])
                        mk.op(dve, lambda: nc.vector.tensor_tensor(out=Qts[0][:], in0=p3[0:64, 0:NG * 64].rearrange("p (g n) -> p g n", n=64),
                                                                   in1=mask3(None, 1), op=ALU.mult), bb3 + [cst_fb], [Qt_b[0]])
                        chk(5)
                        mk.op(act, lambda: nc.scalar.copy(out=Qs[0][:], in_=NA[:, :, 0:64]), [NA_b], [Q_b[0]])
                        mk.op(dve, lambda: nc.vector.tensor_tensor(out=Ps[0][:], in0=NA[:, :, 0:64], in1=mask3(None, 2), op=ALU.add), [NA_b, cst_fb], [P_b[0]])
                        qi = 0
                        pi = 0
                        for m_ in range(1, 6):
                            qn = 1 - qi
                            pn = 1 - pi
                            if m_ < 5:
                                bq, bbq = bank()
                                pq = pap(bq)
                                for g in range(NG):
                                    mk.op(pe, lambda g=g, qi=qi, pq=pq: nc.tensor.matmul(pq[0:64, g * 64:(g + 1) * 64], Qts[qi][:, g, :], Qs[qi][:, g, :], start=True, stop=True),
                                          [Q_b[qi], Qt_b[qi]], bbq, signal=(g == NG - 1))
                            bqt, bbqt = bank()
                            pqt = pap(bqt)
                            for g in range(NG):
                                mk.op(pe, lambda g=g, qi=qi, pqt=pqt: nc.tensor.matmul(pqt[0:64, g * 64:(g + 1) * 64], Qs[qi][:, g, :], Qts[qi][:, g, :], start=True, stop=True),
                                      [Q_b[qi], Qt_b[qi]], bbqt, signal=(g == NG - 1))
                            if m_ < 5:
                                mk.op(act, lambda qn=qn, pq=pq: nc.scalar.copy(out=Qs[qn][:], in_=pq[0:64, 0:NG * 64].rearrange("p (g n) -> p g n", n=64)), bbq, [Q_b[qn]])
                            mk.op(dve, lambda qn=qn, pqt=pqt: nc.vector.tensor_copy(out=Qts[qn][:], in_=pqt[0:64, 0:NG * 64].rearrange("p (g n) -> p g n", n=64)), bbqt, [Qt_b[qn]])
                            bp_, bbp_ = bank()
                            pp_ = pap(bp_)
                            for g in range(NG):
                                mk.op(pe, lambda g=g, pi=pi, pp_=pp_: nc.tensor.matmul(pp_[0:64, g * 64:(g + 1) * 64], idb[:], Ps[pi][:, g, :], start=True, stop=False),
                                      [cbf_b, P_b[pi]], bbp_, signal=False)
                                mk.op(pe, lambda g=g, pi=pi, qn=qn, pp_=pp_: nc.tensor.matmul(pp_[0:64, g * 64:(g + 1) * 64], Qts[qn][:, g, :], Ps[pi][:, g, :], start=False, stop=True),
                                      [Qt_b[qn], P_b[pi]], bbp_, signal=(g == NG - 1))
                            mk.op(act, lambda pn=pn, pp_=pp_: nc.scalar.copy(out=Ps[pn][:], in_=pp_[0:64, 0:NG * 64].rearrange("p (g n) -> p g n", n=64)), bbp_, [P_b[pn]])
                            qi = qn
                            pi = pn
                        Pf, Pf_b = Ps[pi], P_b[pi]
                        chk(6)
                        for c in range(NCH):
                            cc = slice(c * 64, c * 64 + 64)
                            br, bbr = bank()
                            pr_ = pap(br)
                            for hh in range(2):
                                g = c * 2 + hh
                                hp = slice(hh * 64, hh * 64 + 64)
                                mk.op(pe, lambda hh=hh, hp=hp: mm64(pr_, hh * 64, AR, 0, hh, c, Hb[hp, fc, :], stop=False),
                                      [AR_b, Hb_b[fc]], bbr, signal=False)
                                mk.op(pe, lambda hh=hh, g=g: nc.tensor.matmul(pr_[0:64, hh * 64:(hh + 1) * 64], KA[:, g, 0:64], Vtok[:, c, hh * 64:(hh + 1) * 64], start=False, stop=True),
                                      [KA_b, Vtok_b], bbr, signal=(hh == 1))
                            mk.op(act, lambda: nc.scalar.copy(out=ru[:].rearrange("p h v -> p (h v)"), in_=pr_[0:64, 0:128]), bbr, [ru_b])
                            bu2, bbu2 = bank()
                            pu_ = pap(bu2)
                            for hh in range(2):
                                g = c * 2 + hh
                                mk.op(pe, lambda hh=hh, g=g: nc.tensor.matmul(pu_[0:64, hh * 64:(hh + 1) * 64], Pf[:, g, :], ru[:, hh, :], start=True, stop=True),
                                      [Pf_b, ru_b], bbu2, signal=(hh == 1))
                            mk.op(act, lambda: nc.scalar.copy(out=Us[:].rearrange("p h v -> p (h v)"), in_=pu_[0:64, 0:128]), bbu2, [Us_b])
                            by, bby = bank()
                            py_ = pap(by)
                            for hh in range(2):
                                g = c * 2 + hh
                                hp = slice(hh * 64, hh * 64 + 64)
                                mk.op(pe, lambda hh=hh, hp=hp: mm64(py_, hh * 64, AR, 1, hh, c, Hb[hp, fc, :], stop=False),
                                      [AR_b, Hb_b[fc]], bby, signal=False)
                                mk.op(pe, lambda hh=hh, g=g: nc.tensor.matmul(py_[0:64, hh * 64:(hh + 1) * 64], NA[:, g, 64:128], Us[:, hh, :], start=False, stop=False),
                                      [NA_b, Us_b], bby, signal=False)
                                mk.op(pe, lambda hh=hh, g=g: nc.tensor.matmul(py_[0:64, hh * 64:(hh + 1) * 64], KA[:, g, 64:128], Vtok[:, c, hh * 64:(hh + 1) * 64], start=False, stop=True),
                                      [KA_b, Vtok_b], bby, signal=(hh == 1))
                            mk.op(act, lambda c=c: nc.scalar.copy(out=ysb[:, c, :], in_=py_[0:64, 0:128]), bby, [ysb_b])
                            bh, bbh = bank()
                            ph_ = pap(bh)
                            for hh in range(2):
                                mk.op(pe, lambda hh=hh: nc.tensor.matmul(ph_[:, hh * 64:(hh + 1) * 64], Btok[:, c, :], Us[:, hh, :], start=True, stop=False),
                                      [Btok_b, Us_b], bbh, signal=False)
                                mk.op(pe, lambda hh=hh: nc.tensor.matmul(ph_[:, hh * 64:(hh + 1) * 64], Ktok[:, c, :], Vtok[:, c, hh * 64:(hh + 1) * 64], start=False, stop=True),
                                      [Ktok_b, Vtok_b], bbh, signal=(hh == 1))
                            wc = c * 64 + 63
                            for hh in range(2):
                                hp = slice(hh * 64, hh * 64 + 64)
                                mk.op(dve, lambda hp=hp: nc.vector.tensor_scalar(out=Hf[hp, fc, :], in0=Hf[hp, fc, :], scalar1=F["wincl"][hp, wc:wc + 1], scalar2=None, op0=ALU.mult),
                                      [Hf_b[fc], FB["wincl"]], [Hf_b[fc]])
                                mk.op(dve, lambda hp=hp, hh=hh: nc.vector.scalar_tensor_tensor(out=Hf[hp, fc, :], in0=ph_[hp, hh * 64:(hh + 1) * 64], scalar=F["wincl"][hp, wc:wc + 1],
                                                                                              in1=Hf[hp, fc, :], op0=ALU.mult, op1=ALU.add),
                                      bbh + [Hf_b[fc], FB["wincl"]], [Hf_b[fc]])
                            mk.op(act, lambda: nc.scalar.copy(out=Hb[:, fc, :], in_=Hf[:, fc, :]), [Hf_b[fc]], [Hb_b[fc]])
                        chk(7)
                        y3 = ysb[:].rearrange("p c (h v) -> p (c h) v", v=64)
                        q3 = ysq[:].rearrange("p c (h v) -> p (c h) v", v=64)
                        n3 = yn[:].rearrange("p c (h v) -> p (c h) v", v=64)
                        mk.op(dve, lambda: nc.vector.tensor_reduce(out=st1[:, 0, :], in_=y3, axis=AX.X, op=ALU.add), [ysb_b], [st_b])
                        mk.op(act, lambda: nc.scalar.activation(out=ysq[:], in_=ysb[:], func=AF.Square), [ysb_b], [ysq_b])
                        mk.op(dve, lambda: nc.vector.tensor_reduce(out=st1[:, 1, :], in_=q3, axis=AX.X, op=ALU.add), [ysq_b], [st_b])
                        mk.op(dve, lambda: nc.vector.tensor_scalar(out=st1[:, 2, :], in0=st1[:, 0, :], scalar1=1.0 / 64, scalar2=None, op0=ALU.mult), [st_b], [st_b])
                        mk.op(dve, lambda: nc.vector.tensor_tensor(out=st1[:, 3, :], in0=st1[:, 2, :], in1=st1[:, 2, :], op=ALU.mult), [st_b], [st_b])
                        mk.op(dve, lambda: nc.vector.scalar_tensor_tensor(out=st1[:, 4, :], in0=st1[:, 1, :], scalar=1.0 / 64, in1=st1[:, 3, :], op0=ALU.mult, op1=ALU.subtract),
                              [st_b], [st_b])
                        mk.op(act, lambda: nc.scalar.activation(out=st1[:, 5, :], in_=st1[:, 4, :], func=AF.Sqrt, bias=LNX_EPS, scale=1.0), [st_b], [st_b])
                        mk.op(dve, lambda: nc.vector.reciprocal(out=st1[:, 3, :], in_=st1[:, 5, :]), [st_b], [st_b])
                        mk.op(dve, lambda: nc.vector.tensor_tensor(out=n3, in0=y3, in1=st1[:, 2, :].unsqueeze(2).broadcast_to([64, NG, 64]), op=ALU.subtract),
                              [ysb_b, st_b], [yn_b])
                        mk.op(dve, lambda: nc.vector.tensor_tensor(out=n3, in0=n3, in1=st1[:, 3, :].unsqueeze(2).broadcast_to([64, NG, 64]), op=ALU.mult),
                              [yn_b, st_b], [yn_b])
                        chk(8)
                        bt2, bbt2 = bank()
                        pt2 = pap(bt2)
                        for c in range(NCH):
                            mk.op(pe, lambda c=c: nc.tensor.transpose(pt2[:, c * 64:(c + 1) * 64], yn[:, c, :], identf),
                                  [yn_b, cst_fb], bbt2, signal=(c == NCH - 1))
                        mk.op(dve, lambda: nc.vector.tensor_tensor(out=F["bs"][:, :nt], in0=F["c2"][:, :nt], in1=pt2[:, :nt], op=ALU.mult), [FB["c2"]] + bbt2, [FB["bs"]])
                        mk.op(dve, lambda: nc.vector.tensor_tensor(out=opT[:, 8 + fc, :nt], in0=F["bs"][:, :nt], in1=F["c1"][:, :nt], op=ALU.add), [FB["bs"], FB["c1"]], [opT_b[8 + fc]])
                    chk(9)
                    for fo in range(NC_F):
                        s = wslot()
                        mk.dma(pool, wi[s][:], wout[l, fo].rearrange("p (c n) -> p c n", n=128), writes=[wi_b[s]])
                        b, bb = bank()
                        for kc in range(NC_F):
                            mk.op(pe, lambda kc=kc, s=s: nc.tensor.matmul(pap(b)[:, :nt], wi[s][:, kc, :], opT[:, kc, :nt], start=(kc == 0), stop=(kc == NC_F - 1)),
                                  [wi_b[s], opT_b[kc]], bb, signal=(kc == NC_F - 1))
                        mk.op(act, lambda fo=fo: nc.scalar.copy(out=yT[:, fo, :nt], in_=pap(b)[:, :nt]), bb, [yT_b[fo]])
                except _Stop:
                    pass
                postnorm_residual(PP_PM, nt, 1.0)
                store_h(dst, t0, nt)
        mk.barrier()

    def mask3(_, kind):
        NG = (TM // 64) * 2
        if kind == 0:
            return cst2_f[:, 0:NG * 128].rearrange("p (g n) -> p g n", n=128)
        if kind == 1:
            return cst2_f[:, 1024:1024 + NG * 64].rearrange("p (g n) -> p g n", n=64)
        return cst2_f[:, 1536:1536 + NG * 64].rearrange("p (g n) -> p g n", n=64)

    for l in range(depth):
        load_params(l)
        src = h0 if l == 0 else hres
        last = (l == depth - 1)
        if "f1" in phases:
            ffn_phase(l, wgu1, wdn1, PP_G1, PP_P1, src, hout if (last and phases[-1] == "f1") else hres)
        if "mix" in phases:
            mixer_phase(l, hres if ("f1" in phases or l > 0) else h0, hout if (last and phases[-1] == "mix") else hres)
        if "f2" in phases:
            ffn_phase(l, wgu2, wdn2, PP_G2, PP_P2, hres, hout if last else hres)
    mk.finish()
    es.close()
    mk.dram_names = set(dr)
    return nc, mk


def _consts():
    c = np.zeros((128, 640), np.float32)
    c[:, 0:128] = 1.0
    c[0:64, 128:192] = 1.0
    c[64:128, 192:256] = 1.0
    j = np.arange(64)[:, None]
    i = np.arange(64)[None, :]
    c[0:64, 256:320] = (j < i)
    c[0:64, 320:384] = (j <= i)
    c[:, 384:512] = np.eye(128)
    c[0:64, 512:576] = (i < j)
    c[0:64, 576:640] = np.eye(64)
    return c


def _fm(v):
    return np.ascontiguousarray(v.reshape(-1, 128).T)


def prep_weights(inp, depth):
    f = np.float32
    out = {}
    for nm, key_gu, key_dn in (("1", "ffn1_w_gu", "ffn1_w_down"), ("2", "ffn2_w_gu", "ffn2_w_down")):
        wgu = np.asarray(inp[key_gu][:depth])
        g = wgu[:, :, :DFF].reshape(depth, NC_F, 128, NJ_FF, 128)
        u = wgu[:, :, DFF:].reshape(depth, NC_F, 128, NJ_FF, 128)
        gu = np.stack([g, u], axis=4)
        out["wgu" + nm] = np.ascontiguousarray(gu.transpose(0, 3, 2, 1, 4, 5)).reshape(depth, NJ_FF, 128, NC_F * 256)
        wd = np.asarray(inp[key_dn][:depth]).reshape(depth, NJ_FF, 128, NC_F, 128)
        out["wdn" + nm] = np.ascontiguousarray(wd.transpose(0, 3, 2, 1, 4)).reshape(depth, NC_F, 128, NJ_FF * 128)
    w_in = np.asarray(inp["w_in"][:depth])
    wcat = np.zeros((depth, D, NJ_IN * 128), f)
    wcat[:, :, :6432] = w_in
    for l in range(1, depth):
        wcat[l, :, 6432:6464] = inp["w_in_vres"][l - 1]
    wc = wcat.reshape(depth, NC_F, 128, NJ_IN, 128)
    out["win"] = np.ascontiguousarray(wc.transpose(0, 3, 2, 1, 4)).reshape(depth, NJ_IN, 128, NC_F * 128)
    wo = np.asarray(inp["w_out"][:depth]).reshape(depth, NC_F, 128, NC_F, 128)
    out["wout"] = np.ascontiguousarray(wo.transpose(0, 3, 2, 1, 4)).reshape(depth, NC_F, 128, NC_F * 128)
    lw = np.zeros((depth, 3, 128, RW), f)
    pp = np.zeros((depth, 128, NPP), f)
    for l in range(depth):
        lw[l, 0, 0:64] = inp["decay_w2"][l]
        lw[l, 0, 64:128] = inp["iclr_a2"][l]
        lw[l, 1] = inp["gate_g2"][l][0:128]
        lw[l, 2, 0:32] = inp["gate_g2"][l][128:160]
        if l >= 1:
            lw[l, 2, 32:64] = inp["vres_v2"][l - 1]
        pp[l, :, PP_G1:PP_G1 + 16] = _fm(inp["ffn1_pre_g"][l])
        pp[l, :, PP_GM:PP_GM + 16] = _fm(inp["mix_pre_g"][l])
        pp[l, :, PP_G2:PP_G2 + 16] = _fm(inp["ffn2_pre_g"][l])
        pp[l, :, PP_P1:PP_P1 + 16] = _fm(inp["ffn1_post_g"][l])
        pp[l, :, PP_PM:PP_PM + 16] = _fm(inp["mix_post_g"][l])
        pp[l, :, PP_P2:PP_P2 + 16] = _fm(inp["ffn2_post_g"][l])
        mu = np.zeros(27 * 128, f)
        mu[:3360] = inp["mu_rwkv"][l]
        if l >= 1:
            mu[3360:3392] = inp["mu_vres"][l - 1]
        pp[l, :, PP_MU:PP_MU + 27] = _fm(mu)
        for jj in range(3):
            pp[l, :, PP_CW + 8 * jj:PP_CW + 8 * jj + 8] = _fm(inp["conv_w"][l][jj])
        pp[l, :, PP_CG:PP_CG + 8] = _fm(inp["conv_norm_g"][l])
        pp[l, :, PP_W0:PP_W0 + 8] = _fm(inp["decay_w0"][l])
        pp[l, :, PP_A0:PP_A0 + 8] = _fm(inp["iclr_a0"][l])
        if l >= 1:
            pp[l, :, PP_V0:PP_V0 + 8] = _fm(inp["vres_v0"][l - 1])
        pp[l, :, PP_KK:PP_KK + 8] = _fm(inp["k_k"][l])
        pp[l, :, PP_KA:PP_KA + 8] = _fm(inp["k_a"][l])
        pp[l, :, PP_RK:PP_RK + 8] = _fm(np.asarray(inp["r_k"][l]).reshape(-1))
        pp[l, :, PP_LG:PP_LG + 8] = _fm(inp["lnx_g"][l])
        pp[l, :, PP_LB:PP_LB + 8] = _fm(inp["lnx_b"][l])
    out["lw"] = lw
    out["pp"] = pp
    out["cst"] = _consts()
    c1_ = out["cst"]
    c2_ = np.zeros((64, 2048), f)
    c2_[:, 0:1024] = np.tile(c1_[0:64, 256:384], (1, 8))
    c2_[:, 1024:1536] = np.tile(c1_[0:64, 512:576], (1, 8))
    c2_[:, 1536:2048] = np.tile(c1_[0:64, 576:640], (1, 8))
    out["cst2"] = c2_
    return out


def run_module(inp, depth, T_PAD, n_valid, n_cores=8):
    x = np.asarray(inp["x"], np.float32)
    B, L, _ = x.shape
    meta = np.asarray(inp["meta_tokens"], np.float32)
    import time as _t
    _t0 = _t.time()
    w = prep_weights(inp, depth)
    _t1 = _t.time()
    nc, mk = build_program(T_PAD, depth)
    print("prep %.1fs build %.1fs ninst %d" % (_t1 - _t0, _t.time() - _t1, mk.ninst), flush=True)
    in_maps = []
    for c in range(n_cores):
        b = c % B
        h0 = np.zeros((D, T_PAD), np.float32)
        h0[:, :NMETA] = meta.T
        h0[:, NMETA:NMETA + L] = x[b].T
        m = {k: v for k, v in w.items() if k in mk.dram_names}
        m["h0"] = h0
        in_maps.append(m)
    res = run_bass_kernel_spmd(nc, in_maps, core_ids=list(range(n_cores)))
    out = np.stack([np.ascontiguousarray(res.results[b]["hout"][:, NMETA:NMETA + L].T) for b in range(B)], axis=0)
    return out.astype(np.float32)


def kernel(**inputs):
    T_PAD = 4608
    inp = {k: np.asarray(v) for k, v in inputs.items()}
    return run_module(inp, DEPTH, T_PAD, NMETA + SEQ)
```

```python
import numpy as np
from contextlib import ExitStack
import concourse.bass as bass
import concourse.mybir as mybir
from concourse.bass_utils import run_bass_kernel_spmd

F32 = mybir.dt.float32
BF16 = mybir.dt.bfloat16
ALU = mybir.AluOpType
AF = mybir.ActivationFunctionType
AX = mybir.AxisListType

D = 2048
DFF = 5632
NMETA = 16
SEQ = 4096
BATCH = 4
DEPTH = 4
RW = 1024
NJ_IN = 51
TT = 512
TM = 256
NC_F = D // 128
NJ_FF = DFF // 128
NORM_EPS = 1e-6
LNX_EPS = 64e-5
DEC_C = 0.6065306597126334

PP_G1, PP_GM, PP_G2 = 0, 16, 32
PP_P1, PP_PM, PP_P2 = 48, 64, 80
PP_MU = 96
PP_CW = 123
PP_CG = 147
PP_W0, PP_A0, PP_V0, PP_KK, PP_KA, PP_RK, PP_LG, PP_LB = 155, 163, 171, 179, 187, 195, 203, 211
NPP = 219


class _Stop(Exception):
    pass


class Buf:
    __slots__ = ("name", "w", "r", "pend")

    def __init__(self, name):
        self.name = name
        self.w = None
        self.r = {}
        self.pend = None


class Sem:
    __slots__ = ("h", "val")

    def __init__(self, h):
        self.h = h
        self.val = 0


class Eng:
    def __init__(self, raw, sem, name):
        self.raw = raw
        self.sem = sem
        self.name = name
        self.seen = {}
        self.pr = []
        self.pw = []
        self.dsems = []
        self.di = 0


class MK:
    def __init__(self, nc, es):
        self.nc = nc
        self.es = es
        self.nsem = 0
        self.pe = Eng(nc.tensor, self.newsem("pe"), "pe")
        self.act = Eng(nc.scalar, self.newsem("act"), "act")
        self.dve = Eng(nc.vector, self.newsem("dve"), "dve")
        self.pool = Eng(nc.gpsimd, self.newsem("pool"), "pool")
        self.sp = Eng(nc.sync, self.newsem("sp"), "sp")
        self.engs = [self.pe, self.act, self.dve, self.pool, self.sp]
        for q in (self.sp, self.pool):
            q.dsems = [self.newsem("d%s%d" % (q.name, i)) for i in range(12)]
        self.ninst = 0

    def newsem(self, name):
        self.nsem += 1
        return Sem(self.es.enter_context(self.nc.semaphore(name)))

    def _waits(self, eng, need):
        for sem, v in need.items():
            if eng.seen.get(sem, 0) < v:
                eng.raw.wait_ge(sem.h, v)
                eng.seen[sem] = v
                self.ninst += 1

    def _need(self, eng, reads, writes, need):
        for b in reads:
            assert b.pend is None or b.pend is eng, (b.name, eng.name)
            if b.w is not None and need.get(b.w[0], 0) < b.w[1]:
                need[b.w[0]] = b.w[1]
        for b in writes:
            assert b.pend is None or b.pend is eng, (b.name, eng.name)
            if b.w is not None and need.get(b.w[0], 0) < b.w[1]:
                need[b.w[0]] = b.w[1]
            for sem, v in b.r.items():
                if need.get(sem, 0) < v:
                    need[sem] = v

    def op(self, eng, fn, reads=(), writes=(), signal=True):
        need = {}
        self._need(eng, reads, writes, need)
        self._waits(eng, need)
        ins = fn()
        self.ninst += 1
        eng.pr.extend(reads)
        eng.pw.extend(writes)
        for b in reads:
            b.pend = eng
        for b in writes:
            b.pend = eng
        if signal:
            eng.sem.val += 1
            ins.then_inc(eng.sem.h, 1)
            v = eng.sem.val
            for b in eng.pw:
                b.w = (eng.sem, v)
                b.r = {}
                b.pend = None
            for b in eng.pr:
                if not (b.w is not None and b.w[0] is eng.sem and b.w[1] == v):
                    b.r[eng.sem] = v
                b.pend = None
            eng.pr = []
            eng.pw = []
            eng.seen[eng.sem] = max(eng.seen.get(eng.sem, 0), 0)
        return ins

    def dma(self, q, out_ap, in_ap, reads=(), writes=()):
        sem = q.dsems[q.di]
        q.di = (q.di + 1) % len(q.dsems)
        need = {sem: sem.val} if sem.val > 0 else {}
        self._need(q, reads, writes, need)
        self._waits(q, need)
        ins = q.raw.dma_start(out=out_ap, in_=in_ap)
        self.ninst += 1
        sem.val += 16
        ins.then_inc(sem.h, 16)
        for b in writes:
            b.w = (sem, sem.val)
            b.r = {}
        for b in reads:
            b.r[sem] = sem.val

    def barrier(self):
        need = {}
        for e in self.engs:
            assert not e.pr and not e.pw, e.name
            if e.sem.val:
                need[e.sem] = e.sem.val
            for s in e.dsems:
                if s.val:
                    need[s] = s.val
        for e in self.engs:
            self._waits(e, dict(need))

    def finish(self):
        need = {}
        for e in self.engs:
            if e.sem.val:
                need[e.sem] = e.sem.val
            for s in e.dsems:
                if s.val:
                    need[s] = s.val
        self._waits(self.sp, need)


def build_program(T_PAD, depth):
    assert T_PAD % TM == 0
    nc = bass.Bass("TRN2", target_bir_lowering=False)
    dr = {}

    def dram(name, shape, dt, kind):
        t = nc.dram_tensor(name, list(shape), dt, kind=kind).ap()
        dr[name] = t
        return t

    h0 = dram("h0", [D, T_PAD], F32, "ExternalInput")
    hout = dram("hout", [D, T_PAD], F32, "ExternalOutput")
    hres = dram("hres", [D, T_PAD], F32, "Internal")
    vfd = dram("vfd", [RW, T_PAD], F32, "Internal")
    import os as _os
    phases = _os.environ.get("MK_PHASES", "f1,mix,f2").split(",")
    wgu1 = wdn1 = wgu2 = wdn2 = None
    if "f1" in phases:
        wgu1 = dram("wgu1", [depth, NJ_FF, 128, NC_F * 256], F32, "ExternalInput")
        wdn1 = dram("wdn1", [depth, NC_F, 128, NJ_FF * 128], F32, "ExternalInput")
    if "f2" in phases:
        wgu2 = dram("wgu2", [depth, NJ_FF, 128, NC_F * 256], F32, "ExternalInput")
        wdn2 = dram("wdn2", [depth, NC_F, 128, NJ_FF * 128], F32, "ExternalInput")
    win = dram("win", [depth, NJ_IN, 128, NC_F * 128], F32, "ExternalInput")
    wout = dram("wout", [depth, NC_F, 128, NC_F * 128], F32, "ExternalInput")
    lwd = dram("lw", [depth, 3, 128, RW], F32, "ExternalInput")
    ppd = dram("pp", [depth, 128, NPP], F32, "ExternalInput")
    cst = dram("cst", [128, 128 + 128 + 256 + 64 + 64], F32, "ExternalInput")
    cst2 = dram("cst2", [64, 2048], F32, "ExternalInput")

    es = ExitStack()
    mk = MK(nc, es)
    pe, act, dve, pool, sp = mk.pe, mk.act, mk.dve, mk.pool, mk.sp

    uid = [0]

    def sbt(stack, name, shape, dt):
        uid[0] += 1
        t = stack.enter_context(nc.sbuf_tensor("s%d_%s" % (uid[0], name), list(shape), dt))
        return t

    psum = es.enter_context(nc.psum_tensor("psum", [128, 8 * 512], F32))
    pbank = [Buf("bank%d" % i) for i in range(8)]
    bank_rr = [0]

    live = [False] * 8

    def bank(n=1):
        for k in range(7):
            b = (bank_rr[0] + k) % 7
            if n == 2 and (b % 2 == 1 or b + 1 > 5):
                continue
            if all(not live[b + i] for i in range(n)):
                for i in range(n):
                    live[b + i] = True
                bank_rr[0] = (b + n) % 7
                return b, pbank[b:b + n]
        raise AssertionError("out of PSUM banks")

    def rel(b, n=1):
        for i in range(n):
            assert live[b + i]
            live[b + i] = False

    def pap(b, n=1):
        return psum[:, b * 512:(b + n) * 512]

    cst_f = sbt(es, "cst_f", [128, 640], F32)
    idb128 = sbt(es, "idb128", [128, 128], BF16)
    cst_fb = Buf("cst_f")
    ones_bf = sbt(es, "ones_bf", [128, 128], BF16)
    blk_bf = sbt(es, "blk_bf", [128, 128], BF16)
    idb = sbt(es, "idb", [64, 64], BF16)
    cbf_b = Buf("cbf")
    pp = sbt(es, "pp", [128, NPP], F32)
    pp_b = Buf("pp")
    lwt = sbt(es, "lwt", [128, 3, RW], BF16)
    lw_b = Buf("lw")
    G = {}

    def alloc_stream(stack, width):
        G["hT"] = sbt(stack, "hT", [128, NC_F, width], F32)
        G["hT_b"] = [Buf("hT%d" % c) for c in range(NC_F)]
        G["yT"] = sbt(stack, "yT", [128, NC_F, width], F32)
        G["yT_b"] = [Buf("yT%d" % c) for c in range(NC_F)]
        G["xT"] = sbt(stack, "xT", [128, NC_F, width], BF16)
        G["xT_b"] = [Buf("xT%d" % c) for c in range(NC_F)]
        return G["hT"], G["hT_b"], G["yT"], G["yT_b"], G["xT"], G["xT_b"]
    sqt = [sbt(es, "sq%d" % i, [128, TT], BF16) for i in range(2)]
    sq_b = [Buf("sq%d" % i) for i in range(2)]
    rbc = sbt(es, "rbc", [128, TT], F32)
    rbc_b = Buf("rbc")
    rbt = sbt(es, "rbt", [128, TT], F32)
    rbt_b = Buf("rbt")
    Hf = sbt(es, "Hf", [128, 8, 64], F32)
    Hf_b = [Buf("Hf%d" % i) for i in range(8)]
    Hb = sbt(es, "Hb", [128, 8, 64], BF16)
    Hb_b = [Buf("Hb%d" % i) for i in range(8)]
    carry = sbt(es, "carry", [128, 27], F32)
    carry_b = [Buf("carry%d" % i) for i in range(27)]
    ccar = sbt(es, "ccar", [128, 8, 2], F32)
    ccar_b = [Buf("ccar%d" % i) for i in range(8)]

    mk.dma(sp, cst_f[:], cst[:, :], writes=[cst_fb])
    cst2_f = sbt(es, "cst2_f", [64, 2048], F32)
    mk.dma(sp, cst2_f[:], cst2[:, :], writes=[cst_fb])
    mk.op(dve, lambda: nc.vector.tensor_copy(out=ones_bf[:], in_=cst_f[:, 0:128]), [cst_fb], [cbf_b])
    mk.op(dve, lambda: nc.vector.tensor_copy(out=blk_bf[:], in_=cst_f[:, 128:256]), [cst_fb], [cbf_b])
    mk.op(dve, lambda: nc.vector.tensor_copy(out=idb[:], in_=cst_f[0:64, 576:640]), [cst_fb], [cbf_b])
    mk.op(dve, lambda: nc.vector.tensor_copy(out=idb128[:], in_=cst_f[:, 384:512]), [cst_fb], [cbf_b])
    maskNA = cst_f[0:64, 256:384]
    maskNT = cst_f[0:64, 512:576]
    identf = cst_f[0:64, 576:640]

    WB = {}
    WBb = {}

    def conv_w(name, src, l):
        if src is None:
            return
        key = name
        if key not in WB:
            WB[key] = nc.dram_tensor(name + "_bf", list(src.shape), BF16, kind="Internal").ap()
            WBb[key] = {}
        n1 = src.shape[1]
        for j in range(n1):
            b_ = Buf("%s_%d_%d" % (name, l, j))
            WBb[key][(l, j)] = b_
            mk.dma(pool, WB[key][l, j], src[l, j], writes=[b_])

    def wsrc(name, l, j):
        return WB[name][l, j], WBb[name][(l, j)]

    def rms_stats(src_tile, src_bufs, nt, eps_scale=1.0 / D):
        b, bb = bank()
        for c in range(NC_F):
            s = c % 2
            mk.op(act, lambda c=c, s=s: nc.scalar.activation(out=sqt[s][:, :nt], in_=src_tile[:, c, :nt], func=AF.Square),
                  [src_bufs[c]], [sq_b[s]])
            mk.op(pe, lambda c=c, s=s: nc.tensor.matmul(pap(b)[:, :nt], ones_bf[:], sqt[s][:, :nt], start=(c == 0), stop=(c == NC_F - 1)),
                  [cbf_b, sq_b[s]], bb, signal=True)
        mk.op(act, lambda: nc.scalar.activation(out=rbt[:, :nt], in_=pap(b)[:, :nt], func=AF.Sqrt, bias=NORM_EPS, scale=eps_scale),
              bb, [rbt_b])
        rel(b)
        mk.op(dve, lambda: nc.vector.reciprocal(out=rbc[:, :nt], in_=rbt[:, :nt]), [rbt_b], [rbc_b])

    def load_h(src, t0, nt):
        hT, hT_b = G["hT"], G["hT_b"]
        v = src.rearrange("(c p) t -> p c t", p=128)
        for half in range(2):
            cs = slice(half * 8, half * 8 + 8)
            mk.dma(sp, hT[:, cs, :nt], v[:, cs, t0:t0 + nt], writes=hT_b[half * 8:half * 8 + 8])

    def store_h(dst, t0, nt):
        hT, hT_b = G["hT"], G["hT_b"]
        v = dst.rearrange("(c p) t -> p c t", p=128)
        for half in range(2):
            cs = slice(half * 8, half * 8 + 8)
            mk.dma(sp, v[:, cs, t0:t0 + nt], hT[:, cs, :nt], reads=hT_b[half * 8:half * 8 + 8])

    def prenorm(gcol, nt):
        hT, hT_b, xT, xT_b = G["hT"], G["hT_b"], G["xT"], G["xT_b"]
        rms_stats(hT, hT_b, nt)
        for c in range(NC_F):
            mk.op(dve, lambda c=c: nc.vector.scalar_tensor_tensor(out=xT[:, c, :nt], in0=hT[:, c, :nt], scalar=pp[:, gcol + c:gcol + c + 1],
                                                                 in1=rbc[:, :nt], op0=ALU.mult, op1=ALU.mult),
                  [hT_b[c], pp_b, rbc_b], [xT_b[c]])

    def postnorm_residual(pcol, nt, half):
        hT, hT_b, yT, yT_b = G["hT"], G["hT_b"], G["yT"], G["yT_b"]
        rms_stats(yT, yT_b, nt)
        for c in range(NC_F):
            mk.op(dve, lambda c=c: nc.vector.scalar_tensor_tensor(out=yT[:, c, :nt], in0=yT[:, c, :nt], scalar=pp[:, pcol + c:pcol + c + 1],
                                                                 in1=rbc[:, :nt], op0=ALU.mult, op1=ALU.mult),
                  [yT_b[c], pp_b, rbc_b], [yT_b[c]])
            mk.op(dve, lambda c=c: nc.vector.scalar_tensor_tensor(out=hT[:, c, :nt], in0=yT[:, c, :nt], scalar=float(half),
                                                                 in1=hT[:, c, :nt], op0=ALU.mult, op1=ALU.add),
                  [yT_b[c], hT_b[c]], [hT_b[c]])

    def load_params(l):
        mk.dma(sp, pp[:], ppd[l], writes=[pp_b])
        mk.dma(pool, lwt[:], lwd[l].rearrange("k p n -> p k n"), writes=[lw_b])
        if l == 0:
            for l2 in range(depth):
                if "f1" in phases:
                    conv_w("wgu1", wgu1, l2)
                    conv_w("wdn1", wdn1, l2)
                if "mix" in phases:
                    conv_w("win", win, l2)
                    conv_w("wout", wout, l2)
                if "f2" in phases:
                    conv_w("wgu2", wgu2, l2)
                    conv_w("wdn2", wdn2, l2)

    def ffn_phase(l, wgu, wdn, gcol, pcol, src, dst):
        with ExitStack() as st:
            hT, hT_b, yT, yT_b, xT, xT_b = alloc_stream(st, TT)
            hid = sbt(st, "hid", [128, NJ_FF, TT], BF16)
            hid_b = [Buf("hid%d" % j) for j in range(NJ_FF)]
            wg = [sbt(st, "wg%d" % i, [128, NC_F, 256], BF16) for i in range(3)]
            wg_b = [Buf("wg%d" % i) for i in range(3)]
            wd = [sbt(st, "wd%d" % i, [128, NJ_FF, 128], BF16) for i in range(2)]
            wd_b = [Buf("wd%d" % i) for i in range(2)]
            sg = [sbt(st, "sg%d" % i, [128, TT], F32) for i in range(2)]
            sg_b = [Buf("sg%d" % i) for i in range(2)]
            for ti in range((T_PAD + TT - 1) // TT):
                t0 = ti * TT
                nt = min(TT, T_PAD - t0)
                load_h(src, t0, nt)
                prenorm(gcol, nt)
                for j in range(NJ_FF):
                    s = j % 2
                    ws = j % 3
                    wsrc_ap, wsrc_b = wsrc(wgu, l, j)
                    mk.dma(sp, wg[ws][:], wsrc_ap.rearrange("p (c n) -> p c n", n=256), reads=[wsrc_b], writes=[wg_b[ws]])
                    ba, bba = bank()
                    bu, bbu = bank()
                    for c in range(NC_F):
                        mk.op(pe, lambda c=c, ws=ws: nc.tensor.matmul(pap(ba)[:, :nt], wg[ws][:, c, 0:128], xT[:, c, :nt], start=(c == 0), stop=(c == NC_F - 1)),
                              [wg_b[ws], xT_b[c]], bba, signal=(c == NC_F - 1))
                    for c in range(NC_F):
                        mk.op(pe, lambda c=c, ws=ws: nc.tensor.matmul(pap(bu)[:, :nt], wg[ws][:, c, 128:256], xT[:, c, :nt], start=(c == 0), stop=(c == NC_F - 1)),
                              [wg_b[ws], xT_b[c]], bbu, signal=(c == NC_F - 1))
                    mk.op(act, lambda s=s: nc.scalar.activation(out=sg[s][:, :nt], in_=pap(ba)[:, :nt], func=AF.Silu), bba, [sg_b[s]])
                    mk.op(dve, lambda s=s, j=j: nc.vector.tensor_tensor(out=hid[:, j, :nt], in0=sg[s][:, :nt], in1=pap(bu)[:, :nt], op=ALU.mult),
                          [sg_b[s]] + bbu, [hid_b[j]])
                    rel(ba)
                    rel(bu)
                for fc in range(NC_F):
                    s = fc % 2
                    wsrc_ap, wsrc_b = wsrc(wdn, l, fc)
                    mk.dma(sp, wd[s][:], wsrc_ap.rearrange("p (k n) -> p k n", n=128), reads=[wsrc_b], writes=[wd_b[s]])
                    b, bb = bank()
                    for kc in range(NJ_FF):
                        mk.op(pe, lambda kc=kc, s=s: nc.tensor.matmul(pap(b)[:, :nt], wd[s][:, kc, :], hid[:, kc, :nt], start=(kc == 0), stop=(kc == NJ_FF - 1)),
                              [wd_b[s], hid_b[kc]], bb, signal=(kc == NJ_FF - 1))
                    mk.op(act, lambda fc=fc: nc.scalar.copy(out=yT[:, fc, :nt], in_=pap(b)[:, :nt]), bb, [yT_b[fc]])
                    rel(b)
                postnorm_residual(pcol, nt, 0.5)
                store_h(dst, t0, nt)
        mk.barrier()

    import os as _os2
    MIXSTOP = int(_os2.environ.get("MK_MIXSTOP", "0"))

    DBG = int(_os2.environ.get("MK_DBG", "0"))

    def mm64(pt, col, L, a, hh, c, rhs, stop=True):
        if hh == 0:
            return nc.tensor.matmul(pt[0:64, col:col + 64], L[0:64, a, c * 64:c * 64 + 64], rhs, start=True, stop=stop)
        return nc.tensor.matmul(pt[:, col:col + 64], L[64:128, a, c * 64:c * 64 + 128], rhs, start=True, stop=stop)

    def chk(n):
        if n == MIXSTOP:
            raise _Stop()

    def mixer_phase(l, src, dst):
        with ExitStack() as st:
            hT, hT_b, yT, yT_b, xT, xT_b = alloc_stream(st, TM)
            nt = TM
            NCH = TM // 64
            wi = [sbt(st, "wi%d" % i, [128, NC_F, 128], BF16) for i in range(3)]
            wi_b = [Buf("wi%d" % i) for i in range(3)]
            wrr = [0]

            def wslot():
                s = wrr[0]
                wrr[0] = (s + 1) % 3
                return s

            def f32t(name, w=TM):
                return sbt(st, name, [128, w], F32), Buf(name)

            def bf16t(name, w=TM):
                return sbt(st, name, [128, w], BF16), Buf(name)

            praw, praw_b = f32t("praw", TM + 1)
            dtmp, dtmp_b = f32t("dtmp")
            L48, L48_b = f32t("L48")
            L49, L49_b = f32t("L49")
            L50, L50_b = f32t("L50")
            LB48, LB48_b = bf16t("LB48")
            LB49, LB49_b = bf16t("LB49")
            LB50, LB50_b = bf16t("LB50")
            gcs, gcs_b = f32t("gcs")
            uc, uc_b = f32t("uc", TM + 2)
            cv, cv_b = f32t("cv")
            zg = sbt(st, "zg", [128, 8, TM], F32)
            zg_b = [Buf("zg%d" % q) for q in range(8)]
            zsq, zsq_b = bf16t("zsq")
            opT = sbt(st, "opT", [128, NC_F, TM], BF16)
            opT_b = [Buf("opT%d" % q) for q in range(NC_F)]
            rT, rT_b = f32t("rT")
            kT, kT_b = f32t("kT")
            vT, vT_b = f32t("vT")
            names = ["sw", "aa", "gg", "sv", "vf", "kk", "rt", "rn", "kkn", "t1", "kmod", "bvec", "csa", "csb",
                     "wincl", "exc", "wexc", "winv", "c1", "c2", "bs"]
            F0 = {}
            FB0 = {}
            for n_ in names:
                F0[n_], FB0[n_] = f32t(n_)
            ytmp, ytmp_b = f32t("ytmp")
            csz = [sbt(st, "csz%d" % i, [128, NCH, 96], F32) for i in range(2)]
            csz_b = [Buf("csz%d" % i) for i in range(2)]
            kksq, kksq_b = bf16t("kksq")
            rkb, rkb_b = bf16t("rkb")
            vbf, vbf_b = bf16t("vbf")
            NG = NCH * 2
            XS = []
            for xi in range(2):
                X = {}
                X["AR"] = sbt(st, "AR", [128, 2, TM + 64], BF16)
                X["BK"] = sbt(st, "BK", [128, 2, TM + 64], BF16)
                X["AR_b"], X["BK_b"] = Buf("AR"), Buf("BK")
                for nm_ in ("Vtok", "Btok", "Ktok"):
                    X[nm_] = sbt(st, nm_, [64, NCH, 128], BF16)
                    X[nm_ + "_b"] = Buf(nm_)
                for nm_ in ("NA", "KA"):
                    X[nm_] = sbt(st, nm_, [64, NG, 128], BF16)
                    X[nm_ + "_b"] = Buf(nm_)
                X["Qs"] = [sbt(st, "Q%d" % i, [64, NG, 64], BF16) for i in range(2)]
                X["Qts"] = [sbt(st, "Qt%d" % i, [64, NG, 64], BF16) for i in range(2)]
                X["Ps"] = [sbt(st, "P%d" % i, [64, NG, 64], BF16) for i in range(2)]
                X["Q_b"] = [Buf("Q%d" % i) for i in range(2)]
                X["Qt_b"] = [Buf("Qt%d" % i) for i in range(2)]
                X["P_b"] = [Buf("P%d" % i) for i in range(2)]
                X["F"] = {}
                X["FB"] = {}
                for nm_ in ("wincl", "c1", "c2"):
                    X["F"][nm_], X["FB"][nm_] = f32t(nm_ + "x")
                XS.append(X)
            ru = sbt(st, "ru", [64, 2, 64], BF16)
            ru_b = Buf("ru")
            Us = sbt(st, "Us", [64, 2, 64], BF16)
            Us_b = Buf("Us")
            ysb = sbt(st, "ysb", [64, NCH, 128], F32)
            ysb_b = Buf("ysb")
            ysq = sbt(st, "ysq", [64, NCH, 128], F32)
            ysq_b = Buf("ysq")
            yn = sbt(st, "yn", [64, NCH, 128], F32)
            yn_b = Buf("yn")
            st1 = sbt(st, "st1", [64, 6, NG], F32)
            st_b = Buf("st1")

            if l == 0:
                pass
            for X in XS:
                mk.op(dve, lambda X=X: nc.vector.memset(X["AR"][:], 0.0), [], [X["AR_b"]])
                mk.op(dve, lambda X=X: nc.vector.memset(X["BK"][:], 0.0), [], [X["BK_b"]])
            for i_ in range(2):
                mk.op(dve, lambda i_=i_: nc.vector.memset(csz[i_][:], 0.0), [], [csz_b[i_]])
            mk.op(dve, lambda: nc.vector.memset(carry[:], 0.0), [], carry_b)
            mk.op(dve, lambda: nc.vector.memset(ccar[:], 0.0), [], ccar_b)
            mk.op(dve, lambda: nc.vector.memset(Hf[:], 0.0), [], Hf_b)
            mk.op(dve, lambda: nc.vector.memset(Hb[:], 0.0), [], Hb_b)

            def proj(j):
                s = wslot()
                wsrc_ap, wsrc_b = wsrc("win", l, j)
                mk.dma(sp, wi[s][:], wsrc_ap.rearrange("p (c n) -> p c n", n=128), reads=[wsrc_b], writes=[wi_b[s]])
                b, bb = bank()
                for c in range(NC_F):
                    mk.op(pe, lambda c=c, s=s: nc.tensor.matmul(pap(b)[:, :nt], wi[s][:, c, :], xT[:, c, :nt], start=(c == 0), stop=(c == NC_F - 1)),
                          [wi_b[s], xT_b[c]], bb, signal=(c == NC_F - 1))
                return b, bb

            def shifted(j, dst_t, dst_b):
                m = j - 24
                b, bb = proj(j)
                mk.op(act, lambda: nc.scalar.copy(out=praw[:, 1:nt + 1], in_=pap(b)[:, :nt]), bb, [praw_b])
                rel(b)
                mk.op(act, lambda: nc.scalar.copy(out=praw[:, 0:1], in_=carry[:, m:m + 1]), [carry_b[m]], [praw_b])
                mk.op(act, lambda: nc.scalar.copy(out=carry[:, m:m + 1], in_=praw[:, nt:nt + 1]), [praw_b], [carry_b[m]])
                mk.op(dve, lambda: nc.vector.tensor_tensor(out=dtmp[:, :nt], in0=praw[:, 0:nt], in1=praw[:, 1:nt + 1], op=ALU.subtract),
                      [praw_b], [dtmp_b])
                mk.op(dve, lambda: nc.vector.scalar_tensor_tensor(out=dst_t[:, :nt], in0=dtmp[:, :nt], scalar=pp[:, PP_MU + m:PP_MU + m + 1],
                                                                 in1=praw[:, 1:nt + 1], op0=ALU.mult, op1=ALU.add),
                      [dtmp_b, praw_b, pp_b], [dst_b])

            for ti in range(T_PAD // TM):
                t0 = ti * TM
                load_h(src, t0, nt)
                prenorm(PP_GM, nt)
                try:
                    shifted(48, L48, L48_b)
                    shifted(49, L49, L49_b)
                    shifted(50, L50, L50_b)
                    mk.op(act, lambda: nc.scalar.activation(out=LB48[0:64, :nt], in_=L48[0:64, :nt], func=AF.Tanh), [L48_b], [LB48_b])
                    mk.op(act, lambda: nc.scalar.copy(out=LB48[64:128, :nt], in_=L48[64:128, :nt]), [L48_b], [LB48_b])
                    mk.op(act, lambda: nc.scalar.activation(out=LB49[:, :nt], in_=L49[:, :nt], func=AF.Sigmoid), [L49_b], [LB49_b])
                    mk.op(act, lambda: nc.scalar.activation(out=LB50[0:32, :nt], in_=L50[0:32, :nt], func=AF.Sigmoid), [L50_b], [LB50_b])
                    mk.op(act, lambda: nc.scalar.copy(out=LB50[32:64, :nt], in_=L50[32:64, :nt]), [L50_b], [LB50_b])
                    chk(1)
                    def gen_conv():
                        bz, bbz = 7, pbank[7:8]
                        for q in range(8):
                            bgb, bbgb = proj(q)
                            bgc, bbgc = proj(8 + q)
                            bu_, bbu_ = proj(16 + q)
                            mk.op(act, lambda: nc.scalar.copy(out=gcs[:, :nt], in_=pap(bgc)[:, :nt]), bbgc, [gcs_b])
                            rel(bgc)
                            mk.op(act, lambda q=q: nc.scalar.copy(out=uc[:, 0:2], in_=ccar[:, q, :]), [ccar_b[q]], [uc_b])
                            mk.op(dve, lambda: nc.vector.tensor_tensor(out=uc[:, 2:nt + 2], in0=gcs[:, :nt], in1=pap(bu_)[:, :nt], op=ALU.mult),
                                  [gcs_b] + bbu_, [uc_b])
                            rel(bu_)
                            mk.op(act, lambda q=q: nc.scalar.copy(out=ccar[:, q, :], in_=uc[:, nt:nt + 2]), [uc_b], [ccar_b[q]])
                            cw = PP_CW + q
                            mk.op(dve, lambda cw=cw: nc.vector.tensor_scalar(out=cv[:, :nt], in0=uc[:, 2:nt + 2], scalar1=pp[:, cw + 16:cw + 17], scalar2=None, op0=ALU.mult),
                                  [uc_b, pp_b], [cv_b])
                            mk.op(dve, lambda cw=cw: nc.vector.scalar_tensor_tensor(out=cv[:, :nt], in0=uc[:, 1:nt + 1], scalar=pp[:, cw + 8:cw + 9], in1=cv[:, :nt],
                                                                                   op0=ALU.mult, op1=ALU.add), [uc_b, pp_b, cv_b], [cv_b])
                            mk.op(dve, lambda cw=cw: nc.vector.scalar_tensor_tensor(out=cv[:, :nt], in0=uc[:, 0:nt], scalar=pp[:, cw:cw + 1], in1=cv[:, :nt],
                                                                                   op0=ALU.mult, op1=ALU.add), [uc_b, pp_b, cv_b], [cv_b])
                            mk.op(dve, lambda: nc.vector.tensor_tensor(out=cv[:, :nt], in0=cv[:, :nt], in1=pap(bgb)[:, :nt], op=ALU.mult),
                                  [cv_b] + bbgb, [cv_b])
                            rel(bgb)
                            mk.op(act, lambda: nc.scalar.activation(out=zsq[:, :nt], in_=cv[:, :nt], func=AF.Square), [cv_b], [zsq_b])
                            mk.op(pe, lambda q=q: nc.tensor.matmul(pap(bz)[:, :nt], ones_bf[:], zsq[:, :nt], start=(q == 0), stop=(q == 7)),
                                  [cbf_b, zsq_b], bbz)
                            mk.op(dve, lambda q=q: nc.vector.tensor_scalar(out=zg[:, q, :nt], in0=cv[:, :nt], scalar1=pp[:, PP_CG + q:PP_CG + q + 1], scalar2=None, op0=ALU.mult),
                                  [cv_b, pp_b], [zg_b[q]])
                            yield
                        mk.op(act, lambda: nc.scalar.activation(out=rbt[:, :nt], in_=pap(bz)[:, :nt], func=AF.Sqrt, bias=NORM_EPS, scale=1.0 / 1024), bbz, [rbt_b])
                        mk.op(dve, lambda: nc.vector.reciprocal(out=rbc[:, :nt], in_=rbt[:, :nt]), [rbt_b], [rbc_b])
                        yield
                        for q in range(8):
                            mk.op(dve, lambda q=q: nc.vector.tensor_tensor(out=opT[:, q, :nt], in0=zg[:, q, :nt], in1=rbc[:, :nt], op=ALU.mult),
                                  [zg_b[q], rbc_b], [opT_b[q]])

                    chk(2)
                    def gen_fc(fc, X):
                        AR, AR_b, BK, BK_b = X["AR"], X["AR_b"], X["BK"], X["BK_b"]
                        Vtok, Btok, Ktok, Vtok_b, Btok_b, Ktok_b = X["Vtok"], X["Btok"], X["Ktok"], X["Vtok_b"], X["Btok_b"], X["Ktok_b"]
                        NA, KA, NA_b, KA_b = X["NA"], X["KA"], X["NA_b"], X["KA_b"]
                        Qs, Qts, Ps, Q_b, Qt_b, P_b = X["Qs"], X["Qts"], X["Ps"], X["Q_b"], X["Qt_b"], X["P_b"]
                        F = dict(F0)
                        FB = dict(FB0)
                        F.update(X["F"])
                        FB.update(X["FB"])
                        shifted(24 + fc, rT, rT_b)
                        yield
                        shifted(32 + fc, kT, kT_b)
                        yield
                        shifted(40 + fc, vT, vT_b)
                        yield
                        fcs = slice(fc * 128, fc * 128 + 128)
                        bd, bbd = bank()
                        mk.op(pe, lambda: nc.tensor.matmul(pap(bd)[:, :nt], lwt[0:64, 0, fcs], LB48[0:64, :nt], start=True, stop=True), [lw_b, LB48_b], bbd)
                        mk.op(act, lambda: nc.scalar.activation(out=F["sw"][:, :nt], in_=pap(bd)[:, :nt], func=AF.Sigmoid, bias=pp[:, PP_W0 + fc:PP_W0 + fc + 1]),
                              bbd + [pp_b], [FB["sw"]])
                        rel(bd)
                        ba_, bba_ = bank()
                        mk.op(pe, lambda: nc.tensor.matmul(pap(ba_)[:, :nt], lwt[64:128, 0, fcs], LB48[64:128, :nt], start=True, stop=True), [lw_b, LB48_b], bba_)
                        mk.op(act, lambda: nc.scalar.activation(out=F["aa"][:, :nt], in_=pap(ba_)[:, :nt], func=AF.Sigmoid, bias=pp[:, PP_A0 + fc:PP_A0 + fc + 1]),
                              bba_ + [pp_b], [FB["aa"]])
                        rel(ba_)
                        bg_, bbg_ = bank()
                        mk.op(pe, lambda: nc.tensor.matmul(pap(bg_)[:, :nt], lwt[:, 1, fcs], LB49[:, :nt], start=True, stop=False), [lw_b, LB49_b], bbg_, signal=False)
                        mk.op(pe, lambda: nc.tensor.matmul(pap(bg_)[:, :nt], lwt[0:32, 2, fcs], LB50[0:32, :nt], start=False, stop=True), [lw_b, LB50_b], bbg_)
                        mk.op(act, lambda: nc.scalar.copy(out=F["gg"][:, :nt], in_=pap(bg_)[:, :nt]), bbg_, [FB["gg"]])
                        rel(bg_)
                        yield
                        vfv = vfd[fc * 128:(fc + 1) * 128, t0:t0 + nt]
                        if l == 0:
                            mk.dma(sp, vfv, vT[:, :nt], reads=[vT_b])
                        else:
                            bv_, bbv_ = bank()
                            mk.op(pe, lambda: nc.tensor.matmul(pap(bv_)[:, :nt], lwt[32:64, 2, fcs], LB50[32:64, :nt], start=True, stop=True), [lw_b, LB50_b], bbv_)
                            mk.op(act, lambda: nc.scalar.activation(out=F["sv"][:, :nt], in_=pap(bv_)[:, :nt], func=AF.Sigmoid, bias=pp[:, PP_V0 + fc:PP_V0 + fc + 1]),
                                  bbv_ + [pp_b], [FB["sv"]])
                            rel(bv_)
                            mk.dma(sp, F["vf"][:, :nt], vfv, writes=[FB["vf"]])
                            mk.op(dve, lambda: nc.vector.tensor_tensor(out=F["vf"][:, :nt], in0=F["vf"][:, :nt], in1=vT[:, :nt], op=ALU.subtract), [FB["vf"], vT_b], [FB["vf"]])
                            mk.op(dve, lambda: nc.vector.tensor_tensor(out=F["vf"][:, :nt], in0=F["vf"][:, :nt], in1=F["sv"][:, :nt], op=ALU.mult), [FB["vf"], FB["sv"]], [FB["vf"]])
                            mk.op(dve, lambda: nc.vector.tensor_tensor(out=vT[:, :nt], in0=vT[:, :nt], in1=F["vf"][:, :nt], op=ALU.add), [FB["vf"], vT_b], [vT_b])
                        mk.op(dve, lambda: nc.vector.tensor_scalar(out=F["kk"][:, :nt], in0=kT[:, :nt], scalar1=pp[:, PP_KK + fc:PP_KK + fc + 1], scalar2=None, op0=ALU.mult),
                              [kT_b, pp_b], [FB["kk"]])
                        mk.op(act, lambda: nc.scalar.activation(out=kksq[:, :nt], in_=F["kk"][:, :nt], func=AF.Square), [FB["kk"]], [kksq_b])
                        bn_, bbn_ = bank()
                        mk.op(pe, lambda: nc.tensor.matmul(pap(bn_)[:, :nt], blk_bf[:], kksq[:, :nt], start=True, stop=True), [cbf_b, kksq_b], bbn_)
                        mk.op(act, lambda: nc.scalar.activation(out=F["rt"][:, :nt], in_=pap(bn_)[:, :nt], func=AF.Sqrt, bias=1e-24, scale=1.0), bbn_, [FB["rt"]])
                        rel(bn_)
                        yield
                        mk.op(dve, lambda: nc.vector.reciprocal(out=F["rn"][:, :nt], in_=F["rt"][:, :nt]), [FB["rt"]], [FB["rn"]])
                        mk.op(dve, lambda: nc.vector.tensor_tensor(out=F["kkn"][:, :nt], in0=F["kk"][:, :nt], in1=F["rn"][:, :nt], op=ALU.mult), [FB["kk"], FB["rn"]], [FB["kkn"]])
                        mk.op(dve, lambda: nc.vector.tensor_scalar(out=F["t1"][:, :nt], in0=F["aa"][:, :nt], scalar1=-1.0, scalar2=pp[:, PP_KA + fc:PP_KA + fc + 1],
                                                                   op0=ALU.add, op1=ALU.mult), [FB["aa"], pp_b], [FB["t1"]])
                        mk.op(dve, lambda: nc.vector.scalar_tensor_tensor(out=F["kmod"][:, :nt], in0=F["t1"][:, :nt], scalar=1.0, in1=kT[:, :nt], op0=ALU.add, op1=ALU.mult),
                              [FB["t1"], kT_b], [FB["kmod"]])
                        mk.op(dve, lambda: nc.vector.tensor_tensor(out=F["bvec"][:, :nt], in0=F["kkn"][:, :nt], in1=F["aa"][:, :nt], op=ALU.mult), [FB["kkn"], FB["aa"]], [FB["bvec"]])
                        mk.op(dve, lambda: nc.vector.scalar_tensor_tensor(out=rkb[:, :nt], in0=rT[:, :nt], scalar=pp[:, PP_RK + fc:PP_RK + fc + 1], in1=F["kmod"][:, :nt],
                                                                         op0=ALU.mult, op1=ALU.mult), [rT_b, pp_b, FB["kmod"]], [rkb_b])
                        bb_, bbb_ = bank()
                        mk.op(pe, lambda: nc.tensor.matmul(pap(bb_)[:, :nt], blk_bf[:], rkb[:, :nt], start=True, stop=True), [cbf_b, rkb_b], bbb_)
                        mk.op(dve, lambda: nc.vector.tensor_tensor(out=F["bs"][:, :nt], in0=vT[:, :nt], in1=pap(bb_)[:, :nt], op=ALU.mult), [vT_b] + bbb_, [FB["bs"]])
                        rel(bb_)
                        yield
                        mk.op(dve, lambda: nc.vector.scalar_tensor_tensor(out=F["c1"][:, :nt], in0=F["bs"][:, :nt], scalar=pp[:, PP_LB + fc:PP_LB + fc + 1], in1=F["gg"][:, :nt],
                                                                         op0=ALU.add, op1=ALU.mult), [FB["bs"], pp_b, FB["gg"]], [FB["c1"]])
                        mk.op(dve, lambda: nc.vector.tensor_scalar(out=F["c2"][:, :nt], in0=F["gg"][:, :nt], scalar1=pp[:, PP_LG + fc:PP_LG + fc + 1], scalar2=None, op0=ALU.mult),
                              [FB["gg"], pp_b], [FB["c2"]])
                        yield
                        mk.op(act, lambda: nc.scalar.copy(out=csz[0][:, :, 32:96], in_=F["sw"][:, :nt].rearrange("p (c t) -> p c t", t=64)), [FB["sw"]], [csz_b[0]])
                        ci = 0
                        for si, sh in enumerate([1, 2, 4, 8, 16, 32]):
                            mk.op(dve, lambda ci=ci, sh=sh: nc.vector.tensor_tensor(out=csz[1 - ci][:, :, 32:96], in0=csz[ci][:, :, 32:96], in1=csz[ci][:, :, 32 - sh:96 - sh], op=ALU.add),
                                  [csz_b[ci]], [csz_b[1 - ci]])
                            ci = 1 - ci
                            if si % 2 == 1:
                                yield
                        mk.op(act, lambda ci=ci: nc.scalar.copy(out=F["csb"][:, :nt].rearrange("p (c t) -> p c t", t=64), in_=csz[ci][:, :, 32:96]), [csz_b[ci]], [FB["csb"]])
                        cur, curb = F["csb"], FB["csb"]
                        cum, cum_b = cur, curb
                        mk.op(act, lambda: nc.scalar.activation(out=F["wincl"][:, :nt], in_=cum[:, :nt], func=AF.Exp, scale=-DEC_C), [cum_b], [FB["wincl"]])
                        mk.op(dve, lambda: nc.vector.tensor_tensor(out=F["exc"][:, :nt], in0=cum[:, :nt], in1=F["sw"][:, :nt], op=ALU.subtract), [cum_b, FB["sw"]], [FB["exc"]])
                        mk.op(act, lambda: nc.scalar.activation(out=F["wexc"][:, :nt], in_=F["exc"][:, :nt], func=AF.Exp, scale=-DEC_C), [FB["exc"]], [FB["wexc"]])
                        mk.op(act, lambda: nc.scalar.activation(out=F["winv"][:, :nt], in_=cum[:, :nt], func=AF.Exp, scale=DEC_C), [cum_b], [FB["winv"]])
                        yield
                        mk.op(dve, lambda: nc.vector.scalar_tensor_tensor(out=AR[:, 0, :nt], in0=F["kkn"][:, :nt], scalar=-1.0, in1=F["wexc"][:, :nt], op0=ALU.mult, op1=ALU.mult),
                              [FB["kkn"], FB["wexc"]], [AR_b])
                        mk.op(dve, lambda: nc.vector.tensor_tensor(out=AR[:, 1, :nt], in0=rT[:, :nt], in1=F["wincl"][:, :nt], op=ALU.mult), [rT_b, FB["wincl"]], [AR_b])
                        mk.op(dve, lambda: nc.vector.tensor_tensor(out=BK[:, 0, :nt], in0=F["bvec"][:, :nt], in1=F["winv"][:, :nt], op=ALU.mult), [FB["bvec"], FB["winv"]], [BK_b])
                        mk.op(dve, lambda: nc.vector.tensor_tensor(out=BK[:, 1, :nt], in0=F["kmod"][:, :nt], in1=F["winv"][:, :nt], op=ALU.mult), [FB["kmod"], FB["winv"]], [BK_b])
                        mk.op(act, lambda: nc.scalar.copy(out=vbf[:, :nt], in_=vT[:, :nt]), [vT_b], [vbf_b])
                        yield
                        for (srcap, srcb, dstt, dstb) in ((vbf, vbf_b, Vtok, Vtok_b), (BK[:, 0, :], BK_b, Btok, Btok_b), (BK[:, 1, :], BK_b, Ktok, Ktok_b)):
                            bt, bbt = bank()
                            pv = pap(bt).bitcast(BF16)
                            for c in range(NCH):
                                mk.op(pe, lambda c=c, srcap=srcap, pv=pv: nc.tensor.transpose(pv[0:64, c * 128:(c + 1) * 128], srcap[:, c * 64:(c + 1) * 64], idb128[:]),
                                      [srcb, cbf_b], bbt, signal=(c == NCH - 1))
                            mk.op(act, lambda dstt=dstt, pv=pv: nc.scalar.copy(out=dstt[:].rearrange("p c f -> p (c f)"), in_=pv[0:64, 0:NCH * 128]), bbt, [dstb])
                            rel(bt)
                            yield
                        b1, bb1 = bank(2)
                        p1 = pap(b1, 2)
                        for c in range(NCH):
                            for hh in range(1 if DBG == 1 else 2):
                                g = c * 2 + hh
                                hp = slice(hh * 64, hh * 64 + 64)
                                cc = slice(c * 64, c * 64 + 64)
                                last = (g == (NG - 2 if DBG == 1 else NG - 1))
                                mk.op(pe, lambda g=g, hp=hp, cc=cc, hh=hh, c=c: (mm64(p1, g * 128, BK, 0, hh, c, AR[hp, 0, cc]), mm64(p1, g * 128 + 64, BK, 0, hh, c, AR[hp, 1, cc]))[1],
                                      [BK_b, AR_b], bb1, signal=last)
                        for hb_ in range(2):
                            gs = slice(hb_ * 4, hb_ * 4 + 4)
                            mk.op(dve, lambda hb_=hb_, gs=gs: nc.vector.tensor_tensor(out=NA[:, gs, :], in0=p1[0:64, hb_ * 512:(hb_ + 1) * 512].rearrange("p (g n) -> p g n", n=128),
                                                                                     in1=mask3(None, 0)[:, 0:4, :], op=ALU.mult), bb1 + [cst_fb], [NA_b])
                        rel(b1, 2)
                        yield
                        b2, bb2 = bank(2)
                        p2 = pap(b2, 2)
                        for c in range(NCH):
                            for hh in range(1 if DBG == 1 else 2):
                                g = c * 2 + hh
                                hp = slice(hh * 64, hh * 64 + 64)
                                cc = slice(c * 64, c * 64 + 64)
                                last = (g == (NG - 2 if DBG == 1 else NG - 1))
                                mk.op(pe, lambda g=g, hp=hp, cc=cc, hh=hh, c=c: (mm64(p2, g * 128, BK, 1, hh, c, AR[hp, 0, cc]), mm64(p2, g * 128 + 64, BK, 1, hh, c, AR[hp, 1, cc]))[1],
                                      [BK_b, AR_b], bb2, signal=last)
                        for hb_ in range(2):
                            gs = slice(hb_ * 4, hb_ * 4 + 4)
                            mk.op(dve, lambda hb_=hb_, gs=gs: nc.vector.tensor_tensor(out=KA[:, gs, :], in0=p2[0:64, hb_ * 512:(hb_ + 1) * 512].rearrange("p (g n) -> p g n", n=128),
                                                                                     in1=mask3(None, 0)[:, 0:4, :], op=ALU.mult), bb2 + [cst_fb], [KA_b])
                        rel(b2, 2)
                        yield
                        b3, bb3 = bank()
                        p3 = pap(b3)
                        for c in range(NCH):
                            for hh in range(1 if DBG == 1 else 2):
                                g = c * 2 + hh
                                hp = slice(hh * 64, hh * 64 + 64)
                                cc = slice(c * 64, c * 64 + 64)
                                last = (g == (NG - 2 if DBG == 1 else NG - 1))
                                mk.op(pe, lambda g=g, hp=hp, cc=cc, hh=hh, c=c: mm64(p3, g * 64, AR, 0, hh, c, BK[hp, 0, cc]),
                                      [BK_b, AR_b], bb3, signal=last)
                        mk.op(dve, lambda: nc.vector.tensor_tensor(out=Qts[0][:], in0=p3[0:64, 0:NG * 64].rearrange("p (g n) -> p g n", n=64),
                                                                   in1=mask3(None, 1), op=ALU.mult), bb3 + [cst_fb], [Qt_b[0]])
                        rel(b3)
                        yield
                        mk.op(act, lambda: nc.scalar.copy(out=Qs[0][:], in_=NA[:, :, 0:64]), [NA_b], [Q_b[0]])
                        mk.op(dve, lambda: nc.vector.tensor_tensor(out=Ps[0][:], in0=NA[:, :, 0:64], in1=mask3(None, 2), op=ALU.add), [NA_b, cst_fb], [P_b[0]])
                        qi = 0
                        pi = 0
                        for m_ in range(1, 6):
                            qn = 1 - qi
                            pn = 1 - pi
                            if m_ < 5:
                                bq, bbq = bank()
                                pq = pap(bq)
                                for g in range(NG):
                                    mk.op(pe, lambda g=g, qi=qi, pq=pq: nc.tensor.matmul(pq[0:64, g * 64:(g + 1) * 64], Qts[qi][:, g, :], Qs[qi][:, g, :], start=True, stop=True),
                                          [Q_b[qi], Qt_b[qi]], bbq, signal=(g == NG - 1))
                            bqt, bbqt = bank()
                            pqt = pap(bqt)
                            for g in range(NG):
                                mk.op(pe, lambda g=g, qi=qi, pqt=pqt: nc.tensor.matmul(pqt[0:64, g * 64:(g + 1) * 64], Qs[qi][:, g, :], Qts[qi][:, g, :], start=True, stop=True),
                                      [Q_b[qi], Qt_b[qi]], bbqt, signal=(g == NG - 1))
                            if m_ < 5:
                                mk.op(act, lambda qn=qn, pq=pq: nc.scalar.copy(out=Qs[qn][:], in_=pq[0:64, 0:NG * 64].rearrange("p (g n) -> p g n", n=64)), bbq, [Q_b[qn]])
                                rel(bq)
                            mk.op(dve, lambda qn=qn, pqt=pqt: nc.vector.tensor_copy(out=Qts[qn][:], in_=pqt[0:64, 0:NG * 64].rearrange("p (g n) -> p g n", n=64)), bbqt, [Qt_b[qn]])
                            rel(bqt)
                            yield
                            bp_, bbp_ = bank()
                            pp_ = pap(bp_)
                            for g in range(NG):
                                mk.op(pe, lambda g=g, pi=pi, qn=qn, pp_=pp_: nc.tensor.matmul(pp_[0:64, g * 64:(g + 1) * 64], Qts[qn][:, g, :], Ps[pi][:, g, :], start=True, stop=True),
                                      [Qt_b[qn], P_b[pi]], bbp_, signal=(g == NG - 1))
                            mk.op(dve, lambda pn=pn, pi=pi, pp_=pp_: nc.vector.tensor_tensor(out=Ps[pn][:], in0=pp_[0:64, 0:NG * 64].rearrange("p (g n) -> p g n", n=64), in1=Ps[pi][:], op=ALU.add),
                                  bbp_ + [P_b[pi]], [P_b[pn]])
                            rel(bp_)
                            yield
                            qi = qn
                            pi = pn
                        Pf, Pf_b = Ps[pi], P_b[pi]
                        yield 'S'
                        for c in range(NCH):
                            cc = slice(c * 64, c * 64 + 64)
                            br, bbr = bank()
                            pr_ = pap(br)
                            for hh in range(2):
                                g = c * 2 + hh
                                hp = slice(hh * 64, hh * 64 + 64)
                                mk.op(pe, lambda hh=hh, hp=hp: mm64(pr_, hh * 64, AR, 0, hh, c, Hb[hp, fc, :], stop=False),
                                      [AR_b, Hb_b[fc]], bbr, signal=False)
                                mk.op(pe, lambda hh=hh, g=g: nc.tensor.matmul(pr_[0:64, hh * 64:(hh + 1) * 64], KA[:, g, 0:64], Vtok[:, c, hh * 64:(hh + 1) * 64], start=False, stop=True),
                                      [KA_b, Vtok_b], bbr, signal=(hh == 1))
                            mk.op(act, lambda: nc.scalar.copy(out=ru[:].rearrange("p h v -> p (h v)"), in_=pr_[0:64, 0:128]), bbr, [ru_b])
                            rel(br)
                            yield
                            bu2, bbu2 = bank()
                            pu_ = pap(bu2)
                            for hh in range(2):
                                g = c * 2 + hh
                                mk.op(pe, lambda hh=hh, g=g: nc.tensor.matmul(pu_[0:64, hh * 64:(hh + 1) * 64], Pf[:, g, :], ru[:, hh, :], start=True, stop=True),
                                      [Pf_b, ru_b], bbu2, signal=(hh == 1))
                            mk.op(act, lambda: nc.scalar.copy(out=Us[:].rearrange("p h v -> p (h v)"), in_=pu_[0:64, 0:128]), bbu2, [Us_b])
                            rel(bu2)
                            yield
                            by, bby = bank()
                            py_ = pap(by)
                            for hh in range(2):
                                g = c * 2 + hh
                                hp = slice(hh * 64, hh * 64 + 64)
                                mk.op(pe, lambda hh=hh, hp=hp: mm64(py_, hh * 64, AR, 1, hh, c, Hb[hp, fc, :], stop=False),
                                      [AR_b, Hb_b[fc]], bby, signal=False)
                                mk.op(pe, lambda hh=hh, g=g: nc.tensor.matmul(py_[0:64, hh * 64:(hh + 1) * 64], NA[:, g, 64:128], Us[:, hh, :], start=False, stop=False),
                                      [NA_b, Us_b], bby, signal=False)
                                mk.op(pe, lambda hh=hh, g=g: nc.tensor.matmul(py_[0:64, hh * 64:(hh + 1) * 64], KA[:, g, 64:128], Vtok[:, c, hh * 64:(hh + 1) * 64], start=False, stop=True),
                                      [KA_b, Vtok_b], bby, signal=(hh == 1))
                            mk.op(act, lambda c=c: nc.scalar.copy(out=ysb[:, c, :], in_=py_[0:64, 0:128]), bby, [ysb_b])
                            rel(by)
                            yield
                            bh, bbh = bank()
                            ph_ = pap(bh)
                            for hh in range(2):
                                mk.op(pe, lambda hh=hh: nc.tensor.matmul(ph_[:, hh * 64:(hh + 1) * 64], Btok[:, c, :], Us[:, hh, :], start=True, stop=False),
                                      [Btok_b, Us_b], bbh, signal=False)
                                mk.op(pe, lambda hh=hh: nc.tensor.matmul(ph_[:, hh * 64:(hh + 1) * 64], Ktok[:, c, :], Vtok[:, c, hh * 64:(hh + 1) * 64], start=False, stop=True),
                                      [Ktok_b, Vtok_b], bbh, signal=(hh == 1))
                            wc = c * 64 + 63
                            for hh in range(2):
                                hp = slice(hh * 64, hh * 64 + 64)
                                mk.op(dve, lambda hp=hp: nc.vector.tensor_scalar(out=Hf[hp, fc, :], in0=Hf[hp, fc, :], scalar1=F["wincl"][hp, wc:wc + 1], scalar2=None, op0=ALU.mult),
                                      [Hf_b[fc], FB["wincl"]], [Hf_b[fc]])
                                mk.op(dve, lambda hp=hp, hh=hh: nc.vector.scalar_tensor_tensor(out=Hf[hp, fc, :], in0=ph_[hp, hh * 64:(hh + 1) * 64], scalar=F["wincl"][hp, wc:wc + 1],
                                                                                              in1=Hf[hp, fc, :], op0=ALU.mult, op1=ALU.add),
                                      bbh + [Hf_b[fc], FB["wincl"]], [Hf_b[fc]])
                            mk.op(act, lambda: nc.scalar.copy(out=Hb[:, fc, :], in_=Hf[:, fc, :]), [Hf_b[fc]], [Hb_b[fc]])
                            rel(bh)
                            yield
                        y3 = ysb[:].rearrange("p c (h v) -> p (c h) v", v=64)
                        q3 = ysq[:].rearrange("p c (h v) -> p (c h) v", v=64)
                        n3 = yn[:].rearrange("p c (h v) -> p (c h) v", v=64)
                        mk.op(dve, lambda: nc.vector.tensor_reduce(out=st1[:, 0, :], in_=y3, axis=AX.X, op=ALU.add), [ysb_b], [st_b])
                        mk.op(act, lambda: nc.scalar.activation(out=ysq[:], in_=ysb[:], func=AF.Square), [ysb_b], [ysq_b])
                        mk.op(dve, lambda: nc.vector.tensor_reduce(out=st1[:, 1, :], in_=q3, axis=AX.X, op=ALU.add), [ysq_b], [st_b])
                        mk.op(dve, lambda: nc.vector.tensor_scalar(out=st1[:, 2, :], in0=st1[:, 0, :], scalar1=1.0 / 64, scalar2=None, op0=ALU.mult), [st_b], [st_b])
                        mk.op(dve, lambda: nc.vector.tensor_tensor(out=st1[:, 3, :], in0=st1[:, 2, :], in1=st1[:, 2, :], op=ALU.mult), [st_b], [st_b])
                        mk.op(dve, lambda: nc.vector.scalar_tensor_tensor(out=st1[:, 4, :], in0=st1[:, 1, :], scalar=1.0 / 64, in1=st1[:, 3, :], op0=ALU.mult, op1=ALU.subtract),
                              [st_b], [st_b])
                        mk.op(act, lambda: nc.scalar.activation(out=st1[:, 5, :], in_=st1[:, 4, :], func=AF.Sqrt, bias=LNX_EPS, scale=1.0), [st_b], [st_b])
                        mk.op(dve, lambda: nc.vector.reciprocal(out=st1[:, 3, :], in_=st1[:, 5, :]), [st_b], [st_b])
                        mk.op(dve, lambda: nc.vector.tensor_tensor(out=n3, in0=y3, in1=st1[:, 2, :].unsqueeze(2).broadcast_to([64, NG, 64]), op=ALU.subtract),
                              [ysb_b, st_b], [yn_b])
                        mk.op(dve, lambda: nc.vector.tensor_tensor(out=n3, in0=n3, in1=st1[:, 3, :].unsqueeze(2).broadcast_to([64, NG, 64]), op=ALU.mult),
                              [yn_b, st_b], [yn_b])
                        yield
                        bt2, bbt2 = bank()
                        pt2 = pap(bt2)
                        for c in range(NCH):
                            mk.op(pe, lambda c=c: nc.tensor.transpose(pt2[:, c * 64:(c + 1) * 64], yn[:, c, :], identf),
                                  [yn_b, cst_fb], bbt2, signal=(c == NCH - 1))
                        mk.op(dve, lambda: nc.vector.tensor_tensor(out=ytmp[:, :nt], in0=F["c2"][:, :nt], in1=pt2[:, :nt], op=ALU.mult), [FB["c2"]] + bbt2, [ytmp_b])
                        rel(bt2)
                        mk.op(dve, lambda: nc.vector.tensor_tensor(out=opT[:, 8 + fc, :nt], in0=ytmp[:, :nt], in1=F["c1"][:, :nt], op=ALU.add), [ytmp_b, FB["c1"]], [opT_b[8 + fc]])

                    def step(g):
                        try:
                            return next(g)
                        except StopIteration:
                            return "END"

                    g_conv = gen_conv()
                    g_cur = gen_fc(0, XS[0])
                    while step(g_cur) != "S":
                        step(g_conv)
                    for fc in range(8):
                        g_next = gen_fc(fc + 1, XS[(fc + 1) % 2]) if fc < 7 else None
                        next_at_s = g_next is None
                        cur_alive = True
                        while cur_alive:
                            if step(g_cur) == "END":
                                cur_alive = False
                            step(g_conv)
                            for _ in range(3):
                                if not next_at_s and step(g_next) == "S":
                                    next_at_s = True
                        while not next_at_s:
                            if step(g_next) == "S":
                                next_at_s = True
                        g_cur = g_next
                    while step(g_conv) != "END":
                        pass

                    chk(9)
                    for fo in range(NC_F):
                        s = wslot()
                        wsrc_ap, wsrc_b = wsrc("wout", l, fo)
                        mk.dma(sp, wi[s][:], wsrc_ap.rearrange("p (c n) -> p c n", n=128), reads=[wsrc_b], writes=[wi_b[s]])
                        b, bb = bank()
                        for kc in range(NC_F):
                            mk.op(pe, lambda kc=kc, s=s: nc.tensor.matmul(pap(b)[:, :nt], wi[s][:, kc, :], opT[:, kc, :nt], start=(kc == 0), stop=(kc == NC_F - 1)),
                                  [wi_b[s], opT_b[kc]], bb, signal=(kc == NC_F - 1))
                        mk.op(act, lambda fo=fo: nc.scalar.copy(out=yT[:, fo, :nt], in_=pap(b)[:, :nt]), bb, [yT_b[fo]])
                        rel(b)
                except _Stop:
                    pass
                postnorm_residual(PP_PM, nt, 1.0)
                store_h(dst, t0, nt)
        mk.barrier()

    def mask3(_, kind):
        NG = (TM // 64) * 2
        if kind == 0:
            return cst2_f[:, 0:NG * 128].rearrange("p (g n) -> p g n", n=128)
        if kind == 1:
            return cst2_f[:, 1024:1024 + NG * 64].rearrange("p (g n) -> p g n", n=64)
        return cst2_f[:, 1536:1536 + NG * 64].rearrange("p (g n) -> p g n", n=64)

    for l in range(depth):
        load_params(l)
        src = h0 if l == 0 else hres
        last = (l == depth - 1)
        if "f1" in phases:
            ffn_phase(l, "wgu1", "wdn1", PP_G1, PP_P1, src, hout if (last and phases[-1] == "f1") else hres)
        if "mix" in phases:
            mixer_phase(l, hres if ("f1" in phases or l > 0) else h0, hout if (last and phases[-1] == "mix") else hres)
        if "f2" in phases:
            ffn_phase(l, "wgu2", "wdn2", PP_G2, PP_P2, hres, hout if last else hres)
    mk.finish()
    es.close()
    mk.dram_names = set(dr)
    return nc, mk


def _consts():
    c = np.zeros((128, 640), np.float32)
    c[:, 0:128] = 1.0
    c[0:64, 128:192] = 1.0
    c[64:128, 192:256] = 1.0
    j = np.arange(64)[:, None]
    i = np.arange(64)[None, :]
    c[0:64, 256:320] = (j < i)
    c[0:64, 320:384] = (j <= i)
    c[:, 384:512] = np.eye(128)
    c[0:64, 512:576] = (i < j)
    c[0:64, 576:640] = np.eye(64)
    return c


def _fm(v):
    return np.ascontiguousarray(v.reshape(-1, 128).T)


def prep_weights(inp, depth):
    f = np.float32
    out = {}
    for nm, key_gu, key_dn in (("1", "ffn1_w_gu", "ffn1_w_down"), ("2", "ffn2_w_gu", "ffn2_w_down")):
        wgu = np.asarray(inp[key_gu][:depth])
        g = wgu[:, :, :DFF].reshape(depth, NC_F, 128, NJ_FF, 128)
        u = wgu[:, :, DFF:].reshape(depth, NC_F, 128, NJ_FF, 128)
        gu = np.stack([g, u], axis=4)
        out["wgu" + nm] = np.ascontiguousarray(gu.transpose(0, 3, 2, 1, 4, 5)).reshape(depth, NJ_FF, 128, NC_F * 256)
        wd = np.asarray(inp[key_dn][:depth]).reshape(depth, NJ_FF, 128, NC_F, 128)
        out["wdn" + nm] = np.ascontiguousarray(wd.transpose(0, 3, 2, 1, 4)).reshape(depth, NC_F, 128, NJ_FF * 128)
    w_in = np.asarray(inp["w_in"][:depth])
    wcat = np.zeros((depth, D, NJ_IN * 128), f)
    wcat[:, :, :6432] = w_in
    for l in range(1, depth):
        wcat[l, :, 6432:6464] = inp["w_in_vres"][l - 1]
    wc = wcat.reshape(depth, NC_F, 128, NJ_IN, 128)
    out["win"] = np.ascontiguousarray(wc.transpose(0, 3, 2, 1, 4)).reshape(depth, NJ_IN, 128, NC_F * 128)
    wo = np.asarray(inp["w_out"][:depth]).reshape(depth, NC_F, 128, NC_F, 128)
    out["wout"] = np.ascontiguousarray(wo.transpose(0, 3, 2, 1, 4)).reshape(depth, NC_F, 128, NC_F * 128)
    lw = np.zeros((depth, 3, 128, RW), f)
    pp = np.zeros((depth, 128, NPP), f)
    for l in range(depth):
        lw[l, 0, 0:64] = inp["decay_w2"][l]
        lw[l, 0, 64:128] = inp["iclr_a2"][l]
        lw[l, 1] = inp["gate_g2"][l][0:128]
        lw[l, 2, 0:32] = inp["gate_g2"][l][128:160]
        if l >= 1:
            lw[l, 2, 32:64] = inp["vres_v2"][l - 1]
        pp[l, :, PP_G1:PP_G1 + 16] = _fm(inp["ffn1_pre_g"][l])
        pp[l, :, PP_GM:PP_GM + 16] = _fm(inp["mix_pre_g"][l])
        pp[l, :, PP_G2:PP_G2 + 16] = _fm(inp["ffn2_pre_g"][l])
        pp[l, :, PP_P1:PP_P1 + 16] = _fm(inp["ffn1_post_g"][l])
        pp[l, :, PP_PM:PP_PM + 16] = _fm(inp["mix_post_g"][l])
        pp[l, :, PP_P2:PP_P2 + 16] = _fm(inp["ffn2_post_g"][l])
        mu = np.zeros(27 * 128, f)
        mu[:3360] = inp["mu_rwkv"][l]
        if l >= 1:
            mu[3360:3392] = inp["mu_vres"][l - 1]
        pp[l, :, PP_MU:PP_MU + 27] = _fm(mu)
        for jj in range(3):
            pp[l, :, PP_CW + 8 * jj:PP_CW + 8 * jj + 8] = _fm(inp["conv_w"][l][jj])
        pp[l, :, PP_CG:PP_CG + 8] = _fm(inp["conv_norm_g"][l])
        pp[l, :, PP_W0:PP_W0 + 8] = _fm(inp["decay_w0"][l])
        pp[l, :, PP_A0:PP_A0 + 8] = _fm(inp["iclr_a0"][l])
        if l >= 1:
            pp[l, :, PP_V0:PP_V0 + 8] = _fm(inp["vres_v0"][l - 1])
        pp[l, :, PP_KK:PP_KK + 8] = _fm(inp["k_k"][l])
        pp[l, :, PP_KA:PP_KA + 8] = _fm(inp["k_a"][l])
        pp[l, :, PP_RK:PP_RK + 8] = _fm(np.asarray(inp["r_k"][l]).reshape(-1))
        pp[l, :, PP_LG:PP_LG + 8] = _fm(inp["lnx_g"][l])
        pp[l, :, PP_LB:PP_LB + 8] = _fm(inp["lnx_b"][l])
    out["lw"] = lw
    out["pp"] = pp
    out["cst"] = _consts()
    c1_ = out["cst"]
    c2_ = np.zeros((64, 2048), f)
    c2_[:, 0:1024] = np.tile(c1_[0:64, 256:384], (1, 8))
    c2_[:, 1024:1536] = np.tile(c1_[0:64, 512:576], (1, 8))
    c2_[:, 1536:2048] = np.tile(c1_[0:64, 576:640], (1, 8))
    out["cst2"] = c2_
    return out


def run_module(inp, depth, T_PAD, n_valid, n_cores=8):
    x = np.asarray(inp["x"], np.float32)
    B, L, _ = x.shape
    meta = np.asarray(inp["meta_tokens"], np.float32)
    import time as _t
    _t0 = _t.time()
    w = prep_weights(inp, depth)
    _t1 = _t.time()
    nc, mk = build_program(T_PAD, depth)
    print("prep %.1fs build %.1fs ninst %d" % (_t1 - _t0, _t.time() - _t1, mk.ninst), flush=True)
    in_maps = []
    for c in range(n_cores):
        b = c % B
        h0 = np.zeros((D, T_PAD), np.float32)
        h0[:, :NMETA] = meta.T
        h0[:, NMETA:NMETA + L] = x[b].T
        m = {k: v for k, v in w.items() if k in mk.dram_names}
        m["h0"] = h0
        in_maps.append(m)
    res = run_bass_kernel_spmd(nc, in_maps, core_ids=list(range(n_cores)))
    out = np.stack([np.ascontiguousarray(res.results[b]["hout"][:, NMETA:NMETA + L].T) for b in range(B)], axis=0)
    return out.astype(np.float32)


def kernel(**inputs):
    T_PAD = 4352
    inp = {k: np.asarray(v) for k, v in inputs.items()}
    return run_module(inp, DEPTH, T_PAD, NMETA + SEQ)
```

```python
import numpy as np
from contextlib import ExitStack
import concourse.bass as bass
import concourse.mybir as mybir
from concourse.bass_utils import run_bass_kernel_spmd

F32 = mybir.dt.float32
BF16 = mybir.dt.bfloat16
ALU = mybir.AluOpType
AF = mybir.ActivationFunctionType
AX = mybir.AxisListType

D = 2048
DFF = 5632
NMETA = 16
SEQ = 4096
BATCH = 4
DEPTH = 4
RW = 1024
NJ_IN = 51
TT = 512
TM = 256
NC_F = D // 128
NJ_FF = DFF // 128
NORM_EPS = 1e-6
LNX_EPS = 64e-5
DEC_C = 0.6065306597126334

PP_G1, PP_GM, PP_G2 = 0, 16, 32
PP_P1, PP_PM, PP_P2 = 48, 64, 80
PP_MU = 96
PP_CW = 123
PP_CG = 147
PP_W0, PP_A0, PP_V0, PP_KK, PP_KA, PP_RK, PP_LG, PP_LB = 155, 163, 171, 179, 187, 195, 203, 211
NPP = 219


class _Stop(Exception):
    pass


class Buf:
    __slots__ = ("name", "w", "r", "pend")

    def __init__(self, name):
        self.name = name
        self.w = None
        self.r = {}
        self.pend = None


class Sem:
    __slots__ = ("h", "val")

    def __init__(self, h):
        self.h = h
        self.val = 0


class Eng:
    def __init__(self, raw, sem, name):
        self.raw = raw
        self.sem = sem
        self.name = name
        self.seen = {}
        self.pr = []
        self.pw = []
        self.dsems = []
        self.di = 0


class MK:
    def __init__(self, nc, es):
        self.nc = nc
        self.es = es
        self.nsem = 0
        self.pe = Eng(nc.tensor, self.newsem("pe"), "pe")
        self.act = Eng(nc.scalar, self.newsem("act"), "act")
        self.dve = Eng(nc.vector, self.newsem("dve"), "dve")
        self.pool = Eng(nc.gpsimd, self.newsem("pool"), "pool")
        self.sp = Eng(nc.sync, self.newsem("sp"), "sp")
        self.engs = [self.pe, self.act, self.dve, self.pool, self.sp]
        for q in (self.sp, self.pool):
            q.dsems = [self.newsem("d%s%d" % (q.name, i)) for i in range(12)]
        self.ninst = 0

    def newsem(self, name):
        self.nsem += 1
        return Sem(self.es.enter_context(self.nc.semaphore(name)))

    def _waits(self, eng, need):
        for sem, v in need.items():
            if eng.seen.get(sem, 0) < v:
                eng.raw.wait_ge(sem.h, v)
                eng.seen[sem] = v
                self.ninst += 1

    def _need(self, eng, reads, writes, need):
        for b in reads:
            assert b.pend is None or b.pend is eng, (b.name, eng.name)
            if b.w is not None and need.get(b.w[0], 0) < b.w[1]:
                need[b.w[0]] = b.w[1]
        for b in writes:
            assert b.pend is None or b.pend is eng, (b.name, eng.name)
            if b.w is not None and need.get(b.w[0], 0) < b.w[1]:
                need[b.w[0]] = b.w[1]
            for sem, v in b.r.items():
                if need.get(sem, 0) < v:
                    need[sem] = v

    def op(self, eng, fn, reads=(), writes=(), signal=True):
        need = {}
        self._need(eng, reads, writes, need)
        self._waits(eng, need)
        ins = fn()
        self.ninst += 1
        eng.pr.extend(reads)
        eng.pw.extend(writes)
        for b in reads:
            b.pend = eng
        for b in writes:
            b.pend = eng
        if signal:
            eng.sem.val += 1
            ins.then_inc(eng.sem.h, 1)
            v = eng.sem.val
            for b in eng.pw:
                b.w = (eng.sem, v)
                b.r = {}
                b.pend = None
            for b in eng.pr:
                if not (b.w is not None and b.w[0] is eng.sem and b.w[1] == v):
                    b.r[eng.sem] = v
                b.pend = None
            eng.pr = []
            eng.pw = []
            eng.seen[eng.sem] = max(eng.seen.get(eng.sem, 0), 0)
        return ins

    def dma(self, q, out_ap, in_ap, reads=(), writes=()):
        sem = q.dsems[q.di]
        q.di = (q.di + 1) % len(q.dsems)
        need = {sem: sem.val} if sem.val > 0 else {}
        self._need(q, reads, writes, need)
        self._waits(q, need)
        ins = q.raw.dma_start(out=out_ap, in_=in_ap)
        self.ninst += 1
        sem.val += 16
        ins.then_inc(sem.h, 16)
        for b in writes:
            b.w = (sem, sem.val)
            b.r = {}
        for b in reads:
            b.r[sem] = sem.val

    def barrier(self):
        need = {}
        for e in self.engs:
            assert not e.pr and not e.pw, e.name
            if e.sem.val:
                need[e.sem] = e.sem.val
            for s in e.dsems:
                if s.val:
                    need[s] = s.val
        for e in self.engs:
            self._waits(e, dict(need))

    def finish(self):
        need = {}
        for e in self.engs:
            if e.sem.val:
                need[e.sem] = e.sem.val
            for s in e.dsems:
                if s.val:
                    need[s] = s.val
        self._waits(self.sp, need)


def build_program(T_PAD, depth):
    assert T_PAD % TM == 0
    nc = bass.Bass("TRN2", target_bir_lowering=False)
    dr = {}

    def dram(name, shape, dt, kind):
        t = nc.dram_tensor(name, list(shape), dt, kind=kind).ap()
        dr[name] = t
        return t

    h0 = dram("h0", [D, T_PAD], F32, "ExternalInput")
    hout = dram("hout", [D, T_PAD], F32, "ExternalOutput")
    hres = dram("hres", [D, T_PAD], F32, "Internal")
    vfd = dram("vfd", [RW, T_PAD], F32, "Internal")
    import os as _os
    phases = _os.environ.get("MK_PHASES", "f1,mix,f2").split(",")
    wgu1 = wdn1 = wgu2 = wdn2 = None
    if "f1" in phases:
        wgu1 = dram("wgu1", [depth, NJ_FF, 128, NC_F * 256], F32, "ExternalInput")
        wdn1 = dram("wdn1", [depth, NC_F, 128, NJ_FF * 128], F32, "ExternalInput")
    if "f2" in phases:
        wgu2 = dram("wgu2", [depth, NJ_FF, 128, NC_F * 256], F32, "ExternalInput")
        wdn2 = dram("wdn2", [depth, NC_F, 128, NJ_FF * 128], F32, "ExternalInput")
    win = dram("win", [depth, NJ_IN, 128, NC_F * 128], F32, "ExternalInput")
    wout = dram("wout", [depth, NC_F, 128, NC_F * 128], F32, "ExternalInput")
    lwd = dram("lw", [depth, 3, 128, RW], F32, "ExternalInput")
    ppd = dram("pp", [depth, 128, NPP], F32, "ExternalInput")
    cst = dram("cst", [128, 128 + 128 + 256 + 64 + 64], F32, "ExternalInput")
    cst2 = dram("cst2", [64, 2048], F32, "ExternalInput")

    es = ExitStack()
    mk = MK(nc, es)
    pe, act, dve, pool, sp = mk.pe, mk.act, mk.dve, mk.pool, mk.sp

    uid = [0]

    def sbt(stack, name, shape, dt):
        uid[0] += 1
        t = stack.enter_context(nc.sbuf_tensor("s%d_%s" % (uid[0], name), list(shape), dt))
        return t

    psum = es.enter_context(nc.psum_tensor("psum", [128, 8 * 512], F32))
    pbank = [Buf("bank%d" % i) for i in range(8)]
    bank_rr = [0]

    live = [False] * 8

    def bank(n=1):
        for k in range(7):
            b = (bank_rr[0] + k) % 7
            if n == 2 and (b % 2 == 1 or b + 1 > 5):
                continue
            if all(not live[b + i] for i in range(n)):
                for i in range(n):
                    live[b + i] = True
                bank_rr[0] = (b + n) % 7
                return b, pbank[b:b + n]
        raise AssertionError("out of PSUM banks")

    def rel(b, n=1):
        for i in range(n):
            assert live[b + i]
            live[b + i] = False

    def pap(b, n=1):
        return psum[:, b * 512:(b + n) * 512]

    cst_f = sbt(es, "cst_f", [128, 640], F32)
    idb128 = sbt(es, "idb128", [128, 128], BF16)
    cst_fb = Buf("cst_f")
    ones_bf = sbt(es, "ones_bf", [128, 128], BF16)
    blk_bf = sbt(es, "blk_bf", [128, 128], BF16)
    idb = sbt(es, "idb", [64, 64], BF16)
    cbf_b = Buf("cbf")
    pp = sbt(es, "pp", [128, NPP], F32)
    pp_b = Buf("pp")
    lwt = sbt(es, "lwt", [128, 3, RW], BF16)
    lw_b = Buf("lw")
    G = {}

    def alloc_stream(stack, width):
        G["hT"] = sbt(stack, "hT", [128, NC_F, width], F32)
        G["hT_b"] = [Buf("hT%d" % c) for c in range(NC_F)]
        G["yT"] = sbt(stack, "yT", [128, NC_F, width], F32)
        G["yT_b"] = [Buf("yT%d" % c) for c in range(NC_F)]
        G["xT"] = sbt(stack, "xT", [128, NC_F, width], BF16)
        G["xT_b"] = [Buf("xT%d" % c) for c in range(NC_F)]
        return G["hT"], G["hT_b"], G["yT"], G["yT_b"], G["xT"], G["xT_b"]
    sqt = [sbt(es, "sq%d" % i, [128, TT], BF16) for i in range(2)]
    sq_b = [Buf("sq%d" % i) for i in range(2)]
    rbc = sbt(es, "rbc", [128, TT], F32)
    rbc_b = Buf("rbc")
    rbt = sbt(es, "rbt", [128, TT], F32)
    rbt_b = Buf("rbt")
    Hf = sbt(es, "Hf", [128, 8, 64], F32)
    Hf_b = [Buf("Hf%d" % i) for i in range(8)]
    Hb = sbt(es, "Hb", [128, 8, 64], BF16)
    Hb_b = [Buf("Hb%d" % i) for i in range(8)]
    carry = sbt(es, "carry", [128, 27], F32)
    carry_b = [Buf("carry%d" % i) for i in range(27)]
    ccar = sbt(es, "ccar", [128, 8, 2], F32)
    ccar_b = [Buf("ccar%d" % i) for i in range(8)]

    mk.dma(sp, cst_f[:], cst[:, :], writes=[cst_fb])
    cst2_f = sbt(es, "cst2_f", [64, 2048], F32)
    mk.dma(sp, cst2_f[:], cst2[:, :], writes=[cst_fb])
    mk.op(dve, lambda: nc.vector.tensor_copy(out=ones_bf[:], in_=cst_f[:, 0:128]), [cst_fb], [cbf_b])
    mk.op(dve, lambda: nc.vector.tensor_copy(out=blk_bf[:], in_=cst_f[:, 128:256]), [cst_fb], [cbf_b])
    mk.op(dve, lambda: nc.vector.tensor_copy(out=idb[:], in_=cst_f[0:64, 576:640]), [cst_fb], [cbf_b])
    mk.op(dve, lambda: nc.vector.tensor_copy(out=idb128[:], in_=cst_f[:, 384:512]), [cst_fb], [cbf_b])
    maskNA = cst_f[0:64, 256:384]
    maskNT = cst_f[0:64, 512:576]
    identf = cst_f[0:64, 576:640]

    WB = {}
    WBb = {}

    def conv_w(name, src, l):
        if src is None:
            return
        key = name
        if key not in WB:
            WB[key] = nc.dram_tensor(name + "_bf", list(src.shape), BF16, kind="Internal").ap()
            WBb[key] = {}
        n1 = src.shape[1]
        for j in range(n1):
            b_ = Buf("%s_%d_%d" % (name, l, j))
            WBb[key][(l, j)] = b_
            mk.dma(pool, WB[key][l, j], src[l, j], writes=[b_])

    def wsrc(name, l, j):
        return WB[name][l, j], WBb[name][(l, j)]

    def rms_stats(src_tile, src_bufs, nt, eps_scale=1.0 / D):
        b, bb = bank()
        for c in range(NC_F):
            s = c % 2
            mk.op(act, lambda c=c, s=s: nc.scalar.activation(out=sqt[s][:, :nt], in_=src_tile[:, c, :nt], func=AF.Square),
                  [src_bufs[c]], [sq_b[s]])
            mk.op(pe, lambda c=c, s=s: nc.tensor.matmul(pap(b)[:, :nt], ones_bf[:], sqt[s][:, :nt], start=(c == 0), stop=(c == NC_F - 1)),
                  [cbf_b, sq_b[s]], bb, signal=True)
        mk.op(act, lambda: nc.scalar.activation(out=rbt[:, :nt], in_=pap(b)[:, :nt], func=AF.Sqrt, bias=NORM_EPS, scale=eps_scale),
              bb, [rbt_b])
        rel(b)
        mk.op(dve, lambda: nc.vector.reciprocal(out=rbc[:, :nt], in_=rbt[:, :nt]), [rbt_b], [rbc_b])

    def load_h(src, t0, nt):
        hT, hT_b = G["hT"], G["hT_b"]
        v = src.rearrange("(c p) t -> p c t", p=128)
        for half in range(2):
            cs = slice(half * 8, half * 8 + 8)
            mk.dma(sp, hT[:, cs, :nt], v[:, cs, t0:t0 + nt], writes=hT_b[half * 8:half * 8 + 8])

    def store_h(dst, t0, nt):
        hT, hT_b = G["hT"], G["hT_b"]
        v = dst.rearrange("(c p) t -> p c t", p=128)
        for half in range(2):
            cs = slice(half * 8, half * 8 + 8)
            mk.dma(sp, v[:, cs, t0:t0 + nt], hT[:, cs, :nt], reads=hT_b[half * 8:half * 8 + 8])

    def prenorm(gcol, nt):
        hT, hT_b, xT, xT_b = G["hT"], G["hT_b"], G["xT"], G["xT_b"]
        rms_stats(hT, hT_b, nt)
        for c in range(NC_F):
            mk.op(dve, lambda c=c: nc.vector.scalar_tensor_tensor(out=xT[:, c, :nt], in0=hT[:, c, :nt], scalar=pp[:, gcol + c:gcol + c + 1],
                                                                 in1=rbc[:, :nt], op0=ALU.mult, op1=ALU.mult),
                  [hT_b[c], pp_b, rbc_b], [xT_b[c]])

    def postnorm_residual(pcol, nt, half):
        hT, hT_b, yT, yT_b = G["hT"], G["hT_b"], G["yT"], G["yT_b"]
        rms_stats(yT, yT_b, nt)
        for c in range(NC_F):
            mk.op(dve, lambda c=c: nc.vector.scalar_tensor_tensor(out=yT[:, c, :nt], in0=yT[:, c, :nt], scalar=pp[:, pcol + c:pcol + c + 1],
                                                                 in1=rbc[:, :nt], op0=ALU.mult, op1=ALU.mult),
                  [yT_b[c], pp_b, rbc_b], [yT_b[c]])
            mk.op(dve, lambda c=c: nc.vector.scalar_tensor_tensor(out=hT[:, c, :nt], in0=yT[:, c, :nt], scalar=float(half),
                                                                 in1=hT[:, c, :nt], op0=ALU.mult, op1=ALU.add),
                  [yT_b[c], hT_b[c]], [hT_b[c]])

    def load_params(l):
        mk.dma(sp, pp[:], ppd[l], writes=[pp_b])
        mk.dma(pool, lwt[:], lwd[l].rearrange("k p n -> p k n"), writes=[lw_b])
        if l == 0:
            for l2 in range(1):
                if "f1" in phases:
                    conv_w("wgu1", wgu1, l2)
                    conv_w("wdn1", wdn1, l2)
                if "mix" in phases:
                    conv_w("win", win, l2)
                    conv_w("wout", wout, l2)
                if "f2" in phases:
                    conv_w("wgu2", wgu2, l2)
                    conv_w("wdn2", wdn2, l2)

    def ffn_phase(l, wgu, wdn, gcol, pcol, src, dst):
        with ExitStack() as st:
            hT, hT_b, yT, yT_b, xT, xT_b = alloc_stream(st, TT)
            hid = sbt(st, "hid", [128, NJ_FF, TT], BF16)
            hid_b = [Buf("hid%d" % j) for j in range(NJ_FF)]
            wg = [sbt(st, "wg%d" % i, [128, NC_F, 256], BF16) for i in range(3)]
            wg_b = [Buf("wg%d" % i) for i in range(3)]
            wd = [sbt(st, "wd%d" % i, [128, NJ_FF, 128], BF16) for i in range(2)]
            wd_b = [Buf("wd%d" % i) for i in range(2)]
            sg = [sbt(st, "sg%d" % i, [128, TT], F32) for i in range(2)]
            sg_b = [Buf("sg%d" % i) for i in range(2)]
            for ti in range((T_PAD + TT - 1) // TT):
                t0 = ti * TT
                nt = min(TT, T_PAD - t0)
                load_h(src, t0, nt)
                prenorm(gcol, nt)
                for j in range(NJ_FF):
                    s = j % 2
                    ws = j % 3
                    wsrc_ap, wsrc_b = wsrc(wgu, l, j)
                    mk.dma(sp, wg[ws][:], wsrc_ap.rearrange("p (c n) -> p c n", n=256), reads=[wsrc_b], writes=[wg_b[ws]])
                    ba, bba = bank()
                    bu, bbu = bank()
                    for c in range(NC_F):
                        mk.op(pe, lambda c=c, ws=ws: nc.tensor.matmul(pap(ba)[:, :nt], wg[ws][:, c, 0:128], xT[:, c, :nt], start=(c == 0), stop=(c == NC_F - 1)),
                              [wg_b[ws], xT_b[c]], bba, signal=(c == NC_F - 1))
                    for c in range(NC_F):
                        mk.op(pe, lambda c=c, ws=ws: nc.tensor.matmul(pap(bu)[:, :nt], wg[ws][:, c, 128:256], xT[:, c, :nt], start=(c == 0), stop=(c == NC_F - 1)),
                              [wg_b[ws], xT_b[c]], bbu, signal=(c == NC_F - 1))
                    mk.op(act, lambda s=s: nc.scalar.activation(out=sg[s][:, :nt], in_=pap(ba)[:, :nt], func=AF.Silu), bba, [sg_b[s]])
                    mk.op(dve, lambda s=s, j=j: nc.vector.tensor_tensor(out=hid[:, j, :nt], in0=sg[s][:, :nt], in1=pap(bu)[:, :nt], op=ALU.mult),
                          [sg_b[s]] + bbu, [hid_b[j]])
                    rel(ba)
                    rel(bu)
                for fc in range(NC_F):
                    s = fc % 2
                    wsrc_ap, wsrc_b = wsrc(wdn, l, fc)
                    mk.dma(sp, wd[s][:], wsrc_ap.rearrange("p (k n) -> p k n", n=128), reads=[wsrc_b], writes=[wd_b[s]])
                    b, bb = bank()
                    for kc in range(NJ_FF):
                        mk.op(pe, lambda kc=kc, s=s: nc.tensor.matmul(pap(b)[:, :nt], wd[s][:, kc, :], hid[:, kc, :nt], start=(kc == 0), stop=(kc == NJ_FF - 1)),
                              [wd_b[s], hid_b[kc]], bb, signal=(kc == NJ_FF - 1))
                    mk.op(act, lambda fc=fc: nc.scalar.copy(out=yT[:, fc, :nt], in_=pap(b)[:, :nt]), bb, [yT_b[fc]])
                    rel(b)
                postnorm_residual(pcol, nt, 0.5)
                store_h(dst, t0, nt)
        mk.barrier()

    import os as _os2
    MIXSTOP = int(_os2.environ.get("MK_MIXSTOP", "0"))

    DBG = int(_os2.environ.get("MK_DBG", "0"))

    def mm64(pt, col, L, a, hh, c, rhs, stop=True):
        if hh == 0:
            return nc.tensor.matmul(pt[0:64, col:col + 64], L[0:64, a, c * 64:c * 64 + 64], rhs, start=True, stop=stop)
        return nc.tensor.matmul(pt[:, col:col + 64], L[64:128, a, c * 64:c * 64 + 128], rhs, start=True, stop=stop)

    def chk(n):
        if n == MIXSTOP:
            raise _Stop()

    def mixer_phase(l, src, dst):
        with ExitStack() as st:
            hT, hT_b, yT, yT_b, xT, xT_b = alloc_stream(st, TM)
            nt = TM
            NCH = TM // 64
            wi = [sbt(st, "wi%d" % i, [128, NC_F, 128], BF16) for i in range(3)]
            wi_b = [Buf("wi%d" % i) for i in range(3)]
            wrr = [0]

            def wslot():
                s = wrr[0]
                wrr[0] = (s + 1) % 3
                return s

            def f32t(name, w=TM):
                return sbt(st, name, [128, w], F32), Buf(name)

            def bf16t(name, w=TM):
                return sbt(st, name, [128, w], BF16), Buf(name)

            praw, praw_b = f32t("praw", TM + 1)
            dtmp, dtmp_b = f32t("dtmp")
            L48, L48_b = f32t("L48")
            L49, L49_b = f32t("L49")
            L50, L50_b = f32t("L50")
            LB48, LB48_b = bf16t("LB48")
            LB49, LB49_b = bf16t("LB49")
            LB50, LB50_b = bf16t("LB50")
            gcs, gcs_b = f32t("gcs")
            uc, uc_b = f32t("uc", TM + 2)
            cv, cv_b = f32t("cv")
            zg = sbt(st, "zg", [128, 8, TM], F32)
            zg_b = [Buf("zg%d" % q) for q in range(8)]
            zsq, zsq_b = bf16t("zsq")
            opT = sbt(st, "opT", [128, NC_F, TM], BF16)
            opT_b = [Buf("opT%d" % q) for q in range(NC_F)]
            rT, rT_b = f32t("rT")
            kT, kT_b = f32t("kT")
            vT, vT_b = f32t("vT")
            names = ["sw", "aa", "gg", "sv", "vf", "kk", "rt", "rn", "kkn", "t1", "kmod", "bvec", "csa", "csb",
                     "wincl", "exc", "wexc", "winv", "c1", "c2", "bs"]
            F0 = {}
            FB0 = {}
            for n_ in names:
                F0[n_], FB0[n_] = f32t(n_)
            ytmp, ytmp_b = f32t("ytmp")
            csz = [sbt(st, "csz%d" % i, [128, NCH, 96], F32) for i in range(2)]
            csz_b = [Buf("csz%d" % i) for i in range(2)]
            kksq, kksq_b = bf16t("kksq")
            rkb, rkb_b = bf16t("rkb")
            vbf, vbf_b = bf16t("vbf")
            NG = NCH * 2
            XS = []
            for xi in range(2):
                X = {}
                X["AR"] = sbt(st, "AR", [128, 2, TM + 64], BF16)
                X["BK"] = sbt(st, "BK", [128, 2, TM + 64], BF16)
                X["AR_b"], X["BK_b"] = Buf("AR"), Buf("BK")
                for nm_ in ("Vtok", "Btok", "Ktok"):
                    X[nm_] = sbt(st, nm_, [64, NCH, 128], BF16)
                    X[nm_ + "_b"] = Buf(nm_)
                for nm_ in ("NA", "KA"):
                    X[nm_] = sbt(st, nm_, [64, NG, 128], BF16)
                    X[nm_ + "_b"] = Buf(nm_)
                X["Qs"] = [sbt(st, "Q%d" % i, [64, NG, 64], BF16) for i in range(2)]
                X["Qts"] = [sbt(st, "Qt%d" % i, [64, NG, 64], BF16) for i in range(2)]
                X["Ps"] = [sbt(st, "P%d" % i, [64, NG, 64], BF16) for i in range(2)]
                X["Q_b"] = [Buf("Q%d" % i) for i in range(2)]
                X["Qt_b"] = [Buf("Qt%d" % i) for i in range(2)]
                X["P_b"] = [Buf("P%d" % i) for i in range(2)]
                X["F"] = {}
                X["FB"] = {}
                for nm_ in ("wincl", "c1", "c2"):
                    X["F"][nm_], X["FB"][nm_] = f32t(nm_ + "x")
                XS.append(X)
            ru = sbt(st, "ru", [64, 2, 64], BF16)
            ru_b = Buf("ru")
            Us = sbt(st, "Us", [64, 2, 64], BF16)
            Us_b = Buf("Us")
            ysb = sbt(st, "ysb", [64, NCH, 128], F32)
            ysb_b = Buf("ysb")
            ysq = sbt(st, "ysq", [64, NCH, 128], F32)
            ysq_b = Buf("ysq")
            yn = sbt(st, "yn", [64, NCH, 128], F32)
            yn_b = Buf("yn")
            st1 = sbt(st, "st1", [64, 6, NG], F32)
            st_b = Buf("st1")

            if l == 0:
                pass
            for X in XS:
                mk.op(dve, lambda X=X: nc.vector.memset(X["AR"][:], 0.0), [], [X["AR_b"]])
                mk.op(dve, lambda X=X: nc.vector.memset(X["BK"][:], 0.0), [], [X["BK_b"]])
            for i_ in range(2):
                mk.op(dve, lambda i_=i_: nc.vector.memset(csz[i_][:], 0.0), [], [csz_b[i_]])
            mk.op(dve, lambda: nc.vector.memset(carry[:], 0.0), [], carry_b)
            mk.op(dve, lambda: nc.vector.memset(ccar[:], 0.0), [], ccar_b)
            mk.op(dve, lambda: nc.vector.memset(Hf[:], 0.0), [], Hf_b)
            mk.op(dve, lambda: nc.vector.memset(Hb[:], 0.0), [], Hb_b)

            def proj(j):
                s = wslot()
                wsrc_ap, wsrc_b = wsrc("win", l, j)
                mk.dma(sp, wi[s][:], wsrc_ap.rearrange("p (c n) -> p c n", n=128), reads=[wsrc_b], writes=[wi_b[s]])
                b, bb = bank()
                for c in range(NC_F):
                    mk.op(pe, lambda c=c, s=s: nc.tensor.matmul(pap(b)[:, :nt], wi[s][:, c, :], xT[:, c, :nt], start=(c == 0), stop=(c == NC_F - 1)),
                          [wi_b[s], xT_b[c]], bb, signal=(c == NC_F - 1))
                return b, bb

            def shifted(j, dst_t, dst_b):
                m = j - 24
                b, bb = proj(j)
                mk.op(act, lambda: nc.scalar.copy(out=praw[:, 1:nt + 1], in_=pap(b)[:, :nt]), bb, [praw_b])
                rel(b)
                mk.op(act, lambda: nc.scalar.copy(out=praw[:, 0:1], in_=carry[:, m:m + 1]), [carry_b[m]], [praw_b])
                mk.op(act, lambda: nc.scalar.copy(out=carry[:, m:m + 1], in_=praw[:, nt:nt + 1]), [praw_b], [carry_b[m]])
                mk.op(dve, lambda: nc.vector.tensor_tensor(out=dtmp[:, :nt], in0=praw[:, 0:nt], in1=praw[:, 1:nt + 1], op=ALU.subtract),
                      [praw_b], [dtmp_b])
                mk.op(dve, lambda: nc.vector.scalar_tensor_tensor(out=dst_t[:, :nt], in0=dtmp[:, :nt], scalar=pp[:, PP_MU + m:PP_MU + m + 1],
                                                                 in1=praw[:, 1:nt + 1], op0=ALU.mult, op1=ALU.add),
                      [dtmp_b, praw_b, pp_b], [dst_b])

            for ti in range(T_PAD // TM):
                t0 = ti * TM
                load_h(src, t0, nt)
                prenorm(PP_GM, nt)
                try:
                    shifted(48, L48, L48_b)
                    shifted(49, L49, L49_b)
                    shifted(50, L50, L50_b)
                    mk.op(act, lambda: nc.scalar.activation(out=LB48[0:64, :nt], in_=L48[0:64, :nt], func=AF.Tanh), [L48_b], [LB48_b])
                    mk.op(act, lambda: nc.scalar.copy(out=LB48[64:128, :nt], in_=L48[64:128, :nt]), [L48_b], [LB48_b])
                    mk.op(act, lambda: nc.scalar.activation(out=LB49[:, :nt], in_=L49[:, :nt], func=AF.Sigmoid), [L49_b], [LB49_b])
                    mk.op(act, lambda: nc.scalar.activation(out=LB50[0:32, :nt], in_=L50[0:32, :nt], func=AF.Sigmoid), [L50_b], [LB50_b])
                    mk.op(act, lambda: nc.scalar.copy(out=LB50[32:64, :nt], in_=L50[32:64, :nt]), [L50_b], [LB50_b])
                    chk(1)
                    def gen_conv():
                        bz, bbz = 7, pbank[7:8]
                        for q in range(8):
                            bgb, bbgb = proj(q)
                            bgc, bbgc = proj(8 + q)
                            bu_, bbu_ = proj(16 + q)
                            mk.op(act, lambda: nc.scalar.copy(out=gcs[:, :nt], in_=pap(bgc)[:, :nt]), bbgc, [gcs_b])
                            rel(bgc)
                            mk.op(act, lambda q=q: nc.scalar.copy(out=uc[:, 0:2], in_=ccar[:, q, :]), [ccar_b[q]], [uc_b])
                            mk.op(dve, lambda: nc.vector.tensor_tensor(out=uc[:, 2:nt + 2], in0=gcs[:, :nt], in1=pap(bu_)[:, :nt], op=ALU.mult),
                                  [gcs_b] + bbu_, [uc_b])
                            rel(bu_)
                            mk.op(act, lambda q=q: nc.scalar.copy(out=ccar[:, q, :], in_=uc[:, nt:nt + 2]), [uc_b], [ccar_b[q]])
                            cw = PP_CW + q
                            mk.op(dve, lambda cw=cw: nc.vector.tensor_scalar(out=cv[:, :nt], in0=uc[:, 2:nt + 2], scalar1=pp[:, cw + 16:cw + 17], scalar2=None, op0=ALU.mult),
                                  [uc_b, pp_b], [cv_b])
                            mk.op(dve, lambda cw=cw: nc.vector.scalar_tensor_tensor(out=cv[:, :nt], in0=uc[:, 1:nt + 1], scalar=pp[:, cw + 8:cw + 9], in1=cv[:, :nt],
                                                                                   op0=ALU.mult, op1=ALU.add), [uc_b, pp_b, cv_b], [cv_b])
                            mk.op(dve, lambda cw=cw: nc.vector.scalar_tensor_tensor(out=cv[:, :nt], in0=uc[:, 0:nt], scalar=pp[:, cw:cw + 1], in1=cv[:, :nt],
                                                                                   op0=ALU.mult, op1=ALU.add), [uc_b, pp_b, cv_b], [cv_b])
                            mk.op(dve, lambda: nc.vector.tensor_tensor(out=cv[:, :nt], in0=cv[:, :nt], in1=pap(bgb)[:, :nt], op=ALU.mult),
                                  [cv_b] + bbgb, [cv_b])
                            rel(bgb)
                            mk.op(act, lambda: nc.scalar.activation(out=zsq[:, :nt], in_=cv[:, :nt], func=AF.Square), [cv_b], [zsq_b])
                            mk.op(pe, lambda q=q: nc.tensor.matmul(pap(bz)[:, :nt], ones_bf[:], zsq[:, :nt], start=(q == 0), stop=(q == 7)),
                                  [cbf_b, zsq_b], bbz)
                            mk.op(dve, lambda q=q: nc.vector.tensor_scalar(out=zg[:, q, :nt], in0=cv[:, :nt], scalar1=pp[:, PP_CG + q:PP_CG + q + 1], scalar2=None, op0=ALU.mult),
                                  [cv_b, pp_b], [zg_b[q]])
                            yield
                        mk.op(act, lambda: nc.scalar.activation(out=rbt[:, :nt], in_=pap(bz)[:, :nt], func=AF.Sqrt, bias=NORM_EPS, scale=1.0 / 1024), bbz, [rbt_b])
                        mk.op(dve, lambda: nc.vector.reciprocal(out=rbc[:, :nt], in_=rbt[:, :nt]), [rbt_b], [rbc_b])
                        yield
                        for q in range(8):
                            mk.op(dve, lambda q=q: nc.vector.tensor_tensor(out=opT[:, q, :nt], in0=zg[:, q, :nt], in1=rbc[:, :nt], op=ALU.mult),
                                  [zg_b[q], rbc_b], [opT_b[q]])

                    chk(2)
                    def gen_fc(fc, X):
                        AR, AR_b, BK, BK_b = X["AR"], X["AR_b"], X["BK"], X["BK_b"]
                        Vtok, Btok, Ktok, Vtok_b, Btok_b, Ktok_b = X["Vtok"], X["Btok"], X["Ktok"], X["Vtok_b"], X["Btok_b"], X["Ktok_b"]
                        NA, KA, NA_b, KA_b = X["NA"], X["KA"], X["NA_b"], X["KA_b"]
                        Qs, Qts, Ps, Q_b, Qt_b, P_b = X["Qs"], X["Qts"], X["Ps"], X["Q_b"], X["Qt_b"], X["P_b"]
                        F = dict(F0)
                        FB = dict(FB0)
                        F.update(X["F"])
                        FB.update(X["FB"])
                        shifted(24 + fc, rT, rT_b)
                        yield
                        shifted(32 + fc, kT, kT_b)
                        yield
                        shifted(40 + fc, vT, vT_b)
                        yield
                        fcs = slice(fc * 128, fc * 128 + 128)
                        bd, bbd = bank()
                        mk.op(pe, lambda: nc.tensor.matmul(pap(bd)[:, :nt], lwt[0:64, 0, fcs], LB48[0:64, :nt], start=True, stop=True), [lw_b, LB48_b], bbd)
                        mk.op(act, lambda: nc.scalar.activation(out=F["sw"][:, :nt], in_=pap(bd)[:, :nt], func=AF.Sigmoid, bias=pp[:, PP_W0 + fc:PP_W0 + fc + 1]),
                              bbd + [pp_b], [FB["sw"]])
                        rel(bd)
                        ba_, bba_ = bank()
                        mk.op(pe, lambda: nc.tensor.matmul(pap(ba_)[:, :nt], lwt[64:128, 0, fcs], LB48[64:128, :nt], start=True, stop=True), [lw_b, LB48_b], bba_)
                        mk.op(act, lambda: nc.scalar.activation(out=F["aa"][:, :nt], in_=pap(ba_)[:, :nt], func=AF.Sigmoid, bias=pp[:, PP_A0 + fc:PP_A0 + fc + 1]),
                              bba_ + [pp_b], [FB["aa"]])
                        rel(ba_)
                        bg_, bbg_ = bank()
                        mk.op(pe, lambda: nc.tensor.matmul(pap(bg_)[:, :nt], lwt[:, 1, fcs], LB49[:, :nt], start=True, stop=False), [lw_b, LB49_b], bbg_, signal=False)
                        mk.op(pe, lambda: nc.tensor.matmul(pap(bg_)[:, :nt], lwt[0:32, 2, fcs], LB50[0:32, :nt], start=False, stop=True), [lw_b, LB50_b], bbg_)
                        mk.op(act, lambda: nc.scalar.copy(out=F["gg"][:, :nt], in_=pap(bg_)[:, :nt]), bbg_, [FB["gg"]])
                        rel(bg_)
                        yield
                        vfv = vfd[fc * 128:(fc + 1) * 128, t0:t0 + nt]
                        if l == 0:
                            mk.dma(sp, vfv, vT[:, :nt], reads=[vT_b])
                        else:
                            bv_, bbv_ = bank()
                            mk.op(pe, lambda: nc.tensor.matmul(pap(bv_)[:, :nt], lwt[32:64, 2, fcs], LB50[32:64, :nt], start=True, stop=True), [lw_b, LB50_b], bbv_)
                            mk.op(act, lambda: nc.scalar.activation(out=F["sv"][:, :nt], in_=pap(bv_)[:, :nt], func=AF.Sigmoid, bias=pp[:, PP_V0 + fc:PP_V0 + fc + 1]),
                                  bbv_ + [pp_b], [FB["sv"]])
                            rel(bv_)
                            mk.dma(sp, F["vf"][:, :nt], vfv, writes=[FB["vf"]])
                            mk.op(dve, lambda: nc.vector.tensor_tensor(out=F["vf"][:, :nt], in0=F["vf"][:, :nt], in1=vT[:, :nt], op=ALU.subtract), [FB["vf"], vT_b], [FB["vf"]])
                            mk.op(dve, lambda: nc.vector.tensor_tensor(out=F["vf"][:, :nt], in0=F["vf"][:, :nt], in1=F["sv"][:, :nt], op=ALU.mult), [FB["vf"], FB["sv"]], [FB["vf"]])
                            mk.op(dve, lambda: nc.vector.tensor_tensor(out=vT[:, :nt], in0=vT[:, :nt], in1=F["vf"][:, :nt], op=ALU.add), [FB["vf"], vT_b], [vT_b])
                        mk.op(dve, lambda: nc.vector.tensor_scalar(out=F["kk"][:, :nt], in0=kT[:, :nt], scalar1=pp[:, PP_KK + fc:PP_KK + fc + 1], scalar2=None, op0=ALU.mult),
                              [kT_b, pp_b], [FB["kk"]])
                        mk.op(act, lambda: nc.scalar.activation(out=kksq[:, :nt], in_=F["kk"][:, :nt], func=AF.Square), [FB["kk"]], [kksq_b])
                        bn_, bbn_ = bank()
                        mk.op(pe, lambda: nc.tensor.matmul(pap(bn_)[:, :nt], blk_bf[:], kksq[:, :nt], start=True, stop=True), [cbf_b, kksq_b], bbn_)
                        mk.op(act, lambda: nc.scalar.activation(out=F["rt"][:, :nt], in_=pap(bn_)[:, :nt], func=AF.Sqrt, bias=1e-24, scale=1.0), bbn_, [FB["rt"]])
                        rel(bn_)
                        yield
                        mk.op(dve, lambda: nc.vector.reciprocal(out=F["rn"][:, :nt], in_=F["rt"][:, :nt]), [FB["rt"]], [FB["rn"]])
                        mk.op(dve, lambda: nc.vector.tensor_tensor(out=F["kkn"][:, :nt], in0=F["kk"][:, :nt], in1=F["rn"][:, :nt], op=ALU.mult), [FB["kk"], FB["rn"]], [FB["kkn"]])
                        mk.op(dve, lambda: nc.vector.tensor_scalar(out=F["t1"][:, :nt], in0=F["aa"][:, :nt], scalar1=-1.0, scalar2=pp[:, PP_KA + fc:PP_KA + fc + 1],
                                                                   op0=ALU.add, op1=ALU.mult), [FB["aa"], pp_b], [FB["t1"]])
                        mk.op(dve, lambda: nc.vector.scalar_tensor_tensor(out=F["kmod"][:, :nt], in0=F["t1"][:, :nt], scalar=1.0, in1=kT[:, :nt], op0=ALU.add, op1=ALU.mult),
                              [FB["t1"], kT_b], [FB["kmod"]])
                        mk.op(dve, lambda: nc.vector.tensor_tensor(out=F["bvec"][:, :nt], in0=F["kkn"][:, :nt], in1=F["aa"][:, :nt], op=ALU.mult), [FB["kkn"], FB["aa"]], [FB["bvec"]])
                        mk.op(dve, lambda: nc.vector.scalar_tensor_tensor(out=rkb[:, :nt], in0=rT[:, :nt], scalar=pp[:, PP_RK + fc:PP_RK + fc + 1], in1=F["kmod"][:, :nt],
                                                                         op0=ALU.mult, op1=ALU.mult), [rT_b, pp_b, FB["kmod"]], [rkb_b])
                        bb_, bbb_ = bank()
                        mk.op(pe, lambda: nc.tensor.matmul(pap(bb_)[:, :nt], blk_bf[:], rkb[:, :nt], start=True, stop=True), [cbf_b, rkb_b], bbb_)
                        mk.op(dve, lambda: nc.vector.tensor_tensor(out=F["bs"][:, :nt], in0=vT[:, :nt], in1=pap(bb_)[:, :nt], op=ALU.mult), [vT_b] + bbb_, [FB["bs"]])
                        rel(bb_)
                        yield
                        mk.op(dve, lambda: nc.vector.scalar_tensor_tensor(out=F["c1"][:, :nt], in0=F["bs"][:, :nt], scalar=pp[:, PP_LB + fc:PP_LB + fc + 1], in1=F["gg"][:, :nt],
                                                                         op0=ALU.add, op1=ALU.mult), [FB["bs"], pp_b, FB["gg"]], [FB["c1"]])
                        mk.op(dve, lambda: nc.vector.tensor_scalar(out=F["c2"][:, :nt], in0=F["gg"][:, :nt], scalar1=pp[:, PP_LG + fc:PP_LG + fc + 1], scalar2=None, op0=ALU.mult),
                              [FB["gg"], pp_b], [FB["c2"]])
                        yield
                        mk.op(act, lambda: nc.scalar.copy(out=csz[0][:, :, 32:96], in_=F["sw"][:, :nt].rearrange("p (c t) -> p c t", t=64)), [FB["sw"]], [csz_b[0]])
                        ci = 0
                        for si, sh in enumerate([1, 2, 4, 8, 16, 32]):
                            mk.op(dve, lambda ci=ci, sh=sh: nc.vector.tensor_tensor(out=csz[1 - ci][:, :, 32:96], in0=csz[ci][:, :, 32:96], in1=csz[ci][:, :, 32 - sh:96 - sh], op=ALU.add),
                                  [csz_b[ci]], [csz_b[1 - ci]])
                            ci = 1 - ci
                            if si % 2 == 1:
                                yield
                        mk.op(act, lambda ci=ci: nc.scalar.copy(out=F["csb"][:, :nt].rearrange("p (c t) -> p c t", t=64), in_=csz[ci][:, :, 32:96]), [csz_b[ci]], [FB["csb"]])
                        cur, curb = F["csb"], FB["csb"]
                        cum, cum_b = cur, curb
                        mk.op(act, lambda: nc.scalar.activation(out=F["wincl"][:, :nt], in_=cum[:, :nt], func=AF.Exp, scale=-DEC_C), [cum_b], [FB["wincl"]])
                        mk.op(dve, lambda: nc.vector.tensor_tensor(out=F["exc"][:, :nt], in0=cum[:, :nt], in1=F["sw"][:, :nt], op=ALU.subtract), [cum_b, FB["sw"]], [FB["exc"]])
                        mk.op(act, lambda: nc.scalar.activation(out=F["wexc"][:, :nt], in_=F["exc"][:, :nt], func=AF.Exp, scale=-DEC_C), [FB["exc"]], [FB["wexc"]])
                        mk.op(act, lambda: nc.scalar.activation(out=F["winv"][:, :nt], in_=cum[:, :nt], func=AF.Exp, scale=DEC_C), [cum_b], [FB["winv"]])
                        yield
                        mk.op(dve, lambda: nc.vector.scalar_tensor_tensor(out=AR[:, 0, :nt], in0=F["kkn"][:, :nt], scalar=-1.0, in1=F["wexc"][:, :nt], op0=ALU.mult, op1=ALU.mult),
                              [FB["kkn"], FB["wexc"]], [AR_b])
                        mk.op(dve, lambda: nc.vector.tensor_tensor(out=AR[:, 1, :nt], in0=rT[:, :nt], in1=F["wincl"][:, :nt], op=ALU.mult), [rT_b, FB["wincl"]], [AR_b])
                        mk.op(dve, lambda: nc.vector.tensor_tensor(out=BK[:, 0, :nt], in0=F["bvec"][:, :nt], in1=F["winv"][:, :nt], op=ALU.mult), [FB["bvec"], FB["winv"]], [BK_b])
                        mk.op(dve, lambda: nc.vector.tensor_tensor(out=BK[:, 1, :nt], in0=F["kmod"][:, :nt], in1=F["winv"][:, :nt], op=ALU.mult), [FB["kmod"], FB["winv"]], [BK_b])
                        mk.op(act, lambda: nc.scalar.copy(out=vbf[:, :nt], in_=vT[:, :nt]), [vT_b], [vbf_b])
                        yield
                        for (srcap, srcb, dstt, dstb) in ((vbf, vbf_b, Vtok, Vtok_b), (BK[:, 0, :], BK_b, Btok, Btok_b), (BK[:, 1, :], BK_b, Ktok, Ktok_b)):
                            bt, bbt = bank()
                            pv = pap(bt).bitcast(BF16)
                            for c in range(NCH):
                                mk.op(pe, lambda c=c, srcap=srcap, pv=pv: nc.tensor.transpose(pv[0:64, c * 128:(c + 1) * 128], srcap[:, c * 64:(c + 1) * 64], idb128[:]),
                                      [srcb, cbf_b], bbt, signal=(c == NCH - 1))
                            mk.op(act, lambda dstt=dstt, pv=pv: nc.scalar.copy(out=dstt[:].rearrange("p c f -> p (c f)"), in_=pv[0:64, 0:NCH * 128]), bbt, [dstb])
                            rel(bt)
                            yield
                        b1, bb1 = bank(2)
                        p1 = pap(b1, 2)
                        for c in range(NCH):
                            for hh in range(1 if DBG == 1 else 2):
                                g = c * 2 + hh
                                hp = slice(hh * 64, hh * 64 + 64)
                                cc = slice(c * 64, c * 64 + 64)
                                last = (g == (NG - 2 if DBG == 1 else NG - 1))
                                mk.op(pe, lambda g=g, hp=hp, cc=cc, hh=hh, c=c: (mm64(p1, g * 128, BK, 0, hh, c, AR[hp, 0, cc]), mm64(p1, g * 128 + 64, BK, 0, hh, c, AR[hp, 1, cc]))[1],
                                      [BK_b, AR_b], bb1, signal=last)
                        for hb_ in range(2):
                            gs = slice(hb_ * 4, hb_ * 4 + 4)
                            mk.op(dve, lambda hb_=hb_, gs=gs: nc.vector.tensor_tensor(out=NA[:, gs, :], in0=p1[0:64, hb_ * 512:(hb_ + 1) * 512].rearrange("p (g n) -> p g n", n=128),
                                                                                     in1=mask3(None, 0)[:, 0:4, :], op=ALU.mult), bb1 + [cst_fb], [NA_b])
                        rel(b1, 2)
                        yield
                        b2, bb2 = bank(2)
                        p2 = pap(b2, 2)
                        for c in range(NCH):
                            for hh in range(1 if DBG == 1 else 2):
                                g = c * 2 + hh
                                hp = slice(hh * 64, hh * 64 + 64)
                                cc = slice(c * 64, c * 64 + 64)
                                last = (g == (NG - 2 if DBG == 1 else NG - 1))
                                mk.op(pe, lambda g=g, hp=hp, cc=cc, hh=hh, c=c: (mm64(p2, g * 128, BK, 1, hh, c, AR[hp, 0, cc]), mm64(p2, g * 128 + 64, BK, 1, hh, c, AR[hp, 1, cc]))[1],
                                      [BK_b, AR_b], bb2, signal=last)
                        for hb_ in range(2):
                            gs = slice(hb_ * 4, hb_ * 4 + 4)
                            mk.op(dve, lambda hb_=hb_, gs=gs: nc.vector.tensor_tensor(out=KA[:, gs, :], in0=p2[0:64, hb_ * 512:(hb_ + 1) * 512].rearrange("p (g n) -> p g n", n=128),
                                                                                     in1=mask3(None, 0)[:, 0:4, :], op=ALU.mult), bb2 + [cst_fb], [KA_b])
                        rel(b2, 2)
                        yield
                        b3, bb3 = bank()
                        p3 = pap(b3)
                        for c in range(NCH):
                            for hh in range(1 if DBG == 1 else 2):
                                g = c * 2 + hh
                                hp = slice(hh * 64, hh * 64 + 64)
                                cc = slice(c * 64, c * 64 + 64)
                                last = (g == (NG - 2 if DBG == 1 else NG - 1))
                                mk.op(pe, lambda g=g, hp=hp, cc=cc, hh=hh, c=c: mm64(p3, g * 64, AR, 0, hh, c, BK[hp, 0, cc]),
                                      [BK_b, AR_b], bb3, signal=last)
                        mk.op(dve, lambda: nc.vector.tensor_tensor(out=Qts[0][:], in0=p3[0:64, 0:NG * 64].rearrange("p (g n) -> p g n", n=64),
                                                                   in1=mask3(None, 1), op=ALU.mult), bb3 + [cst_fb], [Qt_b[0]])
                        rel(b3)
                        yield
                        mk.op(act, lambda: nc.scalar.copy(out=Qs[0][:], in_=NA[:, :, 0:64]), [NA_b], [Q_b[0]])
                        mk.op(dve, lambda: nc.vector.tensor_tensor(out=Ps[0][:], in0=NA[:, :, 0:64], in1=mask3(None, 2), op=ALU.add), [NA_b, cst_fb], [P_b[0]])
                        qi = 0
                        pi = 0
                        for m_ in range(1, 6):
                            qn = 1 - qi
                            pn = 1 - pi
                            if m_ < 5:
                                bq, bbq = bank()
                                pq = pap(bq)
                                for g in range(NG):
                                    mk.op(pe, lambda g=g, qi=qi, pq=pq: nc.tensor.matmul(pq[0:64, g * 64:(g + 1) * 64], Qts[qi][:, g, :], Qs[qi][:, g, :], start=True, stop=True),
                                          [Q_b[qi], Qt_b[qi]], bbq, signal=(g == NG - 1))
                            bqt, bbqt = bank()
                            pqt = pap(bqt)
                            for g in range(NG):
                                mk.op(pe, lambda g=g, qi=qi, pqt=pqt: nc.tensor.matmul(pqt[0:64, g * 64:(g + 1) * 64], Qs[qi][:, g, :], Qts[qi][:, g, :], start=True, stop=True),
                                      [Q_b[qi], Qt_b[qi]], bbqt, signal=(g == NG - 1))
                            if m_ < 5:
                                mk.op(act, lambda qn=qn, pq=pq: nc.scalar.copy(out=Qs[qn][:], in_=pq[0:64, 0:NG * 64].rearrange("p (g n) -> p g n", n=64)), bbq, [Q_b[qn]])
                                rel(bq)
                            mk.op(dve, lambda qn=qn, pqt=pqt: nc.vector.tensor_copy(out=Qts[qn][:], in_=pqt[0:64, 0:NG * 64].rearrange("p (g n) -> p g n", n=64)), bbqt, [Qt_b[qn]])
                            rel(bqt)
                            yield
                            bp_, bbp_ = bank()
                            pp_ = pap(bp_)
                            for g in range(NG):
                                mk.op(pe, lambda g=g, pi=pi, qn=qn, pp_=pp_: nc.tensor.matmul(pp_[0:64, g * 64:(g + 1) * 64], Qts[qn][:, g, :], Ps[pi][:, g, :], start=True, stop=True),
                                      [Qt_b[qn], P_b[pi]], bbp_, signal=(g == NG - 1))
                            mk.op(dve, lambda pn=pn, pi=pi, pp_=pp_: nc.vector.tensor_tensor(out=Ps[pn][:], in0=pp_[0:64, 0:NG * 64].rearrange("p (g n) -> p g n", n=64), in1=Ps[pi][:], op=ALU.add),
                                  bbp_ + [P_b[pi]], [P_b[pn]])
                            rel(bp_)
                            yield
                            qi = qn
                            pi = pn
                        Pf, Pf_b = Ps[pi], P_b[pi]
                        yield 'S'
                        for c in range(NCH):
                            cc = slice(c * 64, c * 64 + 64)
                            br, bbr = bank()
                            pr_ = pap(br)
                            for hh in range(2):
                                g = c * 2 + hh
                                hp = slice(hh * 64, hh * 64 + 64)
                                mk.op(pe, lambda hh=hh, hp=hp: mm64(pr_, hh * 64, AR, 0, hh, c, Hb[hp, fc, :], stop=False),
                                      [AR_b, Hb_b[fc]], bbr, signal=False)
                                mk.op(pe, lambda hh=hh, g=g: nc.tensor.matmul(pr_[0:64, hh * 64:(hh + 1) * 64], KA[:, g, 0:64], Vtok[:, c, hh * 64:(hh + 1) * 64], start=False, stop=True),
                                      [KA_b, Vtok_b], bbr, signal=(hh == 1))
                            mk.op(act, lambda: nc.scalar.copy(out=ru[:].rearrange("p h v -> p (h v)"), in_=pr_[0:64, 0:128]), bbr, [ru_b])
                            rel(br)
                            yield
                            bu2, bbu2 = bank()
                            pu_ = pap(bu2)
                            for hh in range(2):
                                g = c * 2 + hh
                                mk.op(pe, lambda hh=hh, g=g: nc.tensor.matmul(pu_[0:64, hh * 64:(hh + 1) * 64], Pf[:, g, :], ru[:, hh, :], start=True, stop=True),
                                      [Pf_b, ru_b], bbu2, signal=(hh == 1))
                            mk.op(act, lambda: nc.scalar.copy(out=Us[:].rearrange("p h v -> p (h v)"), in_=pu_[0:64, 0:128]), bbu2, [Us_b])
                            rel(bu2)
                            yield
                            by, bby = bank()
                            py_ = pap(by)
                            for hh in range(2):
                                g = c * 2 + hh
                                hp = slice(hh * 64, hh * 64 + 64)
                                mk.op(pe, lambda hh=hh, hp=hp: mm64(py_, hh * 64, AR, 1, hh, c, Hb[hp, fc, :], stop=False),
                                      [AR_b, Hb_b[fc]], bby, signal=False)
                                mk.op(pe, lambda hh=hh, g=g: nc.tensor.matmul(py_[0:64, hh * 64:(hh + 1) * 64], NA[:, g, 64:128], Us[:, hh, :], start=False, stop=False),
                                      [NA_b, Us_b], bby, signal=False)
                                mk.op(pe, lambda hh=hh, g=g: nc.tensor.matmul(py_[0:64, hh * 64:(hh + 1) * 64], KA[:, g, 64:128], Vtok[:, c, hh * 64:(hh + 1) * 64], start=False, stop=True),
                                      [KA_b, Vtok_b], bby, signal=(hh == 1))
                            mk.op(act, lambda c=c: nc.scalar.copy(out=ysb[:, c, :], in_=py_[0:64, 0:128]), bby, [ysb_b])
                            rel(by)
                            yield
                            bh, bbh = bank()
                            ph_ = pap(bh)
                            for hh in range(2):
                                mk.op(pe, lambda hh=hh: nc.tensor.matmul(ph_[:, hh * 64:(hh + 1) * 64], Btok[:, c, :], Us[:, hh, :], start=True, stop=False),
                                      [Btok_b, Us_b], bbh, signal=False)
                                mk.op(pe, lambda hh=hh: nc.tensor.matmul(ph_[:, hh * 64:(hh + 1) * 64], Ktok[:, c, :], Vtok[:, c, hh * 64:(hh + 1) * 64], start=False, stop=True),
                                      [Ktok_b, Vtok_b], bbh, signal=(hh == 1))
                            wc = c * 64 + 63
                            for hh in range(2):
                                hp = slice(hh * 64, hh * 64 + 64)
                                mk.op(dve, lambda hp=hp: nc.vector.tensor_scalar(out=Hf[hp, fc, :], in0=Hf[hp, fc, :], scalar1=F["wincl"][hp, wc:wc + 1], scalar2=None, op0=ALU.mult),
                                      [Hf_b[fc], FB["wincl"]], [Hf_b[fc]])
                                mk.op(dve, lambda hp=hp, hh=hh: nc.vector.scalar_tensor_tensor(out=Hf[hp, fc, :], in0=ph_[hp, hh * 64:(hh + 1) * 64], scalar=F["wincl"][hp, wc:wc + 1],
                                                                                              in1=Hf[hp, fc, :], op0=ALU.mult, op1=ALU.add),
                                      bbh + [Hf_b[fc], FB["wincl"]], [Hf_b[fc]])
                            mk.op(act, lambda: nc.scalar.copy(out=Hb[:, fc, :], in_=Hf[:, fc, :]), [Hf_b[fc]], [Hb_b[fc]])
                            rel(bh)
                            yield
                        y3 = ysb[:].rearrange("p c (h v) -> p (c h) v", v=64)
                        q3 = ysq[:].rearrange("p c (h v) -> p (c h) v", v=64)
                        n3 = yn[:].rearrange("p c (h v) -> p (c h) v", v=64)
                        mk.op(dve, lambda: nc.vector.tensor_reduce(out=st1[:, 0, :], in_=y3, axis=AX.X, op=ALU.add), [ysb_b], [st_b])
                        mk.op(act, lambda: nc.scalar.activation(out=ysq[:], in_=ysb[:], func=AF.Square), [ysb_b], [ysq_b])
                        mk.op(dve, lambda: nc.vector.tensor_reduce(out=st1[:, 1, :], in_=q3, axis=AX.X, op=ALU.add), [ysq_b], [st_b])
                        mk.op(dve, lambda: nc.vector.tensor_scalar(out=st1[:, 2, :], in0=st1[:, 0, :], scalar1=1.0 / 64, scalar2=None, op0=ALU.mult), [st_b], [st_b])
                        mk.op(dve, lambda: nc.vector.tensor_tensor(out=st1[:, 3, :], in0=st1[:, 2, :], in1=st1[:, 2, :], op=ALU.mult), [st_b], [st_b])
                        mk.op(dve, lambda: nc.vector.scalar_tensor_tensor(out=st1[:, 4, :], in0=st1[:, 1, :], scalar=1.0 / 64, in1=st1[:, 3, :], op0=ALU.mult, op1=ALU.subtract),
                              [st_b], [st_b])
                        mk.op(act, lambda: nc.scalar.activation(out=st1[:, 5, :], in_=st1[:, 4, :], func=AF.Sqrt, bias=LNX_EPS, scale=1.0), [st_b], [st_b])
                        mk.op(dve, lambda: nc.vector.reciprocal(out=st1[:, 3, :], in_=st1[:, 5, :]), [st_b], [st_b])
                        mk.op(dve, lambda: nc.vector.tensor_tensor(out=n3, in0=y3, in1=st1[:, 2, :].unsqueeze(2).broadcast_to([64, NG, 64]), op=ALU.subtract),
                              [ysb_b, st_b], [yn_b])
                        mk.op(dve, lambda: nc.vector.tensor_tensor(out=n3, in0=n3, in1=st1[:, 3, :].unsqueeze(2).broadcast_to([64, NG, 64]), op=ALU.mult),
                              [yn_b, st_b], [yn_b])
                        yield
                        bt2, bbt2 = bank()
                        pt2 = pap(bt2)
                        for c in range(NCH):
                            mk.op(pe, lambda c=c: nc.tensor.transpose(pt2[:, c * 64:(c + 1) * 64], yn[:, c, :], identf),
                                  [yn_b, cst_fb], bbt2, signal=(c == NCH - 1))
                        mk.op(dve, lambda: nc.vector.tensor_tensor(out=ytmp[:, :nt], in0=F["c2"][:, :nt], in1=pt2[:, :nt], op=ALU.mult), [FB["c2"]] + bbt2, [ytmp_b])
                        rel(bt2)
                        mk.op(dve, lambda: nc.vector.tensor_tensor(out=opT[:, 8 + fc, :nt], in0=ytmp[:, :nt], in1=F["c1"][:, :nt], op=ALU.add), [ytmp_b, FB["c1"]], [opT_b[8 + fc]])

                    def step(g):
                        try:
                            return next(g)
                        except StopIteration:
                            return "END"

                    g_conv = gen_conv()
                    g_cur = gen_fc(0, XS[0])
                    while step(g_cur) != "S":
                        step(g_conv)
                    for fc in range(8):
                        g_next = gen_fc(fc + 1, XS[(fc + 1) % 2]) if fc < 7 else None
                        next_at_s = g_next is None
                        cur_alive = True
                        while cur_alive:
                            if step(g_cur) == "END":
                                cur_alive = False
                            step(g_conv)
                            for _ in range(3):
                                if not next_at_s and step(g_next) == "S":
                                    next_at_s = True
                        while not next_at_s:
                            if step(g_next) == "S":
                                next_at_s = True
                        g_cur = g_next
                    while step(g_conv) != "END":
                        pass

                    chk(9)
                    for fo in range(NC_F):
                        s = wslot()
                        wsrc_ap, wsrc_b = wsrc("wout", l, fo)
                        mk.dma(sp, wi[s][:], wsrc_ap.rearrange("p (c n) -> p c n", n=128), reads=[wsrc_b], writes=[wi_b[s]])
                        b, bb = bank()
                        for kc in range(NC_F):
                            mk.op(pe, lambda kc=kc, s=s: nc.tensor.matmul(pap(b)[:, :nt], wi[s][:, kc, :], opT[:, kc, :nt], start=(kc == 0), stop=(kc == NC_F - 1)),
                                  [wi_b[s], opT_b[kc]], bb, signal=(kc == NC_F - 1))
                        mk.op(act, lambda fo=fo: nc.scalar.copy(out=yT[:, fo, :nt], in_=pap(b)[:, :nt]), bb, [yT_b[fo]])
                        rel(b)
                except _Stop:
                    pass
                postnorm_residual(PP_PM, nt, 1.0)
                store_h(dst, t0, nt)
        mk.barrier()

    def mask3(_, kind):
        NG = (TM // 64) * 2
        if kind == 0:
            return cst2_f[:, 0:NG * 128].rearrange("p (g n) -> p g n", n=128)
        if kind == 1:
            return cst2_f[:, 1024:1024 + NG * 64].rearrange("p (g n) -> p g n", n=64)
        return cst2_f[:, 1536:1536 + NG * 64].rearrange("p (g n) -> p g n", n=64)

    for l in range(depth):
        load_params(l)
        src = h0 if l == 0 else hres
        last = (l == depth - 1)
        if "f1" in phases:
            ffn_phase(l, "wgu1", "wdn1", PP_G1, PP_P1, src, hout if (last and phases[-1] == "f1") else hres)
        if l + 1 < depth:
            l2 = l + 1
            if "f1" in phases:
                conv_w("wgu1", wgu1, l2)
                conv_w("wdn1", wdn1, l2)
            if "mix" in phases:
                conv_w("win", win, l2)
                conv_w("wout", wout, l2)
            if "f2" in phases:
                conv_w("wgu2", wgu2, l2)
                conv_w("wdn2", wdn2, l2)
        if "mix" in phases:
            mixer_phase(l, hres if ("f1" in phases or l > 0) else h0, hout if (last and phases[-1] == "mix") else hres)
        if "f2" in phases:
            ffn_phase(l, "wgu2", "wdn2", PP_G2, PP_P2, hres, hout if last else hres)
    mk.finish()
    es.close()
    mk.dram_names = set(dr)
    return nc, mk


def _consts():
    c = np.zeros((128, 640), np.float32)
    c[:, 0:128] = 1.0
    c[0:64, 128:192] = 1.0
    c[64:128, 192:256] = 1.0
    j = np.arange(64)[:, None]
    i = np.arange(64)[None, :]
    c[0:64, 256:320] = (j < i)
    c[0:64, 320:384] = (j <= i)
    c[:, 384:512] = np.eye(128)
    c[0:64, 512:576] = (i < j)
    c[0:64, 576:640] = np.eye(64)
    return c


def _fm(v):
    return np.ascontiguousarray(v.reshape(-1, 128).T)


def prep_weights(inp, depth):
    f = np.float32
    out = {}
    for nm, key_gu, key_dn in (("1", "ffn1_w_gu", "ffn1_w_down"), ("2", "ffn2_w_gu", "ffn2_w_down")):
        wgu = np.asarray(inp[key_gu][:depth])
        g = wgu[:, :, :DFF].reshape(depth, NC_F, 128, NJ_FF, 128)
        u = wgu[:, :, DFF:].reshape(depth, NC_F, 128, NJ_FF, 128)
        gu = np.stack([g, u], axis=4)
        out["wgu" + nm] = np.ascontiguousarray(gu.transpose(0, 3, 2, 1, 4, 5)).reshape(depth, NJ_FF, 128, NC_F * 256)
        wd = np.asarray(inp[key_dn][:depth]).reshape(depth, NJ_FF, 128, NC_F, 128)
        out["wdn" + nm] = np.ascontiguousarray(wd.transpose(0, 3, 2, 1, 4)).reshape(depth, NC_F, 128, NJ_FF * 128)
    w_in = np.asarray(inp["w_in"][:depth])
    wcat = np.zeros((depth, D, NJ_IN * 128), f)
    wcat[:, :, :6432] = w_in
    for l in range(1, depth):
        wcat[l, :, 6432:6464] = inp["w_in_vres"][l - 1]
    wc = wcat.reshape(depth, NC_F, 128, NJ_IN, 128)
    out["win"] = np.ascontiguousarray(wc.transpose(0, 3, 2, 1, 4)).reshape(depth, NJ_IN, 128, NC_F * 128)
    wo = np.asarray(inp["w_out"][:depth]).reshape(depth, NC_F, 128, NC_F, 128)
    out["wout"] = np.ascontiguousarray(wo.transpose(0, 3, 2, 1, 4)).reshape(depth, NC_F, 128, NC_F * 128)
    lw = np.zeros((depth, 3, 128, RW), f)
    pp = np.zeros((depth, 128, NPP), f)
    for l in range(depth):
        lw[l, 0, 0:64] = inp["decay_w2"][l]
        lw[l, 0, 64:128] = inp["iclr_a2"][l]
        lw[l, 1] = inp["gate_g2"][l][0:128]
        lw[l, 2, 0:32] = inp["gate_g2"][l][128:160]
        if l >= 1:
            lw[l, 2, 32:64] = inp["vres_v2"][l - 1]
        pp[l, :, PP_G1:PP_G1 + 16] = _fm(inp["ffn1_pre_g"][l])
        pp[l, :, PP_GM:PP_GM + 16] = _fm(inp["mix_pre_g"][l])
        pp[l, :, PP_G2:PP_G2 + 16] = _fm(inp["ffn2_pre_g"][l])
        pp[l, :, PP_P1:PP_P1 + 16] = _fm(inp["ffn1_post_g"][l])
        pp[l, :, PP_PM:PP_PM + 16] = _fm(inp["mix_post_g"][l])
        pp[l, :, PP_P2:PP_P2 + 16] = _fm(inp["ffn2_post_g"][l])
        mu = np.zeros(27 * 128, f)
        mu[:3360] = inp["mu_rwkv"][l]
        if l >= 1:
            mu[3360:3392] = inp["mu_vres"][l - 1]
        pp[l, :, PP_MU:PP_MU + 27] = _fm(mu)
        for jj in range(3):
            pp[l, :, PP_CW + 8 * jj:PP_CW + 8 * jj + 8] = _fm(inp["conv_w"][l][jj])
        pp[l, :, PP_CG:PP_CG + 8] = _fm(inp["conv_norm_g"][l])
        pp[l, :, PP_W0:PP_W0 + 8] = _fm(inp["decay_w0"][l])
        pp[l, :, PP_A0:PP_A0 + 8] = _fm(inp["iclr_a0"][l])
        if l >= 1:
            pp[l, :, PP_V0:PP_V0 + 8] = _fm(inp["vres_v0"][l - 1])
        pp[l, :, PP_KK:PP_KK + 8] = _fm(inp["k_k"][l])
        pp[l, :, PP_KA:PP_KA + 8] = _fm(inp["k_a"][l])
        pp[l, :, PP_RK:PP_RK + 8] = _fm(np.asarray(inp["r_k"][l]).reshape(-1))
        pp[l, :, PP_LG:PP_LG + 8] = _fm(inp["lnx_g"][l])
        pp[l, :, PP_LB:PP_LB + 8] = _fm(inp["lnx_b"][l])
    out["lw"] = lw
    out["pp"] = pp
    out["cst"] = _consts()
    c1_ = out["cst"]
    c2_ = np.zeros((64, 2048), f)
    c2_[:, 0:1024] = np.tile(c1_[0:64, 256:384], (1, 8))
    c2_[:, 1024:1536] = np.tile(c1_[0:64, 512:576], (1, 8))
    c2_[:, 1536:2048] = np.tile(c1_[0:64, 576:640], (1, 8))
    out["cst2"] = c2_
    return out


def run_module(inp, depth, T_PAD, n_valid, n_cores=8):
    x = np.asarray(inp["x"], np.float32)
    B, L, _ = x.shape
    meta = np.asarray(inp["meta_tokens"], np.float32)
    import time as _t
    _t0 = _t.time()
    w = prep_weights(inp, depth)
    _t1 = _t.time()
    nc, mk = build_program(T_PAD, depth)
    print("prep %.1fs build %.1fs ninst %d" % (_t1 - _t0, _t.time() - _t1, mk.ninst), flush=True)
    in_maps = []
    for c in range(n_cores):
        b = c % B
        h0 = np.zeros((D, T_PAD), np.float32)
        h0[:, :NMETA] = meta.T
        h0[:, NMETA:NMETA + L] = x[b].T
        m = {k: v for k, v in w.items() if k in mk.dram_names}
        m["h0"] = h0
        in_maps.append(m)
    res = run_bass_kernel_spmd(nc, in_maps, core_ids=list(range(n_cores)))
    out = np.stack([np.ascontiguousarray(res.results[b]["hout"][:, NMETA:NMETA + L].T) for b in range(B)], axis=0)
    return out.astype(np.float32)


def kernel(**inputs):
    T_PAD = 4352
    inp = {k: np.asarray(v) for k, v in inputs.items()}
    return run_module(inp, DEPTH, T_PAD, NMETA + SEQ)
```
